# Optimizing a Trainium2 kernel written in Bass

```python
import jax, jax.numpy as jnp
from jax import lax
import numpy as np

D_MODEL = 1024
BATCH = 1
SEQ = 16384
DEPTH = 2
DEC_BATCH = 32
DEC_SEQ = 8
PAST_LEN = 16384
PAGE_SIZE = 128

D_CONV = D_MODEL
CONV_WIDTH = 31
HEAD_DIM = 64
HEADS_PER_GROUP = 4
ATTN_GROUPS = ((128, 1), (512, 4), (2048, 16))
N_GROUPS = len(ATTN_GROUPS)
D_ATTN = N_GROUPS * HEADS_PER_GROUP * HEAD_DIM
D_ATTN_OUT = HEADS_PER_GROUP * HEAD_DIM
N_BRANCHES = 2
D_IN = 2 * D_CONV + 3 * D_ATTN + N_BRANCHES * D_MODEL
D_FF = 3 * D_MODEL
FFN_CONV_WIDTH = 3
Q_BLOCK = 128
EPS = 1e-6

kernel_name = "hybrid_conformer_dilated_attn_decoder_step"


def rms_norm(x, g):
    xf = x.astype(jnp.float32)
    y = xf * lax.rsqrt(jnp.mean(xf * xf, axis=-1, keepdims=True) + EPS)
    return (y * g.astype(jnp.float32)).astype(x.dtype)


def layer_norm(x, g, b):
    xf = x.astype(jnp.float32)
    mu = jnp.mean(xf, axis=-1, keepdims=True)
    xc = xf - mu
    var = jnp.mean(xc * xc, axis=-1, keepdims=True)
    y = xc * lax.rsqrt(var + EPS) * g.astype(jnp.float32) + b.astype(jnp.float32)
    return y.astype(x.dtype)


def causal_depthwise_conv(u, hist, w, b):
    width = w.shape[0]
    xp = jnp.concatenate([hist.astype(u.dtype), u], axis=1)
    y = lax.conv_general_dilated(xp, w[:, None, :].astype(u.dtype), window_strides=(1,), padding='VALID',
                                 dimension_numbers=('NWC', 'WIO', 'NWC'), feature_group_count=u.shape[-1])
    return y + b.astype(u.dtype), xp[:, xp.shape[1] - (width - 1):]


def dilated_window_attention(q, k_ext, v_ext, offset, dilation, n_keys):
    n, t, h, hd = q.shape
    qb = Q_BLOCK if t % Q_BLOCK == 0 else t
    n_blocks = t // qb
    scale = HEAD_DIM ** -0.5
    steps = jnp.arange(n_keys, dtype=jnp.int32) * dilation

    def block(bi):
        q0 = bi * qb
        pos = offset + q0 + jnp.arange(qb, dtype=jnp.int32)
        kidx = pos[:, None] - steps[None, :]
        valid = kidx >= 0
        kidx = jnp.maximum(kidx, 0)
        kg = jnp.take(k_ext, kidx, axis=1)
        vg = jnp.take(v_ext, kidx, axis=1)
        qblk = lax.dynamic_slice_in_dim(q, q0, qb, axis=1)
        s = jnp.einsum('nqhd,nqkhd->nhqk', qblk, kg, preferred_element_type=jnp.float32) * scale
        s = jnp.where(valid[None, None], s, -jnp.inf)
        m = jnp.max(s, axis=-1, keepdims=True)
        p = jnp.exp(s - m)
        l = jnp.sum(p, axis=-1, keepdims=True)
        o = jnp.einsum('nhqk,nqkhd->nqhd', p, vg.astype(jnp.float32))
        o = o / jnp.transpose(l, (0, 2, 1, 3))
        lse = jnp.transpose((m + jnp.log(l))[..., 0], (0, 2, 1))
        return o, lse

    o, lse = lax.map(block, jnp.arange(n_blocks, dtype=jnp.int32))
    o = jnp.transpose(o, (1, 0, 2, 3, 4)).reshape(n, t, h, hd)
    lse = jnp.transpose(lse, (1, 0, 2, 3)).reshape(n, t, h)
    return o, lse


def trunk_layer(x, conv_hist, kv_hist, ffn_hist, norm_attn_g, w_in, conv_dw_w, conv_dw_b, conv_ln_g,
                conv_ln_b, w_conv_out, w_attn_out, w_out, norm_ffn_g, w_up, ffn_dw_w, ffn_dw_b, w_down):
    n, t, _ = x.shape
    h = rms_norm(x, norm_attn_g)
    proj = jnp.einsum('ntd,de->nte', h, w_in)
    glu_in, q, k, v, gates = jnp.split(
        proj, [2 * D_CONV, 2 * D_CONV + D_ATTN, 2 * D_CONV + 2 * D_ATTN, 2 * D_CONV + 3 * D_ATTN], axis=-1)

    a, b = jnp.split(glu_in, 2, axis=-1)
    u = a * jax.nn.sigmoid(b)
    c, new_conv = causal_depthwise_conv(u, conv_hist, conv_dw_w, conv_dw_b)
    c = jax.nn.silu(layer_norm(c, conv_ln_g, conv_ln_b))
    branch_conv = jnp.einsum('ntc,cd->ntd', c, w_conv_out)

    q = q.reshape(n, t, N_GROUPS, HEADS_PER_GROUP, HEAD_DIM)
    k = k.reshape(n, t, N_GROUPS, HEADS_PER_GROUP, HEAD_DIM)
    v = v.reshape(n, t, N_GROUPS, HEADS_PER_GROUP, HEAD_DIM)
    outs, lses, new_kv = [], [], []
    for g, (window, dilation) in enumerate(ATTN_GROUPS):
        k_hist, v_hist = kv_hist[g]
        hist_len = k_hist.shape[1]
        k_ext = jnp.concatenate([k_hist.astype(k.dtype), k[:, :, g]], axis=1)
        v_ext = jnp.concatenate([v_hist.astype(v.dtype), v[:, :, g]], axis=1)
        o, lse = dilated_window_attention(q[:, :, g], k_ext, v_ext, hist_len, dilation, window // dilation + 1)
        outs.append(o)
        lses.append(lse)
        keep = hist_len if hist_len > 0 else min(window, t)
        new_kv.append((k_ext[:, k_ext.shape[1] - keep:], v_ext[:, v_ext.shape[1] - keep:]))
    wts = jax.nn.softmax(jnp.stack(lses, axis=0), axis=0)
    attn = jnp.sum(wts[..., None] * jnp.stack(outs, axis=0), axis=0)
    attn = attn.reshape(n, t, D_ATTN_OUT).astype(x.dtype)
    branch_attn = jnp.einsum('nta,ad->ntd', attn, w_attn_out)

    g_conv, g_attn = jnp.split(gates, 2, axis=-1)
    mixed = jax.nn.sigmoid(g_conv) * branch_conv + jax.nn.sigmoid(g_attn) * branch_attn
    x = x + jnp.einsum('ntd,de->nte', mixed, w_out)

    h = rms_norm(x, norm_ffn_g)
    up = jnp.einsum('ntd,df->ntf', h, w_up)
    up_c, new_ffn = causal_depthwise_conv(up, ffn_hist, ffn_dw_w, ffn_dw_b)
    val, gate = jnp.split(up_c, 2, axis=-1)
    x = x + jnp.einsum('ntf,fd->ntd', jax.nn.silu(gate) * val, w_down)
    return x, new_conv, new_kv, new_ffn


def setup_inputs(seed: int = 0) -> dict:
    key = jax.random.key(seed)
    ks = jax.random.split(key, 32)
    f32 = jnp.float32

    def nrm(k, shape, scale):
        return jax.random.normal(k, shape, f32) * scale

    bufs = [min(w, PAST_LEN) for w, _ in ATTN_GROUPS]
    kv_shape = lambda L: (DEPTH, DEC_BATCH, L, HEADS_PER_GROUP, HEAD_DIM)
    return {
        "x_prompt": nrm(ks[0], (BATCH, SEQ, D_MODEL), 1.0),
        "x_sample": nrm(ks[1], (DEC_BATCH, DEC_SEQ, D_MODEL), 1.0),
        "state_conv": nrm(ks[2], (DEPTH, DEC_BATCH, CONV_WIDTH - 1, D_CONV), 0.5),
        "cache_k_w128": nrm(ks[3], kv_shape(bufs[0]), 1.0),
        "cache_v_w128": nrm(ks[4], kv_shape(bufs[0]), 1.0),
        "cache_k_w512": nrm(ks[5], kv_shape(bufs[1]), 1.0),
        "cache_v_w512": nrm(ks[6], kv_shape(bufs[1]), 1.0),
        "cache_k_w2048": nrm(ks[7], kv_shape(bufs[2]), 1.0),
        "cache_v_w2048": nrm(ks[8], kv_shape(bufs[2]), 1.0),
        "state_ffn_conv": nrm(ks[9], (DEPTH, DEC_BATCH, FFN_CONV_WIDTH - 1, 2 * D_FF), 1.0),
        "norm_attn_g": 1.0 + nrm(ks[10], (DEPTH, D_MODEL), 0.01),
        "w_in": nrm(ks[11], (DEPTH, D_MODEL, D_IN), D_MODEL ** -0.5),
        "conv_dw_w": nrm(ks[12], (DEPTH, CONV_WIDTH, D_CONV), CONV_WIDTH ** -0.5),
        "conv_dw_b": nrm(ks[13], (DEPTH, D_CONV), 0.01),
        "conv_ln_g": 1.0 + nrm(ks[14], (DEPTH, D_CONV), 0.01),
        "conv_ln_b": nrm(ks[15], (DEPTH, D_CONV), 0.01),
        "w_conv_out": nrm(ks[16], (DEPTH, D_CONV, D_MODEL), D_CONV ** -0.5),
        "w_attn_out": nrm(ks[17], (DEPTH, D_ATTN_OUT, D_MODEL), D_ATTN_OUT ** -0.5),
        "w_out": nrm(ks[18], (DEPTH, D_MODEL, D_MODEL), D_MODEL ** -0.5),
        "norm_ffn_g": 1.0 + nrm(ks[19], (DEPTH, D_MODEL), 0.01),
        "w_up": nrm(ks[20], (DEPTH, D_MODEL, 2 * D_FF), D_MODEL ** -0.5),
        "ffn_dw_w": nrm(ks[21], (DEPTH, FFN_CONV_WIDTH, 2 * D_FF), FFN_CONV_WIDTH ** -0.5),
        "ffn_dw_b": nrm(ks[22], (DEPTH, 2 * D_FF), 0.01),
        "w_down": nrm(ks[23], (DEPTH, D_FF, D_MODEL), D_FF ** -0.5),
        "norm_final_g": 1.0 + nrm(ks[24], (D_MODEL,), 0.01),
    }


def reference(x_prompt, x_sample, state_conv, cache_k_w128, cache_v_w128, cache_k_w512, cache_v_w512,
              cache_k_w2048, cache_v_w2048, state_ffn_conv, norm_attn_g, w_in, conv_dw_w, conv_dw_b,
              conv_ln_g, conv_ln_b, w_conv_out, w_attn_out, w_out, norm_ffn_g, w_up, ffn_dw_w, ffn_dw_b,
              w_down, norm_final_g):
    nb = x_prompt.shape[0]
    dt = x_prompt.dtype
    sample_kv = ((cache_k_w128, cache_v_w128), (cache_k_w512, cache_v_w512), (cache_k_w2048, cache_v_w2048))
    xp, xs = x_prompt, x_sample
    conv_p, conv_s, ffn_p, ffn_s = [], [], [], []
    kv_p = [[[], []] for _ in range(N_GROUPS)]
    kv_s = [[[], []] for _ in range(N_GROUPS)]
    for l in range(DEPTH):
        params = (norm_attn_g[l], w_in[l], conv_dw_w[l], conv_dw_b[l], conv_ln_g[l], conv_ln_b[l],
                  w_conv_out[l], w_attn_out[l], w_out[l], norm_ffn_g[l], w_up[l], ffn_dw_w[l], ffn_dw_b[l],
                  w_down[l])
        hist_kv_p = [(jnp.zeros((nb, 0, HEADS_PER_GROUP, HEAD_DIM), dt),
                      jnp.zeros((nb, 0, HEADS_PER_GROUP, HEAD_DIM), dt)) for _ in range(N_GROUPS)]
        xp, c_p, nkv_p, f_p = trunk_layer(xp, jnp.zeros((nb, CONV_WIDTH - 1, D_CONV), dt), hist_kv_p,
                                          jnp.zeros((nb, FFN_CONV_WIDTH - 1, 2 * D_FF), dt), *params)
        hist_kv_s = [(kc[l], vc[l]) for kc, vc in sample_kv]
        xs, c_s, nkv_s, f_s = trunk_layer(xs, state_conv[l], hist_kv_s, state_ffn_conv[l], *params)
        conv_p.append(c_p)
        conv_s.append(c_s)
        ffn_p.append(f_p)
        ffn_s.append(f_s)
        for g in range(N_GROUPS):
            kv_p[g][0].append(nkv_p[g][0])
            kv_p[g][1].append(nkv_p[g][1])
            kv_s[g][0].append(nkv_s[g][0])
            kv_s[g][1].append(nkv_s[g][1])
    y_prompt = rms_norm(xp, norm_final_g)
    y_sample = rms_norm(xs, norm_final_g)
    st = lambda lst: jnp.stack(lst, axis=0)
    return (y_prompt, y_sample,
            st(conv_p), st(conv_s),
            st(kv_p[0][0]), st(kv_p[0][1]), st(kv_s[0][0]), st(kv_s[0][1]),
            st(kv_p[1][0]), st(kv_p[1][1]), st(kv_s[1][0]), st(kv_s[1][1]),
            st(kv_p[2][0]), st(kv_p[2][1]), st(kv_s[2][0]), st(kv_s[2][1]),
            st(ffn_p), st(ffn_s))
```

```python
import numpy as np
import concourse.bass as bass
import concourse.mybir as mybir
from concourse.bass_utils import run_bass_kernel_spmd

F32 = mybir.dt.float32
F32R = mybir.dt.float32r
BF16 = mybir.dt.bfloat16
AF = mybir.ActivationFunctionType
ALU = mybir.AluOpType

NCORES = 8
D = 1024
DIN = 6400
DFF = 3072
SEQ = 16384
OWN = SEQ // NCORES
T = 512
NT = 14
NS = NT * T
OWN0 = NS - OWN
NSEQ = 4
LS = 8
TS = NSEQ * LS
GROUPS = ((128, 1), (512, 4), (2048, 16))
EPS = 1e-6
KVW = 576
QOFF, KOFF, VOFF, GCOFF, GAOFF = 2048, 2816, 3584, 4352, 5376

V_GATT = 0
V_GFFN = 16
V_GFIN = 32
V_CB = 40
V_LNG = 56
V_LNB = 72
V_CW = 88
V_FB = V_CW + 2 * 8 * 31
V_FW = V_FB + 96
NV = V_FW + 2 * 48 * 3


class Res:
    __slots__ = ("name", "w", "r")

    def __init__(self, name):
        self.name = name
        self.w = {}
        self.r = {}


class Sched:
    def __init__(self, nc, n_dma_sems=48):
        self.nc = nc
        self.eng = {}
        self.sems = {}
        for name, h in (("pe", nc.tensor), ("act", nc.scalar), ("dve", nc.vector),
                        ("pool", nc.gpsimd), ("sp", nc.sync)):
            self.eng[name] = dict(h=h, sem=nc.alloc_semaphore("s_" + name), cnt=0, seen={})
            self.sems[name] = self.eng[name]["sem"]
        self.dma_sems = []
        for i in range(n_dma_sems):
            k = "d%d" % i
            self.sems[k] = nc.alloc_semaphore("s_" + k)
            self.dma_sems.append(dict(key=k, val=0))
        self.dma_rr = 0
        self.n_inst = 0
        self.n_wait = 0

    def _need(self, reads, writes, skip_key=None):
        need = {}
        for r in reads:
            for k, v in r.w.items():
                if need.get(k, 0) < v:
                    need[k] = v
        for r in writes:
            for k, v in r.w.items():
                if need.get(k, 0) < v:
                    need[k] = v
            for k, v in r.r.items():
                if need.get(k, 0) < v:
                    need[k] = v
        if skip_key is not None:
            need.pop(skip_key, None)
        return need

    def _waits(self, ename, need):
        e = self.eng[ename]
        for k, v in need.items():
            if e["seen"].get(k, 0) >= v:
                continue
            e["h"].wait_ge(self.sems[k], v)
            e["seen"][k] = v
            self.n_wait += 1

    def _commit(self, reads, writes, key, val):
        for r in writes:
            r.w = {key: val}
            r.r = {}
        for r in reads:
            if r.r.get(key, 0) < val:
                r.r[key] = val

    def op(self, ename, fn, reads=(), writes=(), inc=True):
        e = self.eng[ename]
        need = self._need(reads, writes, skip_key=("pe" if ename == "pe" else None))
        self._waits(ename, need)
        ins = fn(e["h"])
        if inc:
            e["cnt"] += 1
            ins.then_inc(e["sem"], 1)
            self._commit(reads, writes, ename, e["cnt"])
        else:
            self._commit(reads, writes, ename, e["cnt"] + 1)
        self.n_inst += 1
        return ins

    def dma(self, qname, out, in_, reads=(), writes=(), merge=False):
        e = self.eng[qname]
        d = self.dma_sems[self.dma_rr]
        self.dma_rr = (self.dma_rr + 1) % len(self.dma_sems)
        if merge:
            need = self._need(reads, ())
            for r in writes:
                for k, v in r.r.items():
                    if need.get(k, 0) < v:
                        need[k] = v
        else:
            need = self._need(reads, writes)
        if d["val"] > 0 and need.get(d["key"], 0) < d["val"]:
            need[d["key"]] = d["val"]
        self._waits(qname, need)
        ins = e["h"].dma_start(out=out, in_=in_)
        d["val"] += 16
        ins.then_inc(self.sems[d["key"]], 16)
        if merge:
            for r in writes:
                r.w[d["key"]] = d["val"]
            self._commit(reads, (), d["key"], d["val"])
        else:
            self._commit(reads, writes, d["key"], d["val"])
        self.n_inst += 1
        return ins

    def finish(self, ename, resources):
        need = self._need(resources, resources)
        self._waits(ename, need)


class Rot:
    def __init__(self, items):
        self.items = items
        self.i = 0

    def next(self):
        it = self.items[self.i]
        self.i = (self.i + 1) % len(self.items)
        return it


def r_(ap):
    return ap


def build_program(max_tiles=None, skip_init=False, skip_out=False, dbg_tile=None):
    nc = bass.Bass("TRN2", target_bir_lowering=False)
    nc.dge_precook = False
    S = Sched(nc)

    def din(name, shape):
        return nc.dram_tensor(name, list(shape), F32, kind="ExternalInput").ap()

    def dout(name, shape):
        return nc.dram_tensor(name, list(shape), F32, kind="ExternalOutput").ap()

    def dscr(name, shape, dt=F32):
        return nc.dram_tensor(name, list(shape), dt, kind="Internal").ap()

    xT = din("xT", [D, NS])
    validT = din("validT", [128, NS])
    validtm_d = din("validtm", [128, NS // 128])
    xsT = din("xsT", [D, TS])
    sconv = din("sconv", [2, D, NSEQ, 30])
    sffn = din("sffn", [2, 2 * DFF, NSEQ, 2])
    ck = [din("ck%d" % g, [2, NSEQ, GROUPS[g][0], 256]) for g in range(3)]
    cv = [din("cv%d" % g, [2, NSEQ, GROUPS[g][0], 256]) for g in range(3)]
    w_in = din("w_in", [2, D, DIN])
    w_conv_out = din("w_conv_out", [2, D, D])
    w_attn_out = din("w_attn_out", [2, 256, D])
    w_out = din("w_out", [2, D, D])
    w_up = din("w_up", [2, D, 2 * DFF])
    w_down = din("w_down", [2, DFF, D])
    vecs_d = din("vecs", [128, NV])
    ident_d = din("ident", [128, 128])
    mprev_d = din("mprev", [128, 4 * 128])
    mcur_d = din("mcur", [128, 4 * 128])

    yT = dout("yT", [D, OWN])
    ysT = dout("ysT", [D, TS])
    ncp = dout("ncp", [2, D, 30])
    ncs = dout("ncs", [2, D, NSEQ, 30])
    nkp = [dout("nkp%d" % g, [2, GROUPS[g][0], 256]) for g in range(3)]
    nvp = [dout("nvp%d" % g, [2, GROUPS[g][0], 256]) for g in range(3)]
    nks = [dout("nks%d" % g, [2, NSEQ, GROUPS[g][0], 256]) for g in range(3)]
    nvs = [dout("nvs%d" % g, [2, NSEQ, GROUPS[g][0], 256]) for g in range(3)]
    nfp = dout("nfp", [2, 2 * DFF, 2])
    nfs = dout("nfs", [2, 2 * DFF, NSEQ, 2])
    out_res = Res("outputs")
    if dbg_tile is not None:
        dbgx = dout("dbgx", [D, T])

    x1scr = dscr("x1scr", [D, NS])
    r_x1 = [Res("x1_%d" % i) for i in range(NT)]
    kvscr = [[dscr("kv_%d_%d" % (l, g), [NS, KVW], BF16) for g in range(3)] for l in range(2)]
    r_kv = [[[Res("kv%d%d_%d" % (l, g, i)) for i in range(NT)] for g in range(3)] for l in range(2)]
    ext = [[dscr("ext_%d_%d" % (l, g), [NSEQ, GROUPS[g][0] + LS, KVW], BF16) for g in range(3)] for l in range(2)]
    r_ext = [[Res("ext%d%d" % (l, g)) for g in range(3)] for l in range(2)]
    WSH = dict(w_in=(D, DIN), w_conv_out=(D, D), w_attn_out=(256, D), w_out=(D, D), w_up=(D, 2 * DFF), w_down=(DFF, D))
    wsrc = dict(w_in=w_in, w_conv_out=w_conv_out, w_attn_out=w_attn_out, w_out=w_out, w_up=w_up, w_down=w_down)
    wbf = {k: [dscr("wb_%s_%d" % (k, l), list(v), BF16) for l in range(2)] for k, v in WSH.items()
           if k == "w_attn_out"}
    wbt = {k: [dscr("wt_%s_%d" % (k, l), [(v[0] // 1024) * (v[1] // 256), 128, 8 * 256], BF16) for l in range(2)]
           for k, v in WSH.items() if k != "w_attn_out"}
    r_wb = {k: [Res("wb_%s_%d" % (k, l)) for l in range(2)] for k in WSH}
    dgscr = [[dscr("dg_%d_%d" % (l, i), [128, 31 * 128], BF16) for i in range(8)] for l in range(2)]
    r_dg = Res("dgscr")

    def sb(name, shape, dt=F32):
        return nc.alloc_sbuf_tensor("sb_" + name, list(shape), dt), Res(name)

    vecs, r_vecs = sb("vecs", [128, NV])
    ident32, r_ident32 = sb("ident32", [128, 128])
    ident, r_ident = sb("ident", [128, 128], BF16)
    mprev, r_mprev = sb("mprev", [128, 4, 128])
    mcur, r_mcur = sb("mcur", [128, 4, 128])
    ones_t, r_ones = sb("ones_t", [128, 128])
    valtm, r_valtm = sb("valtm", [128, NS // 128])
    eps_t, r_eps = sb("eps_t", [128, 1])
    x_sb, r_x = sb("x_sb", [128, 8, T])
    h_sb, r_h = sb("h_sb", [128, 8, T], BF16)
    vmask, r_vmask = sb("vmask", [128, T])
    rstd, r_rstd = sb("rstd", [128, T])
    mean, r_mean = sb("mean", [128, T])
    tmpA, r_tmpA = sb("tmpA", [128, T])
    tmpB, r_tmpB = sb("tmpB", [128, T])
    cbuf, r_cbuf = sb("cbuf", [128, 8, T])
    q_sb, r_q = sb("q_sb", [128, 6, T], BF16)
    acc, r_acc = sb("acc", [128, 4, T])
    attn_sb, r_attn = sb("attn_sb", [64, 4, T], BF16)
    uhist, r_uhist = sb("uhist", [128, 8, NSEQ * 30])
    uwork = Rot([sb("uwork%d" % i, [128, 30 + T], BF16) for i in range(3)])
    uws = Rot([sb("uws%d" % i, [128, NSEQ * (30 + LS)]) for i in range(2)])
    dgbuf = Rot([sb("dg%d" % i, [128, 31, 128], BF16) for i in range(2)])
    bufR, r_bufR = sb("bufR", [128, 16, T], BF16)
    r_b0, r_b1 = Res("bufR0"), Res("bufR1")
    fhist, r_fhist = sb("fhist", [128, 48, NSEQ * 2])
    r_fh = [Res("fh%d" % i) for i in range(48)]
    upext = Rot([sb("upext%d" % i, [128, 2 + T]) for i in range(4)])
    upc = Rot([sb("upc%d" % i, [128, T]) for i in range(6)])
    wslot = Rot([sb("w%d" % i, [128, 8, 256], BF16) for i in range(6)])
    wstage = Rot([sb("wst%d" % i, [128, 8, 256]) for i in range(2)])
    wao, r_wao = sb("wao", [64, 4, 512], BF16)
    kvrow = Rot([sb("kvrow%d" % i, [128, KVW], BF16) for i in range(2)])
    kvrow32 = Rot([sb("kvrow32_%d" % i, [128, 512]) for i in range(2)])
    kvt = Rot([sb("kvt%d" % i, [128, KVW], BF16) for i in range(6)])
    kT = Rot([sb("kT%d" % i, [128, 4, 128], BF16) for i in range(4)])
    p_sb = Rot([sb("p%d" % i, [128, 4, 128], BF16) for i in range(6)])

    def ps(name, shape):
        return nc.alloc_psum_tensor("ps_" + name, list(shape), F32), Res(name)

    mainps = Rot([ps("mps%d" % i, [128, 512]) for i in range(4)])
    sps = Rot([ps("sps%d" % i, [128, 4, 128]) for i in range(2)])
    pvps, r_pvps = ps("pvps", [128, 2, 4, 128])

    S.dma("sp", vecs[:], vecs_d, writes=[r_vecs])
    S.dma("sp", ident32[:], ident_d, writes=[r_ident32])
    S.op("dve", lambda e: e.tensor_copy(out=ident[:], in_=ident32[:]), reads=[r_ident32], writes=[r_ident])
    S.dma("sp", mprev[:], mprev_d.rearrange("p (h q) -> p h q", h=4), writes=[r_mprev])
    S.dma("sp", mcur[:], mcur_d.rearrange("p (h q) -> p h q", h=4), writes=[r_mcur])
    S.op("dve", lambda e: e.memset(ones_t[:], 1.0), writes=[r_ones])
    S.op("dve", lambda e: e.memset(eps_t[:], EPS), writes=[r_eps])
    for rot in (kvt, kT, p_sb):
        for (tt_, rr_) in rot.items:
            S.op("pool", lambda e, tt_=tt_: e.memset(tt_[:], 0.0), writes=[rr_])
    S.dma("sp", valtm[:], validtm_d, writes=[r_valtm])
    cast_rr = [0]

    def cast_op(out_ap, in_ap, reads, writes):
        k = cast_rr[0] % 3
        cast_rr[0] += 1
        if k == 0:
            S.op("act", lambda e: e.activation(out=out_ap, in_=in_ap, func=AF.Copy), reads=reads, writes=writes)
        elif k == 1:
            S.op("pool", lambda e: e.tensor_copy(out=out_ap, in_=in_ap), reads=reads, writes=writes)
        else:
            S.op("dve", lambda e: e.tensor_copy(out=out_ap, in_=in_ap), reads=reads, writes=writes)

    for l in range(2):
        for i in range(8):
            dgt, rdg = dgbuf.next()
            for j in range(31):
                en = "dve" if j % 2 == 0 else "pool"
                S.op(en, lambda e, dgt=dgt, j=j, l=l, i=i: e.tensor_scalar(
                    out=dgt[:, j, :], in0=ident32[:, :], scalar1=vecs[:, V_CW + (l * 8 + i) * 31 + j:V_CW + (l * 8 + i) * 31 + j + 1],
                    scalar2=None, op0=ALU.mult), reads=[r_ident32, r_vecs], writes=[rdg])
            S.dma("pool", dgscr[l][i], dgt[:].rearrange("p j k -> p (j k)"), reads=[rdg], writes=[r_dg], merge=True)
    for l in range(2):
        for name in ("w_in", "w_conv_out", "w_attn_out", "w_out", "w_up", "w_down"):
            R_, C_ = WSH[name]
            src, rdst = wsrc[name][l], r_wb[name][l]
            for r0 in range(0, R_, 1024):
                kch = min(1024, R_ - r0) // 128
                for c0 in range(0, C_, 256):
                    st, rst = wstage.next()
                    S.dma("sp", st[:, 0:kch, :], src[r0:r0 + 128 * kch, c0:c0 + 256].rearrange(
                        "(c p) e -> p c e", p=128), writes=[rst])
                    wt, rw = wslot.next()
                    cast_op(wt[:, 0:kch, :], st[:, 0:kch, :], [rst], [rw])
                    if name == "w_attn_out":
                        S.dma("pool", wbf[name][l][r0:r0 + 128 * kch, c0:c0 + 256].rearrange(
                            "(c p) e -> p c e", p=128), wt[:, 0:kch, :], reads=[rw], writes=[rdst], merge=True)
                    else:
                        bidx = (r0 // 1024) * (C_ // 256) + c0 // 256
                        S.dma("pool", wbt[name][l][bidx].rearrange("p (c e) -> p c e", c=8), wt[:, 0:8, :],
                              reads=[rw], writes=[rdst], merge=True)
    for l in ([] if skip_init else range(2)):
        for g in range(3):
            W = GROUPS[g][0]
            for n in range(NSEQ):
                for r0 in range(0, W, 128):
                    st, rst = wstage.next()
                    stv = st[:, 0:2, :]
                    S.dma("sp", stv[:, 0, :], ck[g][l, n, r0:r0 + 128, :], writes=[rst])
                    S.dma("sp", stv[:, 1, :], cv[g][l, n, r0:r0 + 128, :], writes=[rst], merge=True)
                    kr, rkr = kvrow.next()
                    cast_op(kr[:, 0:512], stv.rearrange("p a b -> p (a b)"), [rst], [rkr])
                    S.op("pool", lambda e, kr=kr: e.memset(kr[:, 512:576], 1.0), writes=[rkr])
                    S.dma("pool", ext[l][g][n, r0:r0 + 128, :], kr[:, :], reads=[rkr], writes=[r_ext[l][g]],
                          merge=True)
                for r0 in range(LS, W, 512):
                    r1 = min(W, r0 + 512)
                    S.dma("pool", nks[g][l, n, r0 - LS:r1 - LS, :], ck[g][l, n, r0:r1, :], writes=[out_res],
                          merge=True)
                    S.dma("pool", nvs[g][l, n, r0 - LS:r1 - LS, :], cv[g][l, n, r0:r1, :], writes=[out_res],
                          merge=True)

    for (st, rst) in wstage.items:
        v16 = st[:, :, :].bitcast(BF16)
        for half in range(2):
            rr = Res("wx")
            rr.w = dict(rst.w)
            rr.r = dict(rst.r)
            wslot.items.append((v16[:, :, 256 * half:256 * half + 256], rr))

    def vcol(off, n=1):
        return vecs[:, off:off + n]

    def load_w(name, l, c0, r0=0, ncols=256, kch=8):
        wt, rw = wslot.next()
        bidx = (r0 // 1024) * (WSH[name][1] // 256) + c0 // 256
        S.dma("sp", wt[:, 0:8, 0:256], wbt[name][l][bidx].rearrange("p (c e) -> p c e", c=8),
              reads=[r_wb[name][l]], writes=[rw])
        return wt, rw

    def rms(l_gain_off, tn, src=x_sb, r_src=None):
        r_src = r_src or r_x
        pt, rp = mainps.next()
        for c in range(8):
            tt, rtt = (tmpA, r_tmpA) if c % 2 == 0 else (tmpB, r_tmpB)
            S.op("act", lambda e, c=c, tt=tt: e.activation(out=tt[:, 0:tn], in_=src[:, c, 0:tn], func=AF.Square),
                 reads=[r_src], writes=[rtt])
            S.op("pe", lambda e, c=c, tt=tt: e.matmul(pt[:, 0:tn], ones_t[:], tt[:, 0:tn], start=(c == 0),
                                                       stop=(c == 7)), reads=[rtt, r_ones], writes=[rp])
        S.op("act", lambda e: e.activation(out=rstd[:, 0:tn], in_=pt[:, 0:tn], func=AF.Sqrt, bias=eps_t[:, 0:1],
                                           scale=1.0 / D), reads=[rp, r_eps], writes=[r_rstd])
        S.op("dve", lambda e: e.reciprocal(out=rstd[:, 0:tn], in_=rstd[:, 0:tn]), reads=[r_rstd], writes=[r_rstd])
        for c in range(8):
            S.op("dve", lambda e, c=c: e.scalar_tensor_tensor(
                out=r_(h_sb[:, c, 0:tn]), in0=src[:, c, 0:tn], scalar=vcol(l_gain_off + c), in1=rstd[:, 0:tn],
                op0=ALU.mult, op1=ALU.mult), reads=[r_src, r_rstd, r_vecs], writes=[r_h])

    def proj_fm(wt, rw, wcol0, rhs_t, r_rhs, rhs_chunk0, tn, kch=8, pt=None, rp=None, start=True, stop=True,
                kparts=128):
        if pt is None:
            pt, rp = mainps.next()
        for c in range(kch):
            S.op("pe", lambda e, c=c: e.matmul(
                pt[:, 0:tn], r_(wt[0:kparts, c, wcol0:wcol0 + 128]), r_(rhs_t[0:kparts, rhs_chunk0 + c, 0:tn]),
                start=(start and c == 0), stop=(stop and c == kch - 1)),
                reads=[rw, r_rhs], writes=[rp], inc=(c == kch - 1))
        return pt, rp

    def attention_unit(l, g, keysrc, r_keys_of_row, qrow0, qcol0, i0, nq, r, d, sample, have_prev=True):
        tiles = []
        if have_prev:
            tiles.append((qrow0 + r + d * (i0 - 128), 128, mprev))
        tiles.append((qrow0 + r + d * i0, nq, mcur))
        qs = qcol0 + r + d * i0
        qsl = slice(qs, qs + d * (nq - 1) + 1, d)
        nqm = nq + (nq % 2)
        qslm = slice(qs, qs + d * (nqm - 1) + 1, d)
        ptiles = []
        for (row0, nk, mk) in tiles:
            kt, rkt = kvt.next()
            rows = keysrc[row0:row0 + d * (nk - 1) + 1:d, :]
            S.dma("pool", kt[0:nk, :], rows, reads=r_keys_of_row(row0, row0 + d * (nk - 1)), writes=[rkt])
            if _CACHE.get("att") == "dma":
                continue
            tp, rtp = mainps.next()
            tp16 = tp[:, :].bitcast(BF16)
            for j in range(2):
                S.op("pe", lambda e, j=j: e.transpose(tp16[:, j * 128:j * 128 + 128], kt[:, j * 128:(j + 1) * 128],
                                                       ident[:, :]),
                     reads=[rkt, r_ident], writes=[rtp], inc=(j == 1))
            ktt, rktt = kT.next()
            tpv = tp16[:, 0:256].rearrange("p (j k) -> p j k", j=2)
            S.op("act", lambda e: e.activation(out=ktt[0:64, 0:4:2, :], in_=tpv[0:64, :, :], func=AF.Copy),
                 reads=[rtp], writes=[rktt])
            S.op("act", lambda e: e.activation(out=ktt[64:128, 1:4:2, :], in_=tpv[64:128, :, :], func=AF.Copy),
                 reads=[rtp], writes=[rktt])
            if _CACHE.get("att") == "tr":
                continue
            sp_, rsp = sps.next()
            for h in range(4):
                S.op("pe", lambda e, h=h: e.matmul(
                    sp_[:, h, 0:nqm], ktt[:, h, :], q_sb[:, 2 * g + h // 2, qslm],
                    start=True, stop=True), reads=[rktt, r_q], writes=[rsp], inc=(h == 3))
            if _CACHE.get("att") == "s":
                continue
            pp, rpp = p_sb.next()
            S.op("act", lambda e: e.activation(out=pp[0:nk, :, 0:nqm], in_=sp_[0:nk, :, 0:nqm], func=AF.Exp),
                 reads=[rsp], writes=[rpp])
            S.op("pool", lambda e, mk=mk: e.tensor_tensor(out=pp[:, :, 0:nqm], in0=pp[:, :, 0:nqm],
                                                           in1=mk[:, :, 0:nqm], op=ALU.mult),
                 reads=[rpp, r_mprev, r_mcur], writes=[rpp])
            ptiles.append((kt, rkt, pp, rpp, nk))
        def back():
            ntile = len(ptiles)
            if _CACHE.get("att") in ("dma", "tr", "s", "exp"):
                return
            for h in range(4):
                for part in range(2):
                    for ti, (kt, rkt, pp, rpp, nk) in enumerate(ptiles):
                        if part == 0:
                            lt = kt[:, 256 + 64 * h:256 + 64 * h + 128]
                        else:
                            lt = kt[:, 448:576]
                        S.op("pe", lambda e, h=h, lt=lt, pp=pp, ti=ti, part=part: e.matmul(
                            pvps[:, part, h, 0:nqm], lt, pp[:, h, 0:nqm],
                            start=(ti == 0), stop=(ti == ntile - 1)), reads=[rkt, rpp, r_ones], writes=[r_pvps],
                            inc=(h == 3 and part == 1 and ti == ntile - 1))
            if _CACHE.get("att") == "pv":
                return
            if g == 0:
                S.op("dve", lambda e: e.tensor_copy(out=acc[0:64, :, qsl], in_=pvps[0:64, 0, :, 0:nq]),
                     reads=[r_pvps], writes=[r_acc])
                S.op("dve", lambda e: e.tensor_copy(out=acc[64:128, :, qsl], in_=pvps[64:128, 1, :, 0:nq]),
                     reads=[r_pvps], writes=[r_acc])
            else:
                S.op("dve", lambda e: e.tensor_tensor(out=acc[0:64, :, qsl], in0=pvps[0:64, 0, :, 0:nq],
                                                      in1=acc[0:64, :, qsl], op=ALU.add),
                     reads=[r_pvps, r_acc], writes=[r_acc])
                S.op("dve", lambda e: e.tensor_tensor(out=acc[64:128, :, qsl], in0=pvps[64:128, 1, :, 0:nq],
                                                      in1=acc[64:128, :, qsl], op=ALU.add),
                     reads=[r_pvps, r_acc], writes=[r_acc])
        return back

    def run_tile(l, ti, kind):
        sample = kind == "sample"
        warm = kind == "warm"
        tn = TS if sample else (128 if warm else T)
        nseq, ls = (NSEQ, LS) if sample else (1, tn)
        tok0 = ti * T + (T - 128 if warm else 0)
        last_layer = l == 1
        Wl = w_in[l]
        if sample:
            if l == 0:
                S.dma("pool", x_sb[:, :, 0:tn], xsT.rearrange("(c p) t -> p c t", p=128), writes=[r_x])
            else:
                S.dma("pool", x_sb[:, :, 0:tn], xs1scr.rearrange("(c p) t -> p c t", p=128),
                      reads=[r_xs1], writes=[r_x])
        else:
            src = xT if l == 0 else x1scr
            rd = [] if l == 0 else [r_x1[ti]]
            S.dma("pool", x_sb[:, :, 0:tn], src[:, tok0:tok0 + tn].rearrange("(c p) t -> p c t", p=128),
                  reads=rd, writes=[r_x])
            if kind != "kv":
                S.dma("pool", vmask[:, 0:tn], validT[:, tok0:tok0 + tn], writes=[r_vmask])
        if _CACHE.get("stop") == "A":
            return
        rms(V_GATT + 8 * l, tn)
        if _CACHE.get("stop") == "B":
            return
        nblk = (tn + 127) // 128
        for g in ([] if warm else range(3)):
            Wg = GROUPS[g][0]
            wk, rwk = load_w("w_in", l, KOFF + 256 * g)
            wv, rwv = load_w("w_in", l, VOFF + 256 * g)
            for b in range(nblk):
                nb = min(128, tn - 128 * b)
                pt, rp = mainps.next()
                for (wt, rw, co) in ((wk, rwk, 0), (wv, rwv, 256)):
                    for c in range(8):
                        S.op("pe", lambda e, c=c, wt=wt, co=co: e.matmul(
                            pt[:, co:co + 256], h_sb[:, c, 128 * b:128 * b + 128], wt[:, c, 0:256],
                            start=(c == 0), stop=(c == 7)), reads=[r_h, rw], writes=[rp], inc=(c == 7))
                kr, rkr = kvrow.next()
                S.op("act", lambda e, kr=kr, pt=pt: e.activation(out=kr[0:nb, 0:512], in_=pt[0:nb, :], func=AF.Copy),
                     reads=[rp], writes=[rkr])
                orow = tok0 + 128 * b - (NS - Wg)
                need32 = sample or (ti >= 10 and orow >= 0)
                if _CACHE.get("cstop") == "mm":
                    continue
                if need32:
                    k32, rk32 = kvrow32.next()
                    S.op("act", lambda e, k32=k32, pt=pt: e.activation(out=k32[:, :], in_=pt[:, :], func=AF.Copy),
                         reads=[rp], writes=[rk32])
                if _CACHE.get("cstop") == "k32":
                    continue
                if sample:
                    S.op("pool", lambda e, kr=kr: e.memset(kr[0:nb, 512:576], 1.0), writes=[rkr])
                    for n in range(NSEQ if _CACHE.get("sdma") != "0" else 0):
                        S.dma("pool", ext[l][g][n, Wg:Wg + LS, :], kr[LS * n:LS * n + LS, :],
                              reads=[rkr], writes=[r_ext[l][g]], merge=True)
                        S.dma("pool", nks[g][l, n, Wg - LS:Wg, :], k32[LS * n:LS * n + LS, 0:256],
                              reads=[rk32], writes=[out_res], merge=True)
                        S.dma("pool", nvs[g][l, n, Wg - LS:Wg, :], k32[LS * n:LS * n + LS, 256:512],
                              reads=[rk32], writes=[out_res], merge=True)
                else:
                    blk = ti * 4 + b
                    S.op("pool", lambda e, kr=kr, blk=blk: e.tensor_scalar(
                        out=kr[0:nb, 512:576], in0=ones_t[0:nb, 0:64], scalar1=valtm[0:nb, blk:blk + 1],
                        scalar2=None, op0=ALU.mult), reads=[r_ones, r_valtm], writes=[rkr])
                    S.dma("pool", kvscr[l][g][tok0 + 128 * b:tok0 + 128 * b + nb, :], kr[0:nb, :],
                          reads=[rkr], writes=[r_kv[l][g][ti]], merge=True)
                    if need32:
                        S.dma("pool", nkp[g][l, orow:orow + nb, :], k32[0:nb, 0:256], reads=[rk32],
                              writes=[out_res], merge=True)
                        S.dma("pool", nvp[g][l, orow:orow + nb, :], k32[0:nb, 256:512], reads=[rk32],
                              writes=[out_res], merge=True)
        if kind == "kv":
            return
        if _CACHE.get("stop") == "C" and kind != "kv":
            return
        for e2 in range(3):
            wq, rwq = load_w("w_in", l, QOFF + 256 * e2)
            for k in range(2):
                e6 = 2 * e2 + k
                pt, rp = proj_fm(wq, rwq, 128 * k, h_sb, r_h, 0, tn)
                S.op("act", lambda e, e6=e6, pt=pt: e.activation(out=q_sb[:, e6, 0:tn], in_=pt[:, 0:tn],
                                                                 func=AF.Copy, scale=0.125), reads=[rp], writes=[r_q])
        if _CACHE.get("stop") == "D" and kind != "kv":
            return
        units = []
        for g in range(int(_CACHE.get("ng", 3))):
            Wg, d = GROUPS[g]
            for n in range(nseq):
                if sample:
                    keysrc = ext[l][g][n]
                    rk = lambda a_, b_, l=l, g=g: [r_ext[l][g]]
                    qrow0, qcol0 = Wg, LS * n
                else:
                    keysrc = kvscr[l][g]
                    rk = lambda a_, b_, l=l, g=g: r_kv[l][g][a_ // T:b_ // T + 1]
                    qrow0, qcol0 = tok0, 0
                for r in range(min(d, ls)):
                    nsub = (ls - r + d - 1) // d
                    for i0 in range(0, nsub, 128):
                        nq = min(128, nsub - i0)
                        units.append((g, keysrc, rk, qrow0, qcol0, i0, nq, r, d))
        uh = uhist[:, :, 0:nseq * 30].rearrange("p c (n t) -> p c n t", n=nseq)
        if sample:
            S.dma("pool", uh, sconv[l].rearrange("(c p) n t -> p c n t", p=128), writes=[r_uhist])
        elif warm:
            S.op("pool", lambda e: e.memset(uhist[:], 0.0), writes=[r_uhist])
        glu_w = {}

        def conv_chunk(i):
            i2, k = i // 2, i % 2
            if k == 0:
                glu_w["a"] = load_w("w_in", l, 256 * i2)
                glu_w["b"] = load_w("w_in", l, 1024 + 256 * i2)
            wa, rwa = glu_w["a"]
            wb, rwb = glu_w["b"]
            pa, rpa = proj_fm(wa, rwa, 128 * k, h_sb, r_h, 0, tn)
            pb, rpb = proj_fm(wb, rwb, 128 * k, h_sb, r_h, 0, tn)
            cdst = cbuf[:, i, 0:tn].rearrange("p (n t) -> p n t", n=nseq)
            cw = V_CW + (l * 8 + i) * 31
            S.op("act", lambda e, pb=pb: e.activation(out=tmpA[:, 0:tn], in_=pb[:, 0:tn], func=AF.Sigmoid),
                 reads=[rpb], writes=[r_tmpA])
            if sample:
                uwt, ruw = uws.next()
                ue = uwt[:, 0:nseq * (30 + ls)].rearrange("p (n t) -> p n t", n=nseq)
                S.op("pool", lambda e, ue=ue, i=i: e.tensor_copy(out=ue[:, :, 0:30], in_=uh[:, i, :, :]),
                     reads=[r_uhist], writes=[ruw])
                S.op("dve", lambda e, pa=pa, ue=ue: e.tensor_tensor(
                    out=ue[:, :, 30:30 + ls], in0=pa[:, 0:tn].rearrange("p (n t) -> p n t", n=nseq),
                    in1=tmpA[:, 0:tn].rearrange("p (n t) -> p n t", n=nseq), op=ALU.mult),
                    reads=[rpa, r_tmpA], writes=[ruw])
                S.op("pool", lambda e, ue=ue, i=i: e.tensor_copy(out=uh[:, i, :, :], in_=ue[:, :, ls:ls + 30]),
                     reads=[ruw], writes=[r_uhist])
                S.op("dve", lambda e, ue=ue, cdst=cdst, cw=cw, i=i: e.tensor_scalar(
                    out=cdst, in0=ue[:, :, 0:ls], scalar1=vcol(cw), scalar2=vcol(V_CB + 8 * l + i),
                    op0=ALU.mult, op1=ALU.add), reads=[ruw, r_vecs], writes=[r_cbuf])
                for j in range(1, 31):
                    S.op("dve", lambda e, ue=ue, j=j, cdst=cdst, cw=cw: e.scalar_tensor_tensor(
                        out=cdst, in0=ue[:, :, j:j + ls], scalar=vcol(cw + j), in1=cdst,
                        op0=ALU.mult, op1=ALU.add), reads=[ruw, r_cbuf, r_vecs], writes=[r_cbuf])
            else:
                uwt, ruw = uwork.next()
                S.op("pool", lambda e, uwt=uwt, i=i: e.tensor_copy(out=uwt[:, 0:30], in_=uhist[:, i, 0:30]),
                     reads=[r_uhist], writes=[ruw])
                S.op("dve", lambda e, pa=pa, uwt=uwt: e.tensor_tensor(out=uwt[:, 30:30 + tn], in0=pa[:, 0:tn],
                                                                      in1=tmpA[:, 0:tn], op=ALU.mult),
                     reads=[rpa, r_tmpA], writes=[ruw])
                S.op("dve", lambda e, pa=pa, i=i: e.tensor_tensor(out=uhist[:, i, 0:30], in0=pa[:, tn - 30:tn],
                                                                   in1=tmpA[:, tn - 30:tn], op=ALU.mult),
                     reads=[rpa, r_tmpA, ruw], writes=[r_uhist])
                dgt, rdg = dgbuf.next()
                S.dma("sp", dgt[:].rearrange("p j k -> p (j k)"), dgscr[l][i], reads=[r_dg], writes=[rdg])
                pc, rpc = mainps.next()
                for j in range(31):
                    S.op("pe", lambda e, j=j, dgt=dgt, uwt=uwt, pc=pc: e.matmul(
                        pc[:, 0:tn], dgt[:, j, :], uwt[:, j:j + tn], start=(j == 0), stop=(j == 30)),
                        reads=[rdg, ruw], writes=[rpc], inc=(j == 30))
                S.op("act", lambda e, pc=pc, i=i: e.activation(out=cbuf[:, i, 0:tn], in_=pc[:, 0:tn],
                                                               func=AF.Identity, bias=vcol(V_CB + 8 * l + i),
                                                               scale=1.0), reads=[rpc, r_vecs], writes=[r_cbuf])
        stride = max(1, (len(units) + 7) // 8)
        next_chunk = 0
        pend_back = None
        for ui, (g, keysrc, rk, qrow0, qcol0, i0, nq, r, d) in enumerate(units):
            bk = attention_unit(l, g, keysrc, rk, qrow0, qcol0, i0, nq, r, d, sample)
            if pend_back is not None:
                pend_back()
            pend_back = bk
            if ui % stride == stride - 1 and next_chunk < 8:
                conv_chunk(next_chunk)
                next_chunk += 1
        if pend_back is not None:
            pend_back()
        while next_chunk < 8:
            conv_chunk(next_chunk)
            next_chunk += 1
        S.op("dve", lambda e: e.tensor_scalar(out=acc[64:128, :, 0:tn], in0=acc[64:128, :, 0:tn], scalar1=1e-30,
                                              scalar2=None, op0=ALU.max), reads=[r_acc], writes=[r_acc])
        for h, (tt_, rt_) in enumerate(((tmpA, r_tmpA), (tmpB, r_tmpB), (rstd, r_rstd), (mean, r_mean))):
            S.op("dve", lambda e, h=h, tt_=tt_: e.reciprocal(out=tt_[0:64, 0:tn], in_=acc[64:128, h, 0:tn]),
                 reads=[r_acc], writes=[rt_])
            S.op("dve", lambda e, h=h, tt_=tt_: e.tensor_tensor(out=attn_sb[:, h, 0:tn], in0=acc[0:64, h, 0:tn],
                                                                in1=tt_[0:64, 0:tn], op=ALU.mult),
                 reads=[r_acc, rt_], writes=[r_attn])
        if sample:
            S.dma("pool", ncs[l].rearrange("(c p) n t -> p c n t", p=128), uh, reads=[r_uhist], writes=[out_res], merge=True)
        elif ti == NT - 1:
            S.dma("pool", ncp[l].rearrange("(c p) t -> p c t", p=128), uhist[:, :, 0:30],
                  reads=[r_uhist], writes=[out_res], merge=True)
        psum_sum, r_psum = mainps.next()
        for i in range(8):
            S.op("pe", lambda e, i=i: e.matmul(psum_sum[:, 0:tn], ones_t[:], cbuf[:, i, 0:tn],
                                                start=(i == 0), stop=(i == 7)),
                 reads=[r_cbuf, r_ones], writes=[r_psum], inc=(i == 7))
        psq, r_psq = mainps.next()
        for i in range(8):
            tt, rtt = (tmpA, r_tmpA) if i % 2 == 0 else (tmpB, r_tmpB)
            S.op("act", lambda e, i=i, tt=tt: e.activation(out=tt[:, 0:tn], in_=cbuf[:, i, 0:tn], func=AF.Square),
                 reads=[r_cbuf], writes=[rtt])
            S.op("pe", lambda e, i=i, tt=tt: e.matmul(psq[:, 0:tn], ones_t[:], tt[:, 0:tn],
                                                       start=(i == 0), stop=(i == 7)),
                 reads=[rtt, r_ones], writes=[r_psq])
        S.op("act", lambda e: e.activation(out=mean[:, 0:tn], in_=psum_sum[:, 0:tn], func=AF.Copy, scale=1.0 / D),
             reads=[r_psum], writes=[r_mean])
        S.op("dve", lambda e: e.tensor_tensor(out=tmpA[:, 0:tn], in0=mean[:, 0:tn], in1=mean[:, 0:tn], op=ALU.mult),
             reads=[r_mean], writes=[r_tmpA])
        S.op("dve", lambda e: e.scalar_tensor_tensor(out=tmpA[:, 0:tn], in0=psq[:, 0:tn], scalar=1.0 / D,
                                                     in1=tmpA[:, 0:tn], op0=ALU.mult, op1=ALU.subtract),
             reads=[r_psq, r_tmpA], writes=[r_tmpA])
        S.op("dve", lambda e: e.tensor_scalar(out=tmpA[:, 0:tn], in0=tmpA[:, 0:tn], scalar1=0.0, scalar2=None,
                                              op0=ALU.max), reads=[r_tmpA], writes=[r_tmpA])
        S.op("act", lambda e: e.activation(out=tmpA[:, 0:tn], in_=tmpA[:, 0:tn], func=AF.Sqrt, bias=eps_t[:, 0:1],
                                           scale=1.0), reads=[r_tmpA, r_eps], writes=[r_tmpA])
        S.op("dve", lambda e: e.reciprocal(out=tmpA[:, 0:tn], in_=tmpA[:, 0:tn]), reads=[r_tmpA], writes=[r_tmpA])
        for i in range(8):
            S.op("dve", lambda e, i=i: e.tensor_tensor(out=cbuf[:, i, 0:tn], in0=cbuf[:, i, 0:tn],
                                                       in1=mean[:, 0:tn], op=ALU.subtract),
                 reads=[r_cbuf, r_mean], writes=[r_cbuf])
            S.op("pool", lambda e, i=i: e.tensor_tensor(out=cbuf[:, i, 0:tn], in0=cbuf[:, i, 0:tn],
                                                        in1=tmpA[:, 0:tn], op=ALU.mult),
                 reads=[r_cbuf, r_tmpA], writes=[r_cbuf])
            S.op("act", lambda e, i=i: e.activation(out=r_(bufR[:, i, 0:tn]), in_=cbuf[:, i, 0:tn], func=AF.Silu,
                                                    bias=vcol(V_LNB + 8 * l + i), scale=vcol(V_LNG + 8 * l + i)),
                 reads=[r_cbuf, r_vecs], writes=[r_b0])
        if _CACHE.get("stop") == "G" and kind != "kv":
            return
        for j2 in range(4):
            if j2 % 2 == 0:
                S.dma("sp", wao[:], wbf["w_attn_out"][l][:, 256 * j2:256 * j2 + 512].rearrange(
                    "(h p) e -> p h e", p=64), reads=[r_wb["w_attn_out"][l]], writes=[r_wao])
            wgc, rwgc = load_w("w_in", l, GCOFF + 256 * j2)
            wga, rwga = load_w("w_in", l, GAOFF + 256 * j2)
            wco, rwco = load_w("w_conv_out", l, 256 * j2)
            for k in range(2):
                j = 2 * j2 + k
                pgc, rpgc = proj_fm(wgc, rwgc, 128 * k, h_sb, r_h, 0, tn)
                S.op("act", lambda e, pgc=pgc: e.activation(out=tmpA[:, 0:tn], in_=pgc[:, 0:tn], func=AF.Sigmoid),
                     reads=[rpgc], writes=[r_tmpA])
                pga, rpga = proj_fm(wga, rwga, 128 * k, h_sb, r_h, 0, tn)
                S.op("act", lambda e, pga=pga: e.activation(out=tmpB[:, 0:tn], in_=pga[:, 0:tn], func=AF.Sigmoid),
                     reads=[rpga], writes=[r_tmpB])
                pbc, rpbc = proj_fm(wco, rwco, 128 * k, bufR, r_b0, 0, tn)
                S.op("dve", lambda e, pbc=pbc: e.tensor_tensor(out=tmpA[:, 0:tn], in0=pbc[:, 0:tn], in1=tmpA[:, 0:tn],
                                                               op=ALU.mult), reads=[rpbc, r_tmpA], writes=[r_tmpA])
                pba, rpba = mainps.next()
                wc0 = 128 * (j % 4)
                for h in range(4):
                    S.op("pe", lambda e, h=h, wc0=wc0, pba=pba: e.matmul(
                        pba[:, 0:tn], r_(wao[:, h, wc0:wc0 + 128]), r_(attn_sb[:, h, 0:tn]),
                        start=(h == 0), stop=(h == 3)), reads=[r_wao, r_attn], writes=[rpba], inc=(h == 3))
                S.op("dve", lambda e, pba=pba: e.tensor_tensor(out=tmpB[:, 0:tn], in0=pba[:, 0:tn], in1=tmpB[:, 0:tn],
                                                               op=ALU.mult), reads=[rpba, r_tmpB], writes=[r_tmpB])
                S.op("pool", lambda e, j=j: e.tensor_tensor(out=r_(bufR[:, 8 + j, 0:tn]), in0=tmpA[:, 0:tn],
                                                            in1=tmpB[:, 0:tn], op=ALU.add),
                     reads=[r_tmpA, r_tmpB], writes=[r_b1])
        if _CACHE.get("stop") == "H" and kind != "kv":
            return
        for j2 in range(4):
            wo, rwo = load_w("w_out", l, 256 * j2)
            for k in range(2):
                j = 2 * j2 + k
                po, rpo = proj_fm(wo, rwo, 128 * k, bufR, r_b1, 8, tn)
                S.op("dve", lambda e, j=j, po=po: e.tensor_tensor(out=x_sb[:, j, 0:tn], in0=po[:, 0:tn],
                                                                  in1=x_sb[:, j, 0:tn], op=ALU.add),
                     reads=[rpo, r_x], writes=[r_x])
                if not sample:
                    S.op("pool", lambda e, j=j: e.tensor_tensor(out=x_sb[:, j, 0:tn], in0=x_sb[:, j, 0:tn],
                                                                in1=vmask[:, 0:tn], op=ALU.mult),
                         reads=[r_x, r_vmask], writes=[r_x])
        if _CACHE.get("stop") == "I" and kind != "kv":
            return
        rms(V_GFFN + 8 * l, tn)
        fh = fhist[:, :, 0:nseq * 2].rearrange("p c (n t) -> p c n t", n=nseq)
        if sample:
            S.dma("pool", fh, sffn[l].rearrange("(c p) n t -> p c n t", p=128), writes=r_fh)
        elif warm:
            S.op("pool", lambda e: e.memset(fhist[:], 0.0), writes=r_fh)
        Wu = w_up[l]
        Wd = w_down[l]
        for ip in range(3):
            for i2 in range(4):
                wv_, rwv_ = load_w("w_up", l, 1024 * ip + 256 * i2)
                wg_, rwg_ = load_w("w_up", l, DFF + 1024 * ip + 256 * i2)
                for k in range(2):
                    il = 2 * i2 + k
                    i = 8 * ip + il
                    res_c = []
                    for (wt, rw, idx) in ((wv_, rwv_, i), (wg_, rwg_, 24 + i)):
                        pu, rpu = proj_fm(wt, rw, 128 * k, h_sb, r_h, 0, tn)
                        uxt, ruxt = upext.next()
                        ux = uxt[:, 0:nseq * (2 + ls)].rearrange("p (n t) -> p n t", n=nseq)
                        S.op("pool", lambda e, ux=ux, idx=idx: e.tensor_copy(out=ux[:, :, 0:2], in_=fh[:, idx, :, :]),
                             reads=[r_fh[idx]], writes=[ruxt])
                        S.op("act", lambda e, ux=ux, pu=pu: e.activation(
                            out=ux[:, :, 2:2 + ls], in_=pu[:, 0:tn].rearrange("p (n t) -> p n t", n=nseq),
                            func=AF.Copy), reads=[rpu], writes=[ruxt])
                        S.op("pool", lambda e, ux=ux, idx=idx: e.tensor_copy(out=fh[:, idx, :, :],
                                                                             in_=ux[:, :, ls:ls + 2]),
                             reads=[ruxt], writes=[r_fh[idx]])
                        uct, ruc = upc.next()
                        uc = uct[:, 0:tn].rearrange("p (n t) -> p n t", n=nseq)
                        fw = V_FW + (l * 48 + idx) * 3
                        S.op("act", lambda e, ux=ux, uc=uc, fw=fw, idx=idx: e.activation(
                            out=uc, in_=ux[:, :, 0:ls], func=AF.Identity, scale=vcol(fw),
                            bias=vcol(V_FB + 48 * l + idx)), reads=[ruxt, r_vecs], writes=[ruc])
                        for j in (1, 2):
                            S.op("dve", lambda e, ux=ux, uc=uc, fw=fw, j=j: e.scalar_tensor_tensor(
                                out=uc, in0=ux[:, :, j:j + ls], scalar=vcol(fw + j), in1=uc,
                                op0=ALU.mult, op1=ALU.add), reads=[ruxt, ruc, r_vecs], writes=[ruc])
                        res_c.append((uct, ruc))
                    (vt, rv), (gt, rg) = res_c
                    S.op("act", lambda e, gt=gt: e.activation(out=gt[:, 0:tn], in_=gt[:, 0:tn], func=AF.Silu),
                         reads=[rg], writes=[rg])
                    S.op("dve", lambda e, vt=vt, gt=gt, il=il: e.tensor_tensor(
                        out=r_(bufR[:, il, 0:tn]), in0=vt[:, 0:tn], in1=gt[:, 0:tn], op=ALU.mult),
                        reads=[rv, rg], writes=[r_b0])
            for j2 in range(4):
                wd_, rwd_ = load_w("w_down", l, 256 * j2, r0=1024 * ip)
                for k in range(2):
                    j = 2 * j2 + k
                    pd, rpd = proj_fm(wd_, rwd_, 128 * k, bufR, r_b0, 0, tn)
                    S.op("dve", lambda e, j=j, pd=pd: e.tensor_tensor(out=x_sb[:, j, 0:tn], in0=pd[:, 0:tn],
                                                                      in1=x_sb[:, j, 0:tn], op=ALU.add),
                         reads=[rpd, r_x], writes=[r_x])
                    if ip == 2 and not sample:
                        S.op("pool", lambda e, j=j: e.tensor_tensor(out=x_sb[:, j, 0:tn], in0=x_sb[:, j, 0:tn],
                                                                    in1=vmask[:, 0:tn], op=ALU.mult),
                             reads=[r_x, r_vmask], writes=[r_x])
        if sample:
            S.dma("pool", nfs[l].rearrange("(c p) n t -> p c n t", p=128), fh, reads=r_fh, writes=[out_res], merge=True)
        elif ti == NT - 1:
            S.dma("pool", nfp[l].rearrange("(c p) t -> p c t", p=128), fhist[:, :, 0:2],
                  reads=r_fh, writes=[out_res], merge=True)
        if _CACHE.get("stop") == "J" and kind != "kv":
            return
        if warm:
            return
        if not last_layer:
            if sample:
                S.dma("pool", xs1scr.rearrange("(c p) t -> p c t", p=128), x_sb[:, :, 0:tn],
                      reads=[r_x], writes=[r_xs1])
            else:
                S.dma("pool", x1scr[:, tok0:tok0 + T].rearrange("(c p) t -> p c t", p=128), x_sb[:, :, :],
                      reads=[r_x], writes=[r_x1[ti]])
        else:
            if sample or tok0 >= OWN0:
                rms(V_GFIN, tn)
                if sample:
                    S.dma("pool", ysT.rearrange("(c p) t -> p c t", p=128), h_sb[:, :, 0:tn],
                          reads=[r_h], writes=[out_res], merge=True)
                else:
                    S.dma("pool", yT[:, tok0 - OWN0:tok0 - OWN0 + T].rearrange("(c p) t -> p c t", p=128),
                          h_sb[:, :, :], reads=[r_h], writes=[out_res], merge=True)

    xs1scr = dscr("xs1scr", [D, TS])
    r_xs1 = Res("xs1")

    ntile_done = 0
    for l in range(2):
        first_full = 4 if l == 0 else 9
        first_kv = 0 if l == 0 else 5
        sched = ([(ti, "kv") for ti in range(first_kv, first_full + 1)] + [(first_full, "warm")]
                 + [(ti, "full") for ti in range(first_full + 1, NT)] + [(0, "sample")])
        for (ti, kind) in sched:
            if max_tiles is not None and ntile_done >= max_tiles:
                continue
            if _CACHE.get("only") == "sample" and kind != "sample":
                ntile_done += 1
                continue
            run_tile(l, ti, kind)
            ntile_done += 1
            if dbg_tile == (l, ti, kind):
                tn_ = TS if kind == "sample" else T
                S.dma("pool", dbgx[:, 0:tn_].rearrange("(c p) t -> p c t", p=128), x_sb[:, :, 0:tn_],
                      reads=[r_x], writes=[out_res], merge=True)
    S.finish("pool", [out_res])
    S.finish("sp", [out_res])
    return nc, S


_CACHE = {}


def _consts():
    jj = np.arange(128)[:, None]
    ii = np.arange(128)[None, :]
    mprev = np.tile((jj >= ii).astype(np.float32), (1, 4))
    mcur = np.tile((jj <= ii).astype(np.float32), (1, 4))
    return mprev, mcur


def _pack_vecs(norm_attn_g, norm_ffn_g, norm_final_g, conv_dw_b, conv_ln_g, conv_ln_b, conv_dw_w, ffn_dw_b, ffn_dw_w):
    v = np.zeros((128, NV), np.float32)

    def pc(a):
        a = np.asarray(a, np.float32)
        c = a.shape[-1] // 128
        a = a.reshape(a.shape[:-1] + (c, 128))
        return np.moveaxis(a, -1, 0)

    v[:, V_GATT:V_GATT + 16] = pc(norm_attn_g).reshape(128, 16)
    v[:, V_GFFN:V_GFFN + 16] = pc(norm_ffn_g).reshape(128, 16)
    v[:, V_GFIN:V_GFIN + 8] = pc(norm_final_g).reshape(128, 8)
    v[:, V_CB:V_CB + 16] = pc(conv_dw_b).reshape(128, 16)
    v[:, V_LNG:V_LNG + 16] = pc(conv_ln_g).reshape(128, 16)
    v[:, V_LNB:V_LNB + 16] = pc(conv_ln_b).reshape(128, 16)
    cw = pc(conv_dw_w)
    v[:, V_CW:V_CW + 2 * 8 * 31] = np.transpose(cw, (0, 1, 3, 2)).reshape(128, -1)
    v[:, V_FB:V_FB + 96] = pc(ffn_dw_b).reshape(128, 96)
    fw = pc(ffn_dw_w)
    v[:, V_FW:V_FW + 2 * 48 * 3] = np.transpose(fw, (0, 1, 3, 2)).reshape(128, -1)
    return v


def kernel(x_prompt, x_sample, state_conv, cache_k_w128, cache_v_w128, cache_k_w512, cache_v_w512,
           cache_k_w2048, cache_v_w2048, state_ffn_conv, norm_attn_g, w_in, conv_dw_w, conv_dw_b,
           conv_ln_g, conv_ln_b, w_conv_out, w_attn_out, w_out, norm_ffn_g, w_up, ffn_dw_w, ffn_dw_b,
           w_down, norm_final_g):
    f = lambda a: np.ascontiguousarray(np.asarray(a, dtype=np.float32))
    if "nc" not in _CACHE:
        _CACHE["nc"] = build_program()
    nc, S = _CACHE["nc"]
    mprev, mcur = _consts()
    vecs = _pack_vecs(norm_attn_g, norm_ffn_g, norm_final_g, conv_dw_b, conv_ln_g, conv_ln_b, conv_dw_w,
                      ffn_dw_b, ffn_dw_w)
    xp = f(x_prompt)[0]
    xpT_pad = np.zeros((D, OWN0 + SEQ), np.float32)
    xpT_pad[:, OWN0:] = xp.T
    ck = [f(cache_k_w128), f(cache_k_w512), f(cache_k_w2048)]
    cv = [f(cache_v_w128), f(cache_v_w512), f(cache_v_w2048)]
    shared = dict(w_in=f(w_in), w_conv_out=f(w_conv_out), w_attn_out=f(w_attn_out), w_out=f(w_out), w_up=f(w_up),
                  w_down=f(w_down), vecs=vecs, ident=np.eye(128, dtype=np.float32), mprev=mprev, mcur=mcur,
                  )
    xs = f(x_sample)
    sc = f(state_conv)
    sf = f(state_ffn_conv)
    in_maps = []
    for c in range(NCORES):
        s = c * OWN
        m = dict(shared)
        m["xT"] = np.ascontiguousarray(xpT_pad[:, s:s + NS])
        val = (np.arange(NS) + s - OWN0 >= 0).astype(np.float32)
        m["validT"] = np.ascontiguousarray(np.broadcast_to(val[None, :], (128, NS)))
        m["validtm"] = np.ascontiguousarray(val.reshape(NS // 128, 128).T)
        n0 = c * NSEQ
        m["xsT"] = np.ascontiguousarray(xs[n0:n0 + NSEQ].reshape(TS, D).T)
        m["sconv"] = np.ascontiguousarray(np.transpose(sc[:, n0:n0 + NSEQ], (0, 3, 1, 2)))
        m["sffn"] = np.ascontiguousarray(np.transpose(sf[:, n0:n0 + NSEQ], (0, 3, 1, 2)))
        for g in range(3):
            W = GROUPS[g][0]
            m["ck%d" % g] = np.ascontiguousarray(ck[g][:, n0:n0 + NSEQ].reshape(2, NSEQ, W, 256))
            m["cv%d" % g] = np.ascontiguousarray(cv[g][:, n0:n0 + NSEQ].reshape(2, NSEQ, W, 256))
        in_maps.append(m)
    if _CACHE.get("dbg_in_maps_only"):
        return in_maps
    res = run_bass_kernel_spmd(nc, in_maps, core_ids=list(range(NCORES)))
    R = res.results
    y_prompt = np.concatenate([R[c]["yT"].T for c in range(NCORES)], axis=0)[None]
    y_sample = np.concatenate([R[c]["ysT"].T.reshape(NSEQ, LS, D) for c in range(NCORES)], axis=0)
    last = R[NCORES - 1]
    conv_p = np.transpose(last["ncp"], (0, 2, 1))[:, None]
    conv_s = np.concatenate([np.transpose(R[c]["ncs"], (0, 2, 3, 1)) for c in range(NCORES)], axis=1)
    outs = [np.ascontiguousarray(y_prompt), np.ascontiguousarray(y_sample),
            np.ascontiguousarray(conv_p), np.ascontiguousarray(conv_s)]
    for g in range(3):
        W = GROUPS[g][0]
        outs.append(np.ascontiguousarray(last["nkp%d" % g].reshape(2, 1, W, 4, 64)))
        outs.append(np.ascontiguousarray(last["nvp%d" % g].reshape(2, 1, W, 4, 64)))
        outs.append(np.concatenate([R[c]["nks%d" % g].reshape(2, NSEQ, W, 4, 64) for c in range(NCORES)], axis=1))
        outs.append(np.concatenate([R[c]["nvs%d" % g].reshape(2, NSEQ, W, 4, 64) for c in range(NCORES)], axis=1))
    ffn_p = np.transpose(last["nfp"], (0, 2, 1))[:, None]
    ffn_s = np.concatenate([np.transpose(R[c]["nfs"], (0, 2, 3, 1)) for c in range(NCORES)], axis=1)
    outs.append(np.ascontiguousarray(ffn_p))
    outs.append(np.ascontiguousarray(ffn_s))
    return tuple(np.asarray(o, dtype=np.float32) for o in outs)
```

```python
import numpy as np
import concourse.bass as bass
import concourse.mybir as mybir
from concourse.bass_utils import run_bass_kernel_spmd

F32 = mybir.dt.float32
F32R = mybir.dt.float32r
BF16 = mybir.dt.bfloat16
AF = mybir.ActivationFunctionType
ALU = mybir.AluOpType

NCORES = 8
D = 1024
DIN = 6400
DFF = 3072
SEQ = 16384
OWN = SEQ // NCORES
T = 512
NT = 14
NS = NT * T
OWN0 = NS - OWN
NSEQ = 4
LS = 8
TS = NSEQ * LS
GROUPS = ((128, 1), (512, 4), (2048, 16))
EPS = 1e-6
KVW = 576
QOFF, KOFF, VOFF, GCOFF, GAOFF = 2048, 2816, 3584, 4352, 5376

V_GATT = 0
V_GFFN = 16
V_GFIN = 32
V_CB = 40
V_LNG = 56
V_LNB = 72
V_CW = 88
V_FB = V_CW + 2 * 8 * 31
V_FW = V_FB + 96
NV = V_FW + 2 * 48 * 3


class Res:
    __slots__ = ("name", "w", "r")

    def __init__(self, name):
        self.name = name
        self.w = {}
        self.r = {}


class Sched:
    def __init__(self, nc, n_dma_sems=48):
        self.nc = nc
        self.eng = {}
        self.sems = {}
        for name, h in (("pe", nc.tensor), ("act", nc.scalar), ("dve", nc.vector),
                        ("pool", nc.gpsimd), ("sp", nc.sync)):
            self.eng[name] = dict(h=h, sem=nc.alloc_semaphore("s_" + name), cnt=0, seen={})
            self.sems[name] = self.eng[name]["sem"]
        self.dma_sems = []
        for i in range(n_dma_sems):
            k = "d%d" % i
            self.sems[k] = nc.alloc_semaphore("s_" + k)
            self.dma_sems.append(dict(key=k, val=0))
        self.dma_rr = 0
        self.n_inst = 0
        self.n_wait = 0

    def _need(self, reads, writes, skip_key=None):
        need = {}
        for r in reads:
            for k, v in r.w.items():
                if need.get(k, 0) < v:
                    need[k] = v
        for r in writes:
            for k, v in r.w.items():
                if need.get(k, 0) < v:
                    need[k] = v
            for k, v in r.r.items():
                if need.get(k, 0) < v:
                    need[k] = v
        if skip_key is not None:
            need.pop(skip_key, None)
        return need

    def _waits(self, ename, need):
        e = self.eng[ename]
        for k, v in need.items():
            if e["seen"].get(k, 0) >= v:
                continue
            e["h"].wait_ge(self.sems[k], v)
            e["seen"][k] = v
            self.n_wait += 1

    def _commit(self, reads, writes, key, val):
        for r in writes:
            r.w = {key: val}
            r.r = {}
        for r in reads:
            if r.r.get(key, 0) < val:
                r.r[key] = val

    def op(self, ename, fn, reads=(), writes=(), inc=True):
        e = self.eng[ename]
        need = self._need(reads, writes, skip_key=("pe" if ename == "pe" else None))
        self._waits(ename, need)
        ins = fn(e["h"])
        if inc:
            e["cnt"] += 1
            ins.then_inc(e["sem"], 1)
            self._commit(reads, writes, ename, e["cnt"])
        else:
            self._commit(reads, writes, ename, e["cnt"] + 1)
        self.n_inst += 1
        return ins

    def dma(self, qname, out, in_, reads=(), writes=(), merge=False):
        e = self.eng[qname]
        d = self.dma_sems[self.dma_rr]
        self.dma_rr = (self.dma_rr + 1) % len(self.dma_sems)
        if merge:
            need = self._need(reads, ())
            for r in writes:
                for k, v in r.r.items():
                    if need.get(k, 0) < v:
                        need[k] = v
        else:
            need = self._need(reads, writes)
        if d["val"] > 0 and need.get(d["key"], 0) < d["val"]:
            need[d["key"]] = d["val"]
        self._waits(qname, need)
        ins = e["h"].dma_start(out=out, in_=in_)
        d["val"] += 16
        ins.then_inc(self.sems[d["key"]], 16)
        if merge:
            for r in writes:
                r.w[d["key"]] = d["val"]
            self._commit(reads, (), d["key"], d["val"])
        else:
            self._commit(reads, writes, d["key"], d["val"])
        self.n_inst += 1
        return ins

    def finish(self, ename, resources):
        need = self._need(resources, resources)
        self._waits(ename, need)


class Rot:
    def __init__(self, items):
        self.items = items
        self.i = 0

    def next(self):
        it = self.items[self.i]
        self.i = (self.i + 1) % len(self.items)
        return it


def r_(ap):
    return ap


def build_program(max_tiles=None, skip_init=False, skip_out=False, dbg_tile=None):
    nc = bass.Bass("TRN2", target_bir_lowering=False)
    nc.dge_precook = False
    S = Sched(nc)

    def din(name, shape):
        return nc.dram_tensor(name, list(shape), F32, kind="ExternalInput").ap()

    def dout(name, shape):
        return nc.dram_tensor(name, list(shape), F32, kind="ExternalOutput").ap()

    def dscr(name, shape, dt=F32):
        return nc.dram_tensor(name, list(shape), dt, kind="Internal").ap()

    xT = din("xT", [D, NS])
    validT = din("validT", [128, NS])
    validtm_d = din("validtm", [128, NS // 128])
    xsT = din("xsT", [D, TS])
    sconv = din("sconv", [2, D, NSEQ, 30])
    sffn = din("sffn", [2, 2 * DFF, NSEQ, 2])
    ck = [din("ck%d" % g, [2, NSEQ, GROUPS[g][0], 256]) for g in range(3)]
    cv = [din("cv%d" % g, [2, NSEQ, GROUPS[g][0], 256]) for g in range(3)]
    w_in = din("w_in", [2, D, DIN])
    w_conv_out = din("w_conv_out", [2, D, D])
    w_attn_out = din("w_attn_out", [2, 256, D])
    w_out = din("w_out", [2, D, D])
    w_up = din("w_up", [2, D, 2 * DFF])
    w_down = din("w_down", [2, DFF, D])
    vecs_d = din("vecs", [128, NV])
    ident_d = din("ident", [128, 128])
    mprev_d = din("mprev", [128, 4 * 128])
    mcur_d = din("mcur", [128, 4 * 128])

    yT = dout("yT", [D, OWN])
    ysT = dout("ysT", [D, TS])
    ncp = dout("ncp", [2, D, 30])
    ncs = dout("ncs", [2, D, NSEQ, 30])
    nkp = [dout("nkp%d" % g, [2, GROUPS[g][0], 256]) for g in range(3)]
    nvp = [dout("nvp%d" % g, [2, GROUPS[g][0], 256]) for g in range(3)]
    nks = [dout("nks%d" % g, [2, NSEQ, GROUPS[g][0], 256]) for g in range(3)]
    nvs = [dout("nvs%d" % g, [2, NSEQ, GROUPS[g][0], 256]) for g in range(3)]
    nfp = dout("nfp", [2, 2 * DFF, 2])
    nfs = dout("nfs", [2, 2 * DFF, NSEQ, 2])
    out_res = Res("outputs")
    if dbg_tile is not None:
        dbgx = dout("dbgx", [D, T])

    x1scr = dscr("x1scr", [D, NS])
    r_x1 = [Res("x1_%d" % i) for i in range(NT)]
    kvscr = [[dscr("kv_%d_%d" % (l, g), [NS, KVW], BF16) for g in range(3)] for l in range(2)]
    r_kv = [[[Res("kv%d%d_%d" % (l, g, i)) for i in range(NT)] for g in range(3)] for l in range(2)]
    ext = [[dscr("ext_%d_%d" % (l, g), [NSEQ, GROUPS[g][0] + LS, KVW], BF16) for g in range(3)] for l in range(2)]
    r_ext = [[Res("ext%d%d" % (l, g)) for g in range(3)] for l in range(2)]
    WSH = dict(w_in=(D, DIN), w_conv_out=(D, D), w_attn_out=(256, D), w_out=(D, D), w_up=(D, 2 * DFF), w_down=(DFF, D))
    wsrc = dict(w_in=w_in, w_conv_out=w_conv_out, w_attn_out=w_attn_out, w_out=w_out, w_up=w_up, w_down=w_down)
    wbf = {k: [dscr("wb_%s_%d" % (k, l), list(v), BF16) for l in range(2)] for k, v in WSH.items()
           if k == "w_attn_out"}
    wbt = {k: [dscr("wt_%s_%d" % (k, l), [(v[0] // 1024) * (v[1] // 256), 128, 8 * 256], BF16) for l in range(2)]
           for k, v in WSH.items() if k != "w_attn_out"}
    r_wb = {k: [Res("wb_%s_%d" % (k, l)) for l in range(2)] for k in WSH}
    dgscr = [[dscr("dg_%d_%d" % (l, i), [128, 31 * 128], BF16) for i in range(8)] for l in range(2)]
    r_dg = Res("dgscr")

    def sb(name, shape, dt=F32):
        return nc.alloc_sbuf_tensor("sb_" + name, list(shape), dt), Res(name)

    vecs, r_vecs = sb("vecs", [128, NV])
    ident32, r_ident32 = sb("ident32", [128, 128])
    ident, r_ident = sb("ident", [128, 128], BF16)
    mprev, r_mprev = sb("mprev", [128, 4, 128])
    mcur, r_mcur = sb("mcur", [128, 4, 128])
    ones_t, r_ones = sb("ones_t", [128, 128])
    valtm, r_valtm = sb("valtm", [128, NS // 128])
    eps_t, r_eps = sb("eps_t", [128, 1])
    x_sb, r_x = sb("x_sb", [128, 8, T])
    h_sb, r_h = sb("h_sb", [128, 8, T], BF16)
    vmask, r_vmask = sb("vmask", [128, T])
    rstd, r_rstd = sb("rstd", [128, T])
    mean, r_mean = sb("mean", [128, T])
    tmpA, r_tmpA = sb("tmpA", [128, T])
    tmpB, r_tmpB = sb("tmpB", [128, T])
    cbuf, r_cbuf = sb("cbuf", [128, 8, T])
    q_sb, r_q = sb("q_sb", [128, 6, T], BF16)
    acc, r_acc = sb("acc", [128, 4, T])
    attn_sb, r_attn = sb("attn_sb", [64, 4, T], BF16)
    uhist, r_uhist = sb("uhist", [128, 8, NSEQ * 30])
    uwork = Rot([sb("uwork%d" % i, [128, 30 + T], BF16) for i in range(3)])
    uws = Rot([sb("uws%d" % i, [128, NSEQ * (30 + LS)]) for i in range(2)])
    dgbuf = Rot([sb("dg%d" % i, [128, 31, 128], BF16) for i in range(2)])
    bufR, r_bufR = sb("bufR", [128, 16, T], BF16)
    r_b0, r_b1 = Res("bufR0"), Res("bufR1")
    fhist, r_fhist = sb("fhist", [128, 48, NSEQ * 2])
    r_fh = [Res("fh%d" % i) for i in range(48)]
    upext = Rot([sb("upext%d" % i, [128, 2 + T]) for i in range(4)])
    upc = Rot([sb("upc%d" % i, [128, T]) for i in range(6)])
    wslot = Rot([sb("w%d" % i, [128, 8, 256], BF16) for i in range(6)])
    wstage = Rot([sb("wst%d" % i, [128, 8, 256]) for i in range(2)])
    wao, r_wao = sb("wao", [64, 4, 512], BF16)
    kvrow = Rot([sb("kvrow%d" % i, [128, KVW], BF16) for i in range(2)])
    kvrow32 = Rot([sb("kvrow32_%d" % i, [128, 512]) for i in range(2)])
    kvt = Rot([sb("kvt%d" % i, [128, KVW], BF16) for i in range(6)])
    kT = Rot([sb("kT%d" % i, [128, 4, 128], BF16) for i in range(4)])
    p_sb = Rot([sb("p%d" % i, [128, 4, 128], BF16) for i in range(6)])

    def ps(name, shape):
        return nc.alloc_psum_tensor("ps_" + name, list(shape), F32), Res(name)

    mainps = Rot([ps("mps%d" % i, [128, 512]) for i in range(4)])
    sps = Rot([ps("sps%d" % i, [128, 4, 128]) for i in range(2)])
    pvps, r_pvps = ps("pvps", [128, 2, 4, 128])

    S.dma("sp", vecs[:], vecs_d, writes=[r_vecs])
    S.dma("sp", ident32[:], ident_d, writes=[r_ident32])
    S.op("dve", lambda e: e.tensor_copy(out=ident[:], in_=ident32[:]), reads=[r_ident32], writes=[r_ident])
    S.dma("sp", mprev[:], mprev_d.rearrange("p (h q) -> p h q", h=4), writes=[r_mprev])
    S.dma("sp", mcur[:], mcur_d.rearrange("p (h q) -> p h q", h=4), writes=[r_mcur])
    S.op("dve", lambda e: e.memset(ones_t[:], 1.0), writes=[r_ones])
    S.op("dve", lambda e: e.memset(eps_t[:], EPS), writes=[r_eps])
    for rot in (kvt, kT, p_sb):
        for (tt_, rr_) in rot.items:
            S.op("pool", lambda e, tt_=tt_: e.memset(tt_[:], 0.0), writes=[rr_])
    S.dma("sp", valtm[:], validtm_d, writes=[r_valtm])
    cast_rr = [0]

    def cast_op(out_ap, in_ap, reads, writes):
        k = cast_rr[0] % 3
        cast_rr[0] += 1
        if k == 0:
            S.op("act", lambda e: e.activation(out=out_ap, in_=in_ap, func=AF.Copy), reads=reads, writes=writes)
        elif k == 1:
            S.op("pool", lambda e: e.tensor_copy(out=out_ap, in_=in_ap), reads=reads, writes=writes)
        else:
            S.op("dve", lambda e: e.tensor_copy(out=out_ap, in_=in_ap), reads=reads, writes=writes)

    for l in range(2):
        for i in range(8):
            dgt, rdg = dgbuf.next()
            for j in range(31):
                en = "dve" if j % 2 == 0 else "pool"
                S.op(en, lambda e, dgt=dgt, j=j, l=l, i=i: e.tensor_scalar(
                    out=dgt[:, j, :], in0=ident32[:, :], scalar1=vecs[:, V_CW + (l * 8 + i) * 31 + j:V_CW + (l * 8 + i) * 31 + j + 1],
                    scalar2=None, op0=ALU.mult), reads=[r_ident32, r_vecs], writes=[rdg])
            S.dma("pool", dgscr[l][i], dgt[:].rearrange("p j k -> p (j k)"), reads=[rdg], writes=[r_dg], merge=True)
    for l in range(2):
        for name in ("w_in", "w_conv_out", "w_attn_out", "w_out", "w_up", "w_down"):
            R_, C_ = WSH[name]
            src, rdst = wsrc[name][l], r_wb[name][l]
            for r0 in range(0, R_, 1024):
                kch = min(1024, R_ - r0) // 128
                for c0 in range(0, C_, 256):
                    st, rst = wstage.next()
                    S.dma("sp", st[:, 0:kch, :], src[r0:r0 + 128 * kch, c0:c0 + 256].rearrange(
                        "(c p) e -> p c e", p=128), writes=[rst])
                    wt, rw = wslot.next()
                    cast_op(wt[:, 0:kch, :], st[:, 0:kch, :], [rst], [rw])
                    if name == "w_attn_out":
                        S.dma("pool", wbf[name][l][r0:r0 + 128 * kch, c0:c0 + 256].rearrange(
                            "(c p) e -> p c e", p=128), wt[:, 0:kch, :], reads=[rw], writes=[rdst], merge=True)
                    else:
                        bidx = (r0 // 1024) * (C_ // 256) + c0 // 256
                        S.dma("pool", wbt[name][l][bidx].rearrange("p (c e) -> p c e", c=8), wt[:, 0:8, :],
                              reads=[rw], writes=[rdst], merge=True)
    for l in ([] if skip_init else range(2)):
        for g in range(3):
            W = GROUPS[g][0]
            for n in range(NSEQ):
                for r0 in range(0, W, 128):
                    st, rst = wstage.next()
                    stv = st[:, 0:2, :]
                    S.dma("sp", stv[:, 0, :], ck[g][l, n, r0:r0 + 128, :], writes=[rst])
                    S.dma("sp", stv[:, 1, :], cv[g][l, n, r0:r0 + 128, :], writes=[rst], merge=True)
                    kr, rkr = kvrow.next()
                    cast_op(kr[:, 0:512], stv.rearrange("p a b -> p (a b)"), [rst], [rkr])
                    S.op("pool", lambda e, kr=kr: e.memset(kr[:, 512:576], 1.0), writes=[rkr])
                    S.dma("pool", ext[l][g][n, r0:r0 + 128, :], kr[:, :], reads=[rkr], writes=[r_ext[l][g]],
                          merge=True)
                for r0 in range(LS, W, 512):
                    r1 = min(W, r0 + 512)
                    S.dma("pool", nks[g][l, n, r0 - LS:r1 - LS, :], ck[g][l, n, r0:r1, :], writes=[out_res],
                          merge=True)
                    S.dma("pool", nvs[g][l, n, r0 - LS:r1 - LS, :], cv[g][l, n, r0:r1, :], writes=[out_res],
                          merge=True)

    for (st, rst) in wstage.items:
        v16 = st[:, :, :].bitcast(BF16)
        for half in range(2):
            rr = Res("wx")
            rr.w = dict(rst.w)
            rr.r = dict(rst.r)
            wslot.items.append((v16[:, :, 256 * half:256 * half + 256], rr))

    def vcol(off, n=1):
        return vecs[:, off:off + n]

    def load_w(name, l, c0, r0=0, ncols=256, kch=8):
        wt, rw = wslot.next()
        bidx = (r0 // 1024) * (WSH[name][1] // 256) + c0 // 256
        S.dma("sp", wt[:, 0:8, 0:256], wbt[name][l][bidx].rearrange("p (c e) -> p c e", c=8),
              reads=[r_wb[name][l]], writes=[rw])
        return wt, rw

    def rms(l_gain_off, tn, src=x_sb, r_src=None):
        r_src = r_src or r_x
        pt, rp = mainps.next()
        for c in range(8):
            tt, rtt = (tmpA, r_tmpA) if c % 2 == 0 else (tmpB, r_tmpB)
            S.op("act", lambda e, c=c, tt=tt: e.activation(out=tt[:, 0:tn], in_=src[:, c, 0:tn], func=AF.Square),
                 reads=[r_src], writes=[rtt])
            S.op("pe", lambda e, c=c, tt=tt: e.matmul(pt[:, 0:tn], ones_t[:], tt[:, 0:tn], start=(c == 0),
                                                       stop=(c == 7)), reads=[rtt, r_ones], writes=[rp])
        S.op("act", lambda e: e.activation(out=rstd[:, 0:tn], in_=pt[:, 0:tn], func=AF.Sqrt, bias=eps_t[:, 0:1],
                                           scale=1.0 / D), reads=[rp, r_eps], writes=[r_rstd])
        S.op("dve", lambda e: e.reciprocal(out=rstd[:, 0:tn], in_=rstd[:, 0:tn]), reads=[r_rstd], writes=[r_rstd])
        for c in range(8):
            S.op("dve", lambda e, c=c: e.scalar_tensor_tensor(
                out=r_(h_sb[:, c, 0:tn]), in0=src[:, c, 0:tn], scalar=vcol(l_gain_off + c), in1=rstd[:, 0:tn],
                op0=ALU.mult, op1=ALU.mult), reads=[r_src, r_rstd, r_vecs], writes=[r_h])

    def proj_fm(wt, rw, wcol0, rhs_t, r_rhs, rhs_chunk0, tn, kch=8, pt=None, rp=None, start=True, stop=True,
                kparts=128):
        if pt is None:
            pt, rp = mainps.next()
        for c in range(kch):
            S.op("pe", lambda e, c=c: e.matmul(
                pt[:, 0:tn], r_(wt[0:kparts, c, wcol0:wcol0 + 128]), r_(rhs_t[0:kparts, rhs_chunk0 + c, 0:tn]),
                start=(start and c == 0), stop=(stop and c == kch - 1)),
                reads=[rw, r_rhs], writes=[rp], inc=(c == kch - 1))
        return pt, rp

    def attention_unit(l, g, keysrc, r_keys_of_row, qrow0, qcol0, i0, nq, r, d, sample, have_prev=True):
        tiles = []
        if have_prev:
            tiles.append((qrow0 + r + d * (i0 - 128), 128, mprev))
        tiles.append((qrow0 + r + d * i0, nq, mcur))
        qs = qcol0 + r + d * i0
        qsl = slice(qs, qs + d * (nq - 1) + 1, d)
        nqm = nq + (nq % 2)
        qslm = slice(qs, qs + d * (nqm - 1) + 1, d)
        ptiles = []
        for (row0, nk, mk) in tiles:
            kt, rkt = kvt.next()
            rows = keysrc[row0:row0 + d * (nk - 1) + 1:d, :]
            S.dma("act", kt[0:nk, :], rows, reads=r_keys_of_row(row0, row0 + d * (nk - 1)), writes=[rkt])
            if _CACHE.get("att") == "dma":
                continue
            tp, rtp = mainps.next()
            tp16 = tp[:, :].bitcast(BF16)
            for j in range(2):
                S.op("pe", lambda e, j=j: e.transpose(tp16[:, j * 128:j * 128 + 128], kt[:, j * 128:(j + 1) * 128],
                                                       ident[:, :]),
                     reads=[rkt, r_ident], writes=[rtp], inc=(j == 1))
            ktt, rktt = kT.next()
            tpv = tp16[:, 0:256].rearrange("p (j k) -> p j k", j=2)
            S.op("act", lambda e: e.activation(out=ktt[0:64, 0:4:2, :], in_=tpv[0:64, :, :], func=AF.Copy),
                 reads=[rtp], writes=[rktt])
            S.op("act", lambda e: e.activation(out=ktt[64:128, 1:4:2, :], in_=tpv[64:128, :, :], func=AF.Copy),
                 reads=[rtp], writes=[rktt])
            if _CACHE.get("att") == "tr":
                continue
            sp_, rsp = sps.next()
            for h in range(4):
                S.op("pe", lambda e, h=h: e.matmul(
                    sp_[:, h, 0:nqm], ktt[:, h, :], q_sb[:, 2 * g + h // 2, qslm],
                    start=True, stop=True), reads=[rktt, r_q], writes=[rsp], inc=(h == 3))
            if _CACHE.get("att") == "s":
                continue
            pp, rpp = p_sb.next()
            S.op("act", lambda e: e.activation(out=pp[0:nk, :, 0:nqm], in_=sp_[0:nk, :, 0:nqm], func=AF.Exp),
                 reads=[rsp], writes=[rpp])
            S.op("pool", lambda e, mk=mk: e.tensor_tensor(out=pp[:, :, 0:nqm], in0=pp[:, :, 0:nqm],
                                                           in1=mk[:, :, 0:nqm], op=ALU.mult),
                 reads=[rpp, r_mprev, r_mcur], writes=[rpp])
            ptiles.append((kt, rkt, pp, rpp, nk))
        def back():
            ntile = len(ptiles)
            if _CACHE.get("att") in ("dma", "tr", "s", "exp"):
                return
            for h in range(4):
                for part in range(2):
                    for ti, (kt, rkt, pp, rpp, nk) in enumerate(ptiles):
                        if part == 0:
                            lt = kt[:, 256 + 64 * h:256 + 64 * h + 128]
                        else:
                            lt = kt[:, 448:576]
                        S.op("pe", lambda e, h=h, lt=lt, pp=pp, ti=ti, part=part: e.matmul(
                            pvps[:, part, h, 0:nqm], lt, pp[:, h, 0:nqm],
                            start=(ti == 0), stop=(ti == ntile - 1)), reads=[rkt, rpp, r_ones], writes=[r_pvps],
                            inc=(h == 3 and part == 1 and ti == ntile - 1))
            if _CACHE.get("att") == "pv":
                return
            if g == 0:
                S.op("dve", lambda e: e.tensor_copy(out=acc[0:64, :, qsl], in_=pvps[0:64, 0, :, 0:nq]),
                     reads=[r_pvps], writes=[r_acc])
                S.op("dve", lambda e: e.tensor_copy(out=acc[64:128, :, qsl], in_=pvps[64:128, 1, :, 0:nq]),
                     reads=[r_pvps], writes=[r_acc])
            else:
                S.op("dve", lambda e: e.tensor_tensor(out=acc[0:64, :, qsl], in0=pvps[0:64, 0, :, 0:nq],
                                                      in1=acc[0:64, :, qsl], op=ALU.add),
                     reads=[r_pvps, r_acc], writes=[r_acc])
                S.op("dve", lambda e: e.tensor_tensor(out=acc[64:128, :, qsl], in0=pvps[64:128, 1, :, 0:nq],
                                                      in1=acc[64:128, :, qsl], op=ALU.add),
                     reads=[r_pvps, r_acc], writes=[r_acc])
        return back

    def run_tile(l, ti, kind):
        sample = kind == "sample"
        warm = kind == "warm"
        tn = TS if sample else (128 if warm else T)
        nseq, ls = (NSEQ, LS) if sample else (1, tn)
        tok0 = ti * T + (T - 128 if warm else 0)
        last_layer = l == 1
        Wl = w_in[l]
        if sample:
            if l == 0:
                S.dma("pool", x_sb[:, :, 0:tn], xsT.rearrange("(c p) t -> p c t", p=128), writes=[r_x])
            else:
                S.dma("pool", x_sb[:, :, 0:tn], xs1scr.rearrange("(c p) t -> p c t", p=128),
                      reads=[r_xs1], writes=[r_x])
        else:
            src = xT if l == 0 else x1scr
            rd = [] if l == 0 else [r_x1[ti]]
            S.dma("pool", x_sb[:, :, 0:tn], src[:, tok0:tok0 + tn].rearrange("(c p) t -> p c t", p=128),
                  reads=rd, writes=[r_x])
            if kind != "kv":
                S.dma("pool", vmask[:, 0:tn], validT[:, tok0:tok0 + tn], writes=[r_vmask])
        if _CACHE.get("stop") == "A":
            return
        rms(V_GATT + 8 * l, tn)
        if _CACHE.get("stop") == "B":
            return
        nblk = (tn + 127) // 128
        for g in ([] if warm else range(3)):
            Wg = GROUPS[g][0]
            wk, rwk = load_w("w_in", l, KOFF + 256 * g)
            wv, rwv = load_w("w_in", l, VOFF + 256 * g)
            for b in range(nblk):
                nb = min(128, tn - 128 * b)
                pt, rp = mainps.next()
                for (wt, rw, co) in ((wk, rwk, 0), (wv, rwv, 256)):
                    for c in range(8):
                        S.op("pe", lambda e, c=c, wt=wt, co=co: e.matmul(
                            pt[:, co:co + 256], h_sb[:, c, 128 * b:128 * b + 128], wt[:, c, 0:256],
                            start=(c == 0), stop=(c == 7)), reads=[r_h, rw], writes=[rp], inc=(c == 7))
                kr, rkr = kvrow.next()
                S.op("act", lambda e, kr=kr, pt=pt: e.activation(out=kr[0:nb, 0:512], in_=pt[0:nb, :], func=AF.Copy),
                     reads=[rp], writes=[rkr])
                orow = tok0 + 128 * b - (NS - Wg)
                need32 = sample or (ti >= 10 and orow >= 0)
                if _CACHE.get("cstop") == "mm":
                    continue
                if need32:
                    k32, rk32 = kvrow32.next()
                    S.op("act", lambda e, k32=k32, pt=pt: e.activation(out=k32[:, :], in_=pt[:, :], func=AF.Copy),
                         reads=[rp], writes=[rk32])
                if _CACHE.get("cstop") == "k32":
                    continue
                if sample:
                    S.op("pool", lambda e, kr=kr: e.memset(kr[0:nb, 512:576], 1.0), writes=[rkr])
                    for n in range(NSEQ if _CACHE.get("sdma") != "0" else 0):
                        S.dma("pool", ext[l][g][n, Wg:Wg + LS, :], kr[LS * n:LS * n + LS, :],
                              reads=[rkr], writes=[r_ext[l][g]], merge=True)
                        S.dma("pool", nks[g][l, n, Wg - LS:Wg, :], k32[LS * n:LS * n + LS, 0:256],
                              reads=[rk32], writes=[out_res], merge=True)
                        S.dma("pool", nvs[g][l, n, Wg - LS:Wg, :], k32[LS * n:LS * n + LS, 256:512],
                              reads=[rk32], writes=[out_res], merge=True)
                else:
                    blk = ti * 4 + b
                    S.op("pool", lambda e, kr=kr, blk=blk: e.tensor_scalar(
                        out=kr[0:nb, 512:576], in0=ones_t[0:nb, 0:64], scalar1=valtm[0:nb, blk:blk + 1],
                        scalar2=None, op0=ALU.mult), reads=[r_ones, r_valtm], writes=[rkr])
                    S.dma("pool", kvscr[l][g][tok0 + 128 * b:tok0 + 128 * b + nb, :], kr[0:nb, :],
                          reads=[rkr], writes=[r_kv[l][g][ti]], merge=True)
                    if need32:
                        S.dma("pool", nkp[g][l, orow:orow + nb, :], k32[0:nb, 0:256], reads=[rk32],
                              writes=[out_res], merge=True)
                        S.dma("pool", nvp[g][l, orow:orow + nb, :], k32[0:nb, 256:512], reads=[rk32],
                              writes=[out_res], merge=True)
        if kind == "kv":
            return
        if _CACHE.get("stop") == "C" and kind != "kv":
            return
        for e2 in range(3):
            wq, rwq = load_w("w_in", l, QOFF + 256 * e2)
            for k in range(2):
                e6 = 2 * e2 + k
                pt, rp = proj_fm(wq, rwq, 128 * k, h_sb, r_h, 0, tn)
                S.op("act", lambda e, e6=e6, pt=pt: e.activation(out=q_sb[:, e6, 0:tn], in_=pt[:, 0:tn],
                                                                 func=AF.Copy, scale=0.125), reads=[rp], writes=[r_q])
        if _CACHE.get("stop") == "D" and kind != "kv":
            return
        units = []
        for g in range(int(_CACHE.get("ng", 3))):
            Wg, d = GROUPS[g]
            for n in range(nseq):
                if sample:
                    keysrc = ext[l][g][n]
                    rk = lambda a_, b_, l=l, g=g: [r_ext[l][g]]
                    qrow0, qcol0 = Wg, LS * n
                else:
                    keysrc = kvscr[l][g]
                    rk = lambda a_, b_, l=l, g=g: r_kv[l][g][a_ // T:b_ // T + 1]
                    qrow0, qcol0 = tok0, 0
                for r in range(min(d, ls)):
                    nsub = (ls - r + d - 1) // d
                    for i0 in range(0, nsub, 128):
                        nq = min(128, nsub - i0)
                        units.append((g, keysrc, rk, qrow0, qcol0, i0, nq, r, d))
        uh = uhist[:, :, 0:nseq * 30].rearrange("p c (n t) -> p c n t", n=nseq)
        if sample:
            S.dma("pool", uh, sconv[l].rearrange("(c p) n t -> p c n t", p=128), writes=[r_uhist])
        elif warm:
            S.op("pool", lambda e: e.memset(uhist[:], 0.0), writes=[r_uhist])
        glu_w = {}

        def conv_chunk(i):
            i2, k = i // 2, i % 2
            if k == 0:
                glu_w["a"] = load_w("w_in", l, 256 * i2)
                glu_w["b"] = load_w("w_in", l, 1024 + 256 * i2)
            wa, rwa = glu_w["a"]
            wb, rwb = glu_w["b"]
            pa, rpa = proj_fm(wa, rwa, 128 * k, h_sb, r_h, 0, tn)
            pb, rpb = proj_fm(wb, rwb, 128 * k, h_sb, r_h, 0, tn)
            cdst = cbuf[:, i, 0:tn].rearrange("p (n t) -> p n t", n=nseq)
            cw = V_CW + (l * 8 + i) * 31
            S.op("act", lambda e, pb=pb: e.activation(out=tmpA[:, 0:tn], in_=pb[:, 0:tn], func=AF.Sigmoid),
                 reads=[rpb], writes=[r_tmpA])
            if sample:
                uwt, ruw = uws.next()
                ue = uwt[:, 0:nseq * (30 + ls)].rearrange("p (n t) -> p n t", n=nseq)
                S.op("pool", lambda e, ue=ue, i=i: e.tensor_copy(out=ue[:, :, 0:30], in_=uh[:, i, :, :]),
                     reads=[r_uhist], writes=[ruw])
                S.op("dve", lambda e, pa=pa, ue=ue: e.tensor_tensor(
                    out=ue[:, :, 30:30 + ls], in0=pa[:, 0:tn].rearrange("p (n t) -> p n t", n=nseq),
                    in1=tmpA[:, 0:tn].rearrange("p (n t) -> p n t", n=nseq), op=ALU.mult),
                    reads=[rpa, r_tmpA], writes=[ruw])
                S.op("pool", lambda e, ue=ue, i=i: e.tensor_copy(out=uh[:, i, :, :], in_=ue[:, :, ls:ls + 30]),
                     reads=[ruw], writes=[r_uhist])
                S.op("dve", lambda e, ue=ue, cdst=cdst, cw=cw, i=i: e.tensor_scalar(
                    out=cdst, in0=ue[:, :, 0:ls], scalar1=vcol(cw), scalar2=vcol(V_CB + 8 * l + i),
                    op0=ALU.mult, op1=ALU.add), reads=[ruw, r_vecs], writes=[r_cbuf])
                for j in range(1, 31):
                    S.op("dve", lambda e, ue=ue, j=j, cdst=cdst, cw=cw: e.scalar_tensor_tensor(
                        out=cdst, in0=ue[:, :, j:j + ls], scalar=vcol(cw + j), in1=cdst,
                        op0=ALU.mult, op1=ALU.add), reads=[ruw, r_cbuf, r_vecs], writes=[r_cbuf])
            else:
                uwt, ruw = uwork.next()
                S.op("pool", lambda e, uwt=uwt, i=i: e.tensor_copy(out=uwt[:, 0:30], in_=uhist[:, i, 0:30]),
                     reads=[r_uhist], writes=[ruw])
                S.op("dve", lambda e, pa=pa, uwt=uwt: e.tensor_tensor(out=uwt[:, 30:30 + tn], in0=pa[:, 0:tn],
                                                                      in1=tmpA[:, 0:tn], op=ALU.mult),
                     reads=[rpa, r_tmpA], writes=[ruw])
                S.op("dve", lambda e, pa=pa, i=i: e.tensor_tensor(out=uhist[:, i, 0:30], in0=pa[:, tn - 30:tn],
                                                                   in1=tmpA[:, tn - 30:tn], op=ALU.mult),
                     reads=[rpa, r_tmpA, ruw], writes=[r_uhist])
                dgt, rdg = dgbuf.next()
                S.dma("sp", dgt[:].rearrange("p j k -> p (j k)"), dgscr[l][i], reads=[r_dg], writes=[rdg])
                pc, rpc = mainps.next()
                for j in range(31):
                    S.op("pe", lambda e, j=j, dgt=dgt, uwt=uwt, pc=pc: e.matmul(
                        pc[:, 0:tn], dgt[:, j, :], uwt[:, j:j + tn], start=(j == 0), stop=(j == 30)),
                        reads=[rdg, ruw], writes=[rpc], inc=(j == 30))
                S.op("act", lambda e, pc=pc, i=i: e.activation(out=cbuf[:, i, 0:tn], in_=pc[:, 0:tn],
                                                               func=AF.Identity, bias=vcol(V_CB + 8 * l + i),
                                                               scale=1.0), reads=[rpc, r_vecs], writes=[r_cbuf])
        stride = max(1, (len(units) + 7) // 8)
        next_chunk = 0
        pend_back = None
        for ui, (g, keysrc, rk, qrow0, qcol0, i0, nq, r, d) in enumerate(units):
            bk = attention_unit(l, g, keysrc, rk, qrow0, qcol0, i0, nq, r, d, sample)
            if pend_back is not None:
                pend_back()
            pend_back = bk
            if ui % stride == stride - 1 and next_chunk < 8:
                conv_chunk(next_chunk)
                next_chunk += 1
        if pend_back is not None:
            pend_back()
        while next_chunk < 8:
            conv_chunk(next_chunk)
            next_chunk += 1
        S.op("dve", lambda e: e.tensor_scalar(out=acc[64:128, :, 0:tn], in0=acc[64:128, :, 0:tn], scalar1=1e-30,
                                              scalar2=None, op0=ALU.max), reads=[r_acc], writes=[r_acc])
        for h, (tt_, rt_) in enumerate(((tmpA, r_tmpA), (tmpB, r_tmpB), (rstd, r_rstd), (mean, r_mean))):
            S.op("dve", lambda e, h=h, tt_=tt_: e.reciprocal(out=tt_[0:64, 0:tn], in_=acc[64:128, h, 0:tn]),
                 reads=[r_acc], writes=[rt_])
            S.op("dve", lambda e, h=h, tt_=tt_: e.tensor_tensor(out=attn_sb[:, h, 0:tn], in0=acc[0:64, h, 0:tn],
                                                                in1=tt_[0:64, 0:tn], op=ALU.mult),
                 reads=[r_acc, rt_], writes=[r_attn])
        if sample:
            S.dma("pool", ncs[l].rearrange("(c p) n t -> p c n t", p=128), uh, reads=[r_uhist], writes=[out_res], merge=True)
        elif ti == NT - 1:
            S.dma("pool", ncp[l].rearrange("(c p) t -> p c t", p=128), uhist[:, :, 0:30],
                  reads=[r_uhist], writes=[out_res], merge=True)
        psum_sum, r_psum = mainps.next()
        for i in range(8):
            S.op("pe", lambda e, i=i: e.matmul(psum_sum[:, 0:tn], ones_t[:], cbuf[:, i, 0:tn],
                                                start=(i == 0), stop=(i == 7)),
                 reads=[r_cbuf, r_ones], writes=[r_psum], inc=(i == 7))
        psq, r_psq = mainps.next()
        for i in range(8):
            tt, rtt = (tmpA, r_tmpA) if i % 2 == 0 else (tmpB, r_tmpB)
            S.op("act", lambda e, i=i, tt=tt: e.activation(out=tt[:, 0:tn], in_=cbuf[:, i, 0:tn], func=AF.Square),
                 reads=[r_cbuf], writes=[rtt])
            S.op("pe", lambda e, i=i, tt=tt: e.matmul(psq[:, 0:tn], ones_t[:], tt[:, 0:tn],
                                                       start=(i == 0), stop=(i == 7)),
                 reads=[rtt, r_ones], writes=[r_psq])
        S.op("act", lambda e: e.activation(out=mean[:, 0:tn], in_=psum_sum[:, 0:tn], func=AF.Copy, scale=1.0 / D),
             reads=[r_psum], writes=[r_mean])
        S.op("dve", lambda e: e.tensor_tensor(out=tmpA[:, 0:tn], in0=mean[:, 0:tn], in1=mean[:, 0:tn], op=ALU.mult),
             reads=[r_mean], writes=[r_tmpA])
        S.op("dve", lambda e: e.scalar_tensor_tensor(out=tmpA[:, 0:tn], in0=psq[:, 0:tn], scalar=1.0 / D,
                                                     in1=tmpA[:, 0:tn], op0=ALU.mult, op1=ALU.subtract),
             reads=[r_psq, r_tmpA], writes=[r_tmpA])
        S.op("dve", lambda e: e.tensor_scalar(out=tmpA[:, 0:tn], in0=tmpA[:, 0:tn], scalar1=0.0, scalar2=None,
                                              op0=ALU.max), reads=[r_tmpA], writes=[r_tmpA])
        S.op("act", lambda e: e.activation(out=tmpA[:, 0:tn], in_=tmpA[:, 0:tn], func=AF.Sqrt, bias=eps_t[:, 0:1],
                                           scale=1.0), reads=[r_tmpA, r_eps], writes=[r_tmpA])
        S.op("dve", lambda e: e.reciprocal(out=tmpA[:, 0:tn], in_=tmpA[:, 0:tn]), reads=[r_tmpA], writes=[r_tmpA])
        for i in range(8):
            S.op("dve", lambda e, i=i: e.tensor_tensor(out=cbuf[:, i, 0:tn], in0=cbuf[:, i, 0:tn],
                                                       in1=mean[:, 0:tn], op=ALU.subtract),
                 reads=[r_cbuf, r_mean], writes=[r_cbuf])
            S.op("pool", lambda e, i=i: e.tensor_tensor(out=cbuf[:, i, 0:tn], in0=cbuf[:, i, 0:tn],
                                                        in1=tmpA[:, 0:tn], op=ALU.mult),
                 reads=[r_cbuf, r_tmpA], writes=[r_cbuf])
            S.op("act", lambda e, i=i: e.activation(out=r_(bufR[:, i, 0:tn]), in_=cbuf[:, i, 0:tn], func=AF.Silu,
                                                    bias=vcol(V_LNB + 8 * l + i), scale=vcol(V_LNG + 8 * l + i)),
                 reads=[r_cbuf, r_vecs], writes=[r_b0])
        if _CACHE.get("stop") == "G" and kind != "kv":
            return
        for j2 in range(4):
            if j2 % 2 == 0:
                S.dma("sp", wao[:], wbf["w_attn_out"][l][:, 256 * j2:256 * j2 + 512].rearrange(
                    "(h p) e -> p h e", p=64), reads=[r_wb["w_attn_out"][l]], writes=[r_wao])
            wgc, rwgc = load_w("w_in", l, GCOFF + 256 * j2)
            wga, rwga = load_w("w_in", l, GAOFF + 256 * j2)
            wco, rwco = load_w("w_conv_out", l, 256 * j2)
            for k in range(2):
                j = 2 * j2 + k
                pgc, rpgc = proj_fm(wgc, rwgc, 128 * k, h_sb, r_h, 0, tn)
                S.op("act", lambda e, pgc=pgc: e.activation(out=tmpA[:, 0:tn], in_=pgc[:, 0:tn], func=AF.Sigmoid),
                     reads=[rpgc], writes=[r_tmpA])
                pga, rpga = proj_fm(wga, rwga, 128 * k, h_sb, r_h, 0, tn)
                S.op("act", lambda e, pga=pga: e.activation(out=tmpB[:, 0:tn], in_=pga[:, 0:tn], func=AF.Sigmoid),
                     reads=[rpga], writes=[r_tmpB])
                pbc, rpbc = proj_fm(wco, rwco, 128 * k, bufR, r_b0, 0, tn)
                S.op("dve", lambda e, pbc=pbc: e.tensor_tensor(out=tmpA[:, 0:tn], in0=pbc[:, 0:tn], in1=tmpA[:, 0:tn],
                                                               op=ALU.mult), reads=[rpbc, r_tmpA], writes=[r_tmpA])
                pba, rpba = mainps.next()
                wc0 = 128 * (j % 4)
                for h in range(4):
                    S.op("pe", lambda e, h=h, wc0=wc0, pba=pba: e.matmul(
                        pba[:, 0:tn], r_(wao[:, h, wc0:wc0 + 128]), r_(attn_sb[:, h, 0:tn]),
                        start=(h == 0), stop=(h == 3)), reads=[r_wao, r_attn], writes=[rpba], inc=(h == 3))
                S.op("dve", lambda e, pba=pba: e.tensor_tensor(out=tmpB[:, 0:tn], in0=pba[:, 0:tn], in1=tmpB[:, 0:tn],
                                                               op=ALU.mult), reads=[rpba, r_tmpB], writes=[r_tmpB])
                S.op("pool", lambda e, j=j: e.tensor_tensor(out=r_(bufR[:, 8 + j, 0:tn]), in0=tmpA[:, 0:tn],
                                                            in1=tmpB[:, 0:tn], op=ALU.add),
                     reads=[r_tmpA, r_tmpB], writes=[r_b1])
        if _CACHE.get("stop") == "H" and kind != "kv":
            return
        for j2 in range(4):
            wo, rwo = load_w("w_out", l, 256 * j2)
            for k in range(2):
                j = 2 * j2 + k
                po, rpo = proj_fm(wo, rwo, 128 * k, bufR, r_b1, 8, tn)
                S.op("dve", lambda e, j=j, po=po: e.tensor_tensor(out=x_sb[:, j, 0:tn], in0=po[:, 0:tn],
                                                                  in1=x_sb[:, j, 0:tn], op=ALU.add),
                     reads=[rpo, r_x], writes=[r_x])
                if not sample:
                    S.op("pool", lambda e, j=j: e.tensor_tensor(out=x_sb[:, j, 0:tn], in0=x_sb[:, j, 0:tn],
                                                                in1=vmask[:, 0:tn], op=ALU.mult),
                         reads=[r_x, r_vmask], writes=[r_x])
        if _CACHE.get("stop") == "I" and kind != "kv":
            return
        rms(V_GFFN + 8 * l, tn)
        fh = fhist[:, :, 0:nseq * 2].rearrange("p c (n t) -> p c n t", n=nseq)
        if sample:
            S.dma("pool", fh, sffn[l].rearrange("(c p) n t -> p c n t", p=128), writes=r_fh)
        elif warm:
            S.op("pool", lambda e: e.memset(fhist[:], 0.0), writes=r_fh)
        Wu = w_up[l]
        Wd = w_down[l]
        for ip in range(3):
            for i2 in range(4):
                wv_, rwv_ = load_w("w_up", l, 1024 * ip + 256 * i2)
                wg_, rwg_ = load_w("w_up", l, DFF + 1024 * ip + 256 * i2)
                for k in range(2):
                    il = 2 * i2 + k
                    i = 8 * ip + il
                    res_c = []
                    for (wt, rw, idx) in ((wv_, rwv_, i), (wg_, rwg_, 24 + i)):
                        pu, rpu = proj_fm(wt, rw, 128 * k, h_sb, r_h, 0, tn)
                        uxt, ruxt = upext.next()
                        ux = uxt[:, 0:nseq * (2 + ls)].rearrange("p (n t) -> p n t", n=nseq)
                        S.op("pool", lambda e, ux=ux, idx=idx: e.tensor_copy(out=ux[:, :, 0:2], in_=fh[:, idx, :, :]),
                             reads=[r_fh[idx]], writes=[ruxt])
                        S.op("act", lambda e, ux=ux, pu=pu: e.activation(
                            out=ux[:, :, 2:2 + ls], in_=pu[:, 0:tn].rearrange("p (n t) -> p n t", n=nseq),
                            func=AF.Copy), reads=[rpu], writes=[ruxt])
                        S.op("pool", lambda e, ux=ux, idx=idx: e.tensor_copy(out=fh[:, idx, :, :],
                                                                             in_=ux[:, :, ls:ls + 2]),
                             reads=[ruxt], writes=[r_fh[idx]])
                        uct, ruc = upc.next()
                        uc = uct[:, 0:tn].rearrange("p (n t) -> p n t", n=nseq)
                        fw = V_FW + (l * 48 + idx) * 3
                        S.op("act", lambda e, ux=ux, uc=uc, fw=fw, idx=idx: e.activation(
                            out=uc, in_=ux[:, :, 0:ls], func=AF.Identity, scale=vcol(fw),
                            bias=vcol(V_FB + 48 * l + idx)), reads=[ruxt, r_vecs], writes=[ruc])
                        for j in (1, 2):
                            S.op("dve", lambda e, ux=ux, uc=uc, fw=fw, j=j: e.scalar_tensor_tensor(
                                out=uc, in0=ux[:, :, j:j + ls], scalar=vcol(fw + j), in1=uc,
                                op0=ALU.mult, op1=ALU.add), reads=[ruxt, ruc, r_vecs], writes=[ruc])
                        res_c.append((uct, ruc))
                    (vt, rv), (gt, rg) = res_c
                    S.op("act", lambda e, gt=gt: e.activation(out=gt[:, 0:tn], in_=gt[:, 0:tn], func=AF.Silu),
                         reads=[rg], writes=[rg])
                    S.op("dve", lambda e, vt=vt, gt=gt, il=il: e.tensor_tensor(
                        out=r_(bufR[:, il, 0:tn]), in0=vt[:, 0:tn], in1=gt[:, 0:tn], op=ALU.mult),
                        reads=[rv, rg], writes=[r_b0])
            for j2 in range(4):
                wd_, rwd_ = load_w("w_down", l, 256 * j2, r0=1024 * ip)
                for k in range(2):
                    j = 2 * j2 + k
                    pd, rpd = proj_fm(wd_, rwd_, 128 * k, bufR, r_b0, 0, tn)
                    S.op("dve", lambda e, j=j, pd=pd: e.tensor_tensor(out=x_sb[:, j, 0:tn], in0=pd[:, 0:tn],
                                                                      in1=x_sb[:, j, 0:tn], op=ALU.add),
                         reads=[rpd, r_x], writes=[r_x])
                    if ip == 2 and not sample:
                        S.op("pool", lambda e, j=j: e.tensor_tensor(out=x_sb[:, j, 0:tn], in0=x_sb[:, j, 0:tn],
                                                                    in1=vmask[:, 0:tn], op=ALU.mult),
                             reads=[r_x, r_vmask], writes=[r_x])
        if sample:
            S.dma("pool", nfs[l].rearrange("(c p) n t -> p c n t", p=128), fh, reads=r_fh, writes=[out_res], merge=True)
        elif ti == NT - 1:
            S.dma("pool", nfp[l].rearrange("(c p) t -> p c t", p=128), fhist[:, :, 0:2],
                  reads=r_fh, writes=[out_res], merge=True)
        if _CACHE.get("stop") == "J" and kind != "kv":
            return
        if warm:
            return
        if not last_layer:
            if sample:
                S.dma("pool", xs1scr.rearrange("(c p) t -> p c t", p=128), x_sb[:, :, 0:tn],
                      reads=[r_x], writes=[r_xs1])
            else:
                S.dma("pool", x1scr[:, tok0:tok0 + T].rearrange("(c p) t -> p c t", p=128), x_sb[:, :, :],
                      reads=[r_x], writes=[r_x1[ti]])
        else:
            if sample or tok0 >= OWN0:
                rms(V_GFIN, tn)
                if sample:
                    S.dma("pool", ysT.rearrange("(c p) t -> p c t", p=128), h_sb[:, :, 0:tn],
                          reads=[r_h], writes=[out_res], merge=True)
                else:
                    S.dma("pool", yT[:, tok0 - OWN0:tok0 - OWN0 + T].rearrange("(c p) t -> p c t", p=128),
                          h_sb[:, :, :], reads=[r_h], writes=[out_res], merge=True)

    xs1scr = dscr("xs1scr", [D, TS])
    r_xs1 = Res("xs1")

    ntile_done = 0
    for l in range(2):
        first_full = 4 if l == 0 else 9
        first_kv = 0 if l == 0 else 5
        sched = ([(ti, "kv") for ti in range(first_kv, first_full + 1)] + [(first_full, "warm")]
                 + [(ti, "full") for ti in range(first_full + 1, NT)] + [(0, "sample")])
        for (ti, kind) in sched:
            if max_tiles is not None and ntile_done >= max_tiles:
                continue
            if _CACHE.get("only") == "sample" and kind != "sample":
                ntile_done += 1
                continue
            run_tile(l, ti, kind)
            ntile_done += 1
            if dbg_tile == (l, ti, kind):
                tn_ = TS if kind == "sample" else T
                S.dma("pool", dbgx[:, 0:tn_].rearrange("(c p) t -> p c t", p=128), x_sb[:, :, 0:tn_],
                      reads=[r_x], writes=[out_res], merge=True)
    S.finish("pool", [out_res])
    S.finish("sp", [out_res])
    return nc, S


_CACHE = {}


def _consts():
    jj = np.arange(128)[:, None]
    ii = np.arange(128)[None, :]
    mprev = np.tile((jj >= ii).astype(np.float32), (1, 4))
    mcur = np.tile((jj <= ii).astype(np.float32), (1, 4))
    return mprev, mcur


def _pack_vecs(norm_attn_g, norm_ffn_g, norm_final_g, conv_dw_b, conv_ln_g, conv_ln_b, conv_dw_w, ffn_dw_b, ffn_dw_w):
    v = np.zeros((128, NV), np.float32)

    def pc(a):
        a = np.asarray(a, np.float32)
        c = a.shape[-1] // 128
        a = a.reshape(a.shape[:-1] + (c, 128))
        return np.moveaxis(a, -1, 0)

    v[:, V_GATT:V_GATT + 16] = pc(norm_attn_g).reshape(128, 16)
    v[:, V_GFFN:V_GFFN + 16] = pc(norm_ffn_g).reshape(128, 16)
    v[:, V_GFIN:V_GFIN + 8] = pc(norm_final_g).reshape(128, 8)
    v[:, V_CB:V_CB + 16] = pc(conv_dw_b).reshape(128, 16)
    v[:, V_LNG:V_LNG + 16] = pc(conv_ln_g).reshape(128, 16)
    v[:, V_LNB:V_LNB + 16] = pc(conv_ln_b).reshape(128, 16)
    cw = pc(conv_dw_w)
    v[:, V_CW:V_CW + 2 * 8 * 31] = np.transpose(cw, (0, 1, 3, 2)).reshape(128, -1)
    v[:, V_FB:V_FB + 96] = pc(ffn_dw_b).reshape(128, 96)
    fw = pc(ffn_dw_w)
    v[:, V_FW:V_FW + 2 * 48 * 3] = np.transpose(fw, (0, 1, 3, 2)).reshape(128, -1)
    return v


def kernel(x_prompt, x_sample, state_conv, cache_k_w128, cache_v_w128, cache_k_w512, cache_v_w512,
           cache_k_w2048, cache_v_w2048, state_ffn_conv, norm_attn_g, w_in, conv_dw_w, conv_dw_b,
           conv_ln_g, conv_ln_b, w_conv_out, w_attn_out, w_out, norm_ffn_g, w_up, ffn_dw_w, ffn_dw_b,
           w_down, norm_final_g):
    f = lambda a: np.ascontiguousarray(np.asarray(a, dtype=np.float32))
    if "nc" not in _CACHE:
        _CACHE["nc"] = build_program()
    nc, S = _CACHE["nc"]
    mprev, mcur = _consts()
    vecs = _pack_vecs(norm_attn_g, norm_ffn_g, norm_final_g, conv_dw_b, conv_ln_g, conv_ln_b, conv_dw_w,
                      ffn_dw_b, ffn_dw_w)
    xp = f(x_prompt)[0]
    xpT_pad = np.zeros((D, OWN0 + SEQ), np.float32)
    xpT_pad[:, OWN0:] = xp.T
    ck = [f(cache_k_w128), f(cache_k_w512), f(cache_k_w2048)]
    cv = [f(cache_v_w128), f(cache_v_w512), f(cache_v_w2048)]
    shared = dict(w_in=f(w_in), w_conv_out=f(w_conv_out), w_attn_out=f(w_attn_out), w_out=f(w_out), w_up=f(w_up),
                  w_down=f(w_down), vecs=vecs, ident=np.eye(128, dtype=np.float32), mprev=mprev, mcur=mcur,
                  )
    xs = f(x_sample)
    sc = f(state_conv)
    sf = f(state_ffn_conv)
    in_maps = []
    for c in range(NCORES):
        s = c * OWN
        m = dict(shared)
        m["xT"] = np.ascontiguousarray(xpT_pad[:, s:s + NS])
        val = (np.arange(NS) + s - OWN0 >= 0).astype(np.float32)
        m["validT"] = np.ascontiguousarray(np.broadcast_to(val[None, :], (128, NS)))
        m["validtm"] = np.ascontiguousarray(val.reshape(NS // 128, 128).T)
        n0 = c * NSEQ
        m["xsT"] = np.ascontiguousarray(xs[n0:n0 + NSEQ].reshape(TS, D).T)
        m["sconv"] = np.ascontiguousarray(np.transpose(sc[:, n0:n0 + NSEQ], (0, 3, 1, 2)))
        m["sffn"] = np.ascontiguousarray(np.transpose(sf[:, n0:n0 + NSEQ], (0, 3, 1, 2)))
        for g in range(3):
            W = GROUPS[g][0]
            m["ck%d" % g] = np.ascontiguousarray(ck[g][:, n0:n0 + NSEQ].reshape(2, NSEQ, W, 256))
            m["cv%d" % g] = np.ascontiguousarray(cv[g][:, n0:n0 + NSEQ].reshape(2, NSEQ, W, 256))
        in_maps.append(m)
    if _CACHE.get("dbg_in_maps_only"):
        return in_maps
    res = run_bass_kernel_spmd(nc, in_maps, core_ids=list(range(NCORES)))
    R = res.results
    y_prompt = np.concatenate([R[c]["yT"].T for c in range(NCORES)], axis=0)[None]
    y_sample = np.concatenate([R[c]["ysT"].T.reshape(NSEQ, LS, D) for c in range(NCORES)], axis=0)
    last = R[NCORES - 1]
    conv_p = np.transpose(last["ncp"], (0, 2, 1))[:, None]
    conv_s = np.concatenate([np.transpose(R[c]["ncs"], (0, 2, 3, 1)) for c in range(NCORES)], axis=1)
    outs = [np.ascontiguousarray(y_prompt), np.ascontiguousarray(y_sample),
            np.ascontiguousarray(conv_p), np.ascontiguousarray(conv_s)]
    for g in range(3):
        W = GROUPS[g][0]
        outs.append(np.ascontiguousarray(last["nkp%d" % g].reshape(2, 1, W, 4, 64)))
        outs.append(np.ascontiguousarray(last["nvp%d" % g].reshape(2, 1, W, 4, 64)))
        outs.append(np.concatenate([R[c]["nks%d" % g].reshape(2, NSEQ, W, 4, 64) for c in range(NCORES)], axis=1))
        outs.append(np.concatenate([R[c]["nvs%d" % g].reshape(2, NSEQ, W, 4, 64) for c in range(NCORES)], axis=1))
    ffn_p = np.transpose(last["nfp"], (0, 2, 1))[:, None]
    ffn_s = np.concatenate([np.transpose(R[c]["nfs"], (0, 2, 3, 1)) for c in range(NCORES)], axis=1)
    outs.append(np.ascontiguousarray(ffn_p))
    outs.append(np.ascontiguousarray(ffn_s))
    return tuple(np.asarray(o, dtype=np.float32) for o in outs)
```

```python
import numpy as np
import concourse.bass as bass
import concourse.mybir as mybir
from concourse.bass_utils import run_bass_kernel_spmd

F32 = mybir.dt.float32
F32R = mybir.dt.float32r
BF16 = mybir.dt.bfloat16
AF = mybir.ActivationFunctionType
ALU = mybir.AluOpType

NCORES = 8
D = 1024
DIN = 6400
DFF = 3072
SEQ = 16384
OWN = SEQ // NCORES
T = 512
NT = 14
NS = NT * T
OWN0 = NS - OWN
NSEQ = 4
LS = 8
TS = NSEQ * LS
GROUPS = ((128, 1), (512, 4), (2048, 16))
EPS = 1e-6
KVW = 576
QOFF, KOFF, VOFF, GCOFF, GAOFF = 2048, 2816, 3584, 4352, 5376

V_GATT = 0
V_GFFN = 16
V_GFIN = 32
V_CB = 40
V_LNG = 56
V_LNB = 72
V_CW = 88
V_FB = V_CW + 2 * 8 * 31
V_FW = V_FB + 96
NV = V_FW + 2 * 48 * 3


class Res:
    __slots__ = ("name", "w", "r")

    def __init__(self, name):
        self.name = name
        self.w = {}
        self.r = {}


class Sched:
    def __init__(self, nc, n_dma_sems=48):
        self.nc = nc
        self.eng = {}
        self.sems = {}
        for name, h in (("pe", nc.tensor), ("act", nc.scalar), ("dve", nc.vector),
                        ("pool", nc.gpsimd), ("sp", nc.sync)):
            self.eng[name] = dict(h=h, sem=nc.alloc_semaphore("s_" + name), cnt=0, seen={})
            self.sems[name] = self.eng[name]["sem"]
        self.dma_sems = []
        for i in range(n_dma_sems):
            k = "d%d" % i
            self.sems[k] = nc.alloc_semaphore("s_" + k)
            self.dma_sems.append(dict(key=k, val=0))
        self.dma_rr = 0
        self.n_inst = 0
        self.n_wait = 0

    def _need(self, reads, writes, skip_key=None):
        need = {}
        for r in reads:
            for k, v in r.w.items():
                if need.get(k, 0) < v:
                    need[k] = v
        for r in writes:
            for k, v in r.w.items():
                if need.get(k, 0) < v:
                    need[k] = v
            for k, v in r.r.items():
                if need.get(k, 0) < v:
                    need[k] = v
        if skip_key is not None:
            need.pop(skip_key, None)
        return need

    def _waits(self, ename, need):
        e = self.eng[ename]
        for k, v in need.items():
            if e["seen"].get(k, 0) >= v:
                continue
            e["h"].wait_ge(self.sems[k], v)
            e["seen"][k] = v
            self.n_wait += 1

    def _commit(self, reads, writes, key, val):
        for r in writes:
            r.w = {key: val}
            r.r = {}
        for r in reads:
            if r.r.get(key, 0) < val:
                r.r[key] = val

    def op(self, ename, fn, reads=(), writes=(), inc=True):
        e = self.eng[ename]
        need = self._need(reads, writes, skip_key=("pe" if ename == "pe" else None))
        self._waits(ename, need)
        ins = fn(e["h"])
        if inc:
            e["cnt"] += 1
            ins.then_inc(e["sem"], 1)
            self._commit(reads, writes, ename, e["cnt"])
        else:
            self._commit(reads, writes, ename, e["cnt"] + 1)
        self.n_inst += 1
        return ins

    def dma(self, qname, out, in_, reads=(), writes=(), merge=False):
        e = self.eng[qname]
        d = self.dma_sems[self.dma_rr]
        self.dma_rr = (self.dma_rr + 1) % len(self.dma_sems)
        if merge:
            need = self._need(reads, ())
            for r in writes:
                for k, v in r.r.items():
                    if need.get(k, 0) < v:
                        need[k] = v
        else:
            need = self._need(reads, writes)
        if d["val"] > 0 and need.get(d["key"], 0) < d["val"]:
            need[d["key"]] = d["val"]
        self._waits(qname, need)
        ins = e["h"].dma_start(out=out, in_=in_)
        d["val"] += 16
        ins.then_inc(self.sems[d["key"]], 16)
        if merge:
            for r in writes:
                r.w[d["key"]] = d["val"]
            self._commit(reads, (), d["key"], d["val"])
        else:
            self._commit(reads, writes, d["key"], d["val"])
        self.n_inst += 1
        return ins

    def finish(self, ename, resources):
        need = self._need(resources, resources)
        self._waits(ename, need)


class Rot:
    def __init__(self, items):
        self.items = items
        self.i = 0

    def next(self):
        it = self.items[self.i]
        self.i = (self.i + 1) % len(self.items)
        return it


def r_(ap):
    return ap


def build_program(max_tiles=None, skip_init=False, skip_out=False, dbg_tile=None):
    nc = bass.Bass("TRN2", target_bir_lowering=False)
    nc.dge_precook = False
    S = Sched(nc)

    def din(name, shape):
        return nc.dram_tensor(name, list(shape), F32, kind="ExternalInput").ap()

    def dout(name, shape):
        return nc.dram_tensor(name, list(shape), F32, kind="ExternalOutput").ap()

    def dscr(name, shape, dt=F32):
        return nc.dram_tensor(name, list(shape), dt, kind="Internal").ap()

    xT = din("xT", [D, NS])
    validT = din("validT", [128, NS])
    validtm_d = din("validtm", [128, NS // 128])
    xsT = din("xsT", [D, TS])
    sconv = din("sconv", [2, D, NSEQ, 30])
    sffn = din("sffn", [2, 2 * DFF, NSEQ, 2])
    ck = [din("ck%d" % g, [2, NSEQ, GROUPS[g][0], 256]) for g in range(3)]
    cv = [din("cv%d" % g, [2, NSEQ, GROUPS[g][0], 256]) for g in range(3)]
    w_in = din("w_in", [2, D, DIN])
    w_conv_out = din("w_conv_out", [2, D, D])
    w_attn_out = din("w_attn_out", [2, 256, D])
    w_out = din("w_out", [2, D, D])
    w_up = din("w_up", [2, D, 2 * DFF])
    w_down = din("w_down", [2, DFF, D])
    vecs_d = din("vecs", [128, NV])
    ident_d = din("ident", [128, 128])
    mprev_d = din("mprev", [128, 4 * 128])
    mcur_d = din("mcur", [128, 4 * 128])

    yT = dout("yT", [D, OWN])
    ysT = dout("ysT", [D, TS])
    ncp = dout("ncp", [2, D, 30])
    ncs = dout("ncs", [2, D, NSEQ, 30])
    nkp = [dout("nkp%d" % g, [2, GROUPS[g][0], 256]) for g in range(3)]
    nvp = [dout("nvp%d" % g, [2, GROUPS[g][0], 256]) for g in range(3)]
    nks = [dout("nks%d" % g, [2, NSEQ, GROUPS[g][0], 256]) for g in range(3)]
    nvs = [dout("nvs%d" % g, [2, NSEQ, GROUPS[g][0], 256]) for g in range(3)]
    nfp = dout("nfp", [2, 2 * DFF, 2])
    nfs = dout("nfs", [2, 2 * DFF, NSEQ, 2])
    out_res = Res("outputs")
    if dbg_tile is not None:
        dbgx = dout("dbgx", [D, T])

    x1scr = dscr("x1scr", [D, NS])
    r_x1 = [Res("x1_%d" % i) for i in range(NT)]
    kvscr = [[dscr("kv_%d_%d" % (l, g), [NS, KVW], BF16) for g in range(3)] for l in range(2)]
    r_kv = [[[Res("kv%d%d_%d" % (l, g, i)) for i in range(NT)] for g in range(3)] for l in range(2)]
    ext = [[dscr("ext_%d_%d" % (l, g), [NSEQ, GROUPS[g][0] + LS, KVW], BF16) for g in range(3)] for l in range(2)]
    r_ext = [[Res("ext%d%d" % (l, g)) for g in range(3)] for l in range(2)]
    WSH = dict(w_in=(D, DIN), w_conv_out=(D, D), w_attn_out=(256, D), w_out=(D, D), w_up=(D, 2 * DFF), w_down=(DFF, D))
    wsrc = dict(w_in=w_in, w_conv_out=w_conv_out, w_attn_out=w_attn_out, w_out=w_out, w_up=w_up, w_down=w_down)
    wbf = {k: [dscr("wb_%s_%d" % (k, l), list(v), BF16) for l in range(2)] for k, v in WSH.items()
           if k == "w_attn_out"}
    wbt = {k: [dscr("wt_%s_%d" % (k, l), [(v[0] // 1024) * (v[1] // 256), 128, 8 * 256], BF16) for l in range(2)]
           for k, v in WSH.items() if k != "w_attn_out"}
    r_wb = {k: [Res("wb_%s_%d" % (k, l)) for l in range(2)] for k in WSH}
    dgscr = [[dscr("dg_%d_%d" % (l, i), [128, 31 * 128], BF16) for i in range(8)] for l in range(2)]
    r_dg = Res("dgscr")

    def sb(name, shape, dt=F32):
        return nc.alloc_sbuf_tensor("sb_" + name, list(shape), dt), Res(name)

    vecs, r_vecs = sb("vecs", [128, NV])
    ident32, r_ident32 = sb("ident32", [128, 128])
    ident, r_ident = sb("ident", [128, 128], BF16)
    mprev, r_mprev = sb("mprev", [128, 4, 128])
    mcur, r_mcur = sb("mcur", [128, 4, 128])
    ones_t, r_ones = sb("ones_t", [128, 128])
    valtm, r_valtm = sb("valtm", [128, NS // 128])
    eps_t, r_eps = sb("eps_t", [128, 1])
    x_sb, r_x = sb("x_sb", [128, 8, T])
    h_sb, r_h = sb("h_sb", [128, 8, T], BF16)
    vmask, r_vmask = sb("vmask", [128, T])
    rstd, r_rstd = sb("rstd", [128, T])
    mean, r_mean = sb("mean", [128, T])
    tmpA, r_tmpA = sb("tmpA", [128, T])
    tmpB, r_tmpB = sb("tmpB", [128, T])
    cbuf, r_cbuf = sb("cbuf", [128, 8, T])
    q_sb, r_q = sb("q_sb", [128, 6, T], BF16)
    acc, r_acc = sb("acc", [128, 4, T])
    attn_sb, r_attn = sb("attn_sb", [64, 4, T], BF16)
    uhist, r_uhist = sb("uhist", [128, 8, NSEQ * 30])
    uwork = Rot([sb("uwork%d" % i, [128, 30 + T], BF16) for i in range(3)])
    uws = Rot([sb("uws%d" % i, [128, NSEQ * (30 + LS)]) for i in range(2)])
    dgbuf = Rot([sb("dg%d" % i, [128, 31, 128], BF16) for i in range(2)])
    bufR, r_bufR = sb("bufR", [128, 16, T], BF16)
    r_b0, r_b1 = Res("bufR0"), Res("bufR1")
    fhist, r_fhist = sb("fhist", [128, 48, NSEQ * 2])
    r_fh = [Res("fh%d" % i) for i in range(48)]
    upext = Rot([sb("upext%d" % i, [128, 2 + T]) for i in range(4)])
    upc = Rot([sb("upc%d" % i, [128, T]) for i in range(6)])
    wslot = Rot([sb("w%d" % i, [128, 8, 256], BF16) for i in range(6)])
    wstage = Rot([sb("wst%d" % i, [128, 8, 256]) for i in range(2)])
    wao, r_wao = sb("wao", [64, 4, 512], BF16)
    kvrow = Rot([sb("kvrow%d" % i, [128, KVW], BF16) for i in range(2)])
    kvrow32 = Rot([sb("kvrow32_%d" % i, [128, 512]) for i in range(2)])
    kvt = Rot([sb("kvt%d" % i, [128, KVW], BF16) for i in range(6)])
    kT = Rot([sb("kT%d" % i, [128, 4, 128], BF16) for i in range(4)])
    p_sb = Rot([sb("p%d" % i, [128, 4, 128], BF16) for i in range(6)])

    def ps(name, shape):
        return nc.alloc_psum_tensor("ps_" + name, list(shape), F32), Res(name)

    mainps = Rot([ps("mps%d" % i, [128, 512]) for i in range(4)])
    sps = Rot([ps("sps%d" % i, [128, 4, 128]) for i in range(2)])
    pvps, r_pvps = ps("pvps", [128, 2, 4, 128])

    S.dma("sp", vecs[:], vecs_d, writes=[r_vecs])
    S.dma("sp", ident32[:], ident_d, writes=[r_ident32])
    S.op("dve", lambda e: e.tensor_copy(out=ident[:], in_=ident32[:]), reads=[r_ident32], writes=[r_ident])
    S.dma("sp", mprev[:], mprev_d.rearrange("p (h q) -> p h q", h=4), writes=[r_mprev])
    S.dma("sp", mcur[:], mcur_d.rearrange("p (h q) -> p h q", h=4), writes=[r_mcur])
    S.op("dve", lambda e: e.memset(ones_t[:], 1.0), writes=[r_ones])
    S.op("dve", lambda e: e.memset(eps_t[:], EPS), writes=[r_eps])
    for rot in (kvt, kT, p_sb):
        for (tt_, rr_) in rot.items:
            S.op("pool", lambda e, tt_=tt_: e.memset(tt_[:], 0.0), writes=[rr_])
    S.dma("sp", valtm[:], validtm_d, writes=[r_valtm])
    cast_rr = [0]

    def cast_op(out_ap, in_ap, reads, writes):
        k = cast_rr[0] % 3
        cast_rr[0] += 1
        if k == 0:
            S.op("act", lambda e: e.activation(out=out_ap, in_=in_ap, func=AF.Copy), reads=reads, writes=writes)
        elif k == 1:
            S.op("pool", lambda e: e.tensor_copy(out=out_ap, in_=in_ap), reads=reads, writes=writes)
        else:
            S.op("dve", lambda e: e.tensor_copy(out=out_ap, in_=in_ap), reads=reads, writes=writes)

    for l in range(2):
        for i in range(8):
            dgt, rdg = dgbuf.next()
            for j in range(31):
                en = "dve" if j % 2 == 0 else "pool"
                S.op(en, lambda e, dgt=dgt, j=j, l=l, i=i: e.tensor_scalar(
                    out=dgt[:, j, :], in0=ident32[:, :], scalar1=vecs[:, V_CW + (l * 8 + i) * 31 + j:V_CW + (l * 8 + i) * 31 + j + 1],
                    scalar2=None, op0=ALU.mult), reads=[r_ident32, r_vecs], writes=[rdg])
            S.dma("pool", dgscr[l][i], dgt[:].rearrange("p j k -> p (j k)"), reads=[rdg], writes=[r_dg], merge=True)
    for l in range(2):
        for name in ("w_in", "w_conv_out", "w_attn_out", "w_out", "w_up", "w_down"):
            R_, C_ = WSH[name]
            src, rdst = wsrc[name][l], r_wb[name][l]
            for r0 in range(0, R_, 1024):
                kch = min(1024, R_ - r0) // 128
                for c0 in range(0, C_, 256):
                    st, rst = wstage.next()
                    S.dma("sp", st[:, 0:kch, :], src[r0:r0 + 128 * kch, c0:c0 + 256].rearrange(
                        "(c p) e -> p c e", p=128), writes=[rst])
                    wt, rw = wslot.next()
                    cast_op(wt[:, 0:kch, :], st[:, 0:kch, :], [rst], [rw])
                    if name == "w_attn_out":
                        S.dma("pool", wbf[name][l][r0:r0 + 128 * kch, c0:c0 + 256].rearrange(
                            "(c p) e -> p c e", p=128), wt[:, 0:kch, :], reads=[rw], writes=[rdst], merge=True)
                    else:
                        bidx = (r0 // 1024) * (C_ // 256) + c0 // 256
                        S.dma("pool", wbt[name][l][bidx].rearrange("p (c e) -> p c e", c=8), wt[:, 0:8, :],
                              reads=[rw], writes=[rdst], merge=True)
    for l in ([] if skip_init else range(2)):
        for g in range(3):
            W = GROUPS[g][0]
            for n in range(NSEQ):
                for r0 in range(0, W, 128):
                    st, rst = wstage.next()
                    stv = st[:, 0:2, :]
                    S.dma("sp", stv[:, 0, :], ck[g][l, n, r0:r0 + 128, :], writes=[rst])
                    S.dma("sp", stv[:, 1, :], cv[g][l, n, r0:r0 + 128, :], writes=[rst], merge=True)
                    kr, rkr = kvrow.next()
                    cast_op(kr[:, 0:512], stv.rearrange("p a b -> p (a b)"), [rst], [rkr])
                    S.op("pool", lambda e, kr=kr: e.memset(kr[:, 512:576], 1.0), writes=[rkr])
                    S.dma("pool", ext[l][g][n, r0:r0 + 128, :], kr[:, :], reads=[rkr], writes=[r_ext[l][g]],
                          merge=True)
                for r0 in range(LS, W, 512):
                    r1 = min(W, r0 + 512)
                    S.dma("pool", nks[g][l, n, r0 - LS:r1 - LS, :], ck[g][l, n, r0:r1, :], writes=[out_res],
                          merge=True)
                    S.dma("pool", nvs[g][l, n, r0 - LS:r1 - LS, :], cv[g][l, n, r0:r1, :], writes=[out_res],
                          merge=True)

    for (st, rst) in wstage.items:
        v16 = st[:, :, :].bitcast(BF16)
        for half in range(2):
            rr = Res("wx")
            rr.w = dict(rst.w)
            rr.r = dict(rst.r)
            wslot.items.append((v16[:, :, 256 * half:256 * half + 256], rr))

    def vcol(off, n=1):
        return vecs[:, off:off + n]

    def load_w(name, l, c0, r0=0, ncols=256, kch=8):
        wt, rw = wslot.next()
        bidx = (r0 // 1024) * (WSH[name][1] // 256) + c0 // 256
        S.dma("sp", wt[:, 0:8, 0:256], wbt[name][l][bidx].rearrange("p (c e) -> p c e", c=8),
              reads=[r_wb[name][l]], writes=[rw])
        return wt, rw

    def rms(l_gain_off, tn, src=x_sb, r_src=None):
        r_src = r_src or r_x
        pt, rp = mainps.next()
        for c in range(8):
            tt, rtt = (tmpA, r_tmpA) if c % 2 == 0 else (tmpB, r_tmpB)
            S.op("act", lambda e, c=c, tt=tt: e.activation(out=tt[:, 0:tn], in_=src[:, c, 0:tn], func=AF.Square),
                 reads=[r_src], writes=[rtt])
            S.op("pe", lambda e, c=c, tt=tt: e.matmul(pt[:, 0:tn], ones_t[:], tt[:, 0:tn], start=(c == 0),
                                                       stop=(c == 7)), reads=[rtt, r_ones], writes=[rp])
        S.op("act", lambda e: e.activation(out=rstd[:, 0:tn], in_=pt[:, 0:tn], func=AF.Sqrt, bias=eps_t[:, 0:1],
                                           scale=1.0 / D), reads=[rp, r_eps], writes=[r_rstd])
        S.op("dve", lambda e: e.reciprocal(out=rstd[:, 0:tn], in_=rstd[:, 0:tn]), reads=[r_rstd], writes=[r_rstd])
        for c in range(8):
            S.op("dve", lambda e, c=c: e.scalar_tensor_tensor(
                out=r_(h_sb[:, c, 0:tn]), in0=src[:, c, 0:tn], scalar=vcol(l_gain_off + c), in1=rstd[:, 0:tn],
                op0=ALU.mult, op1=ALU.mult), reads=[r_src, r_rstd, r_vecs], writes=[r_h])

    def proj_fm(wt, rw, wcol0, rhs_t, r_rhs, rhs_chunk0, tn, kch=8, pt=None, rp=None, start=True, stop=True,
                kparts=128):
        if pt is None:
            pt, rp = mainps.next()
        for c in range(kch):
            S.op("pe", lambda e, c=c: e.matmul(
                pt[:, 0:tn], r_(wt[0:kparts, c, wcol0:wcol0 + 128]), r_(rhs_t[0:kparts, rhs_chunk0 + c, 0:tn]),
                start=(start and c == 0), stop=(stop and c == kch - 1)),
                reads=[rw, r_rhs], writes=[rp], inc=(c == kch - 1))
        return pt, rp

    def attention_unit(l, g, keysrc, r_keys_of_row, qrow0, qcol0, i0, nq, r, d, sample, have_prev=True):
        tiles = []
        if have_prev:
            tiles.append((qrow0 + r + d * (i0 - 128), 128, mprev))
        tiles.append((qrow0 + r + d * i0, nq, mcur))
        qs = qcol0 + r + d * i0
        qsl = slice(qs, qs + d * (nq - 1) + 1, d)
        nqm = nq + (nq % 2)
        qslm = slice(qs, qs + d * (nqm - 1) + 1, d)
        ptiles = []
        for (row0, nk, mk) in tiles:
            kt, rkt = kvt.next()
            rows = keysrc[row0:row0 + d * (nk - 1) + 1:d, :]
            S.dma("act", kt[0:nk, :], rows, reads=r_keys_of_row(row0, row0 + d * (nk - 1)), writes=[rkt])
            if _CACHE.get("att") == "dma":
                continue
            tp, rtp = mainps.next()
            tp16 = tp[:, :].bitcast(BF16)
            for j in range(2):
                S.op("pe", lambda e, j=j: e.transpose(tp16[:, j * 128:j * 128 + 128], kt[:, j * 128:(j + 1) * 128],
                                                       ident[:, :]),
                     reads=[rkt, r_ident], writes=[rtp], inc=(j == 1))
            ktt, rktt = kT.next()
            tpv = tp16[:, 0:256].rearrange("p (j k) -> p j k", j=2)
            S.op("act", lambda e: e.activation(out=ktt[0:64, 0:4:2, :], in_=tpv[0:64, :, :], func=AF.Copy),
                 reads=[rtp], writes=[rktt])
            S.op("act", lambda e: e.activation(out=ktt[64:128, 1:4:2, :], in_=tpv[64:128, :, :], func=AF.Copy),
                 reads=[rtp], writes=[rktt])
            if _CACHE.get("att") == "tr":
                continue
            sp_, rsp = sps.next()
            for h in range(4):
                S.op("pe", lambda e, h=h: e.matmul(
                    sp_[:, h, 0:nqm], ktt[:, h, :], q_sb[:, 2 * g + h // 2, qslm],
                    start=True, stop=True), reads=[rktt, r_q], writes=[rsp], inc=(h == 3))
            if _CACHE.get("att") == "s":
                continue
            pp, rpp = p_sb.next()
            S.op("act", lambda e: e.activation(out=pp[0:nk, :, 0:nqm], in_=sp_[0:nk, :, 0:nqm], func=AF.Exp),
                 reads=[rsp], writes=[rpp])
            S.op("pool", lambda e, mk=mk: e.tensor_tensor(out=pp[:, :, 0:nqm], in0=pp[:, :, 0:nqm],
                                                           in1=mk[:, :, 0:nqm], op=ALU.mult),
                 reads=[rpp, r_mprev, r_mcur], writes=[rpp])
            ptiles.append((kt, rkt, pp, rpp, nk))
        def back():
            ntile = len(ptiles)
            if _CACHE.get("att") in ("dma", "tr", "s", "exp"):
                return
            for h in range(4):
                for part in range(2):
                    for ti, (kt, rkt, pp, rpp, nk) in enumerate(ptiles):
                        if part == 0:
                            lt = kt[:, 256 + 64 * h:256 + 64 * h + 128]
                        else:
                            lt = kt[:, 448:576]
                        S.op("pe", lambda e, h=h, lt=lt, pp=pp, ti=ti, part=part: e.matmul(
                            pvps[:, part, h, 0:nqm], lt, pp[:, h, 0:nqm],
                            start=(ti == 0), stop=(ti == ntile - 1)), reads=[rkt, rpp, r_ones], writes=[r_pvps],
                            inc=(h == 3 and part == 1 and ti == ntile - 1))
            if _CACHE.get("att") == "pv":
                return
            if g == 0:
                S.op("dve", lambda e: e.tensor_copy(out=acc[0:64, :, qsl], in_=pvps[0:64, 0, :, 0:nq]),
                     reads=[r_pvps], writes=[r_acc])
                S.op("dve", lambda e: e.tensor_copy(out=acc[64:128, :, qsl], in_=pvps[64:128, 1, :, 0:nq]),
                     reads=[r_pvps], writes=[r_acc])
            else:
                S.op("dve", lambda e: e.tensor_tensor(out=acc[0:64, :, qsl], in0=pvps[0:64, 0, :, 0:nq],
                                                      in1=acc[0:64, :, qsl], op=ALU.add),
                     reads=[r_pvps, r_acc], writes=[r_acc])
                S.op("dve", lambda e: e.tensor_tensor(out=acc[64:128, :, qsl], in0=pvps[64:128, 1, :, 0:nq],
                                                      in1=acc[64:128, :, qsl], op=ALU.add),
                     reads=[r_pvps, r_acc], writes=[r_acc])
        return back

    def run_tile(l, ti, kind):
        sample = kind == "sample"
        warm = kind == "warm"
        tn = TS if sample else (128 if warm else T)
        nseq, ls = (NSEQ, LS) if sample else (1, tn)
        tok0 = ti * T + (T - 128 if warm else 0)
        last_layer = l == 1
        Wl = w_in[l]
        if sample:
            if l == 0:
                S.dma("sp", x_sb[:, :, 0:tn], xsT.rearrange("(c p) t -> p c t", p=128), writes=[r_x])
            else:
                S.dma("sp", x_sb[:, :, 0:tn], xs1scr.rearrange("(c p) t -> p c t", p=128),
                      reads=[r_xs1], writes=[r_x])
        else:
            src = xT if l == 0 else x1scr
            rd = [] if l == 0 else [r_x1[ti]]
            S.dma("sp", x_sb[:, :, 0:tn], src[:, tok0:tok0 + tn].rearrange("(c p) t -> p c t", p=128),
                  reads=rd, writes=[r_x])
            if kind != "kv":
                S.dma("sp", vmask[:, 0:tn], validT[:, tok0:tok0 + tn], writes=[r_vmask])
        if _CACHE.get("stop") == "A":
            return
        rms(V_GATT + 8 * l, tn)
        if _CACHE.get("stop") == "B":
            return
        nblk = (tn + 127) // 128
        for g in ([] if warm else range(3)):
            Wg = GROUPS[g][0]
            wk, rwk = load_w("w_in", l, KOFF + 256 * g)
            wv, rwv = load_w("w_in", l, VOFF + 256 * g)
            for b in range(nblk):
                nb = min(128, tn - 128 * b)
                pt, rp = mainps.next()
                for (wt, rw, co) in ((wk, rwk, 0), (wv, rwv, 256)):
                    for c in range(8):
                        S.op("pe", lambda e, c=c, wt=wt, co=co: e.matmul(
                            pt[:, co:co + 256], h_sb[:, c, 128 * b:128 * b + 128], wt[:, c, 0:256],
                            start=(c == 0), stop=(c == 7)), reads=[r_h, rw], writes=[rp], inc=(c == 7))
                kr, rkr = kvrow.next()
                S.op("act", lambda e, kr=kr, pt=pt: e.activation(out=kr[0:nb, 0:512], in_=pt[0:nb, :], func=AF.Copy),
                     reads=[rp], writes=[rkr])
                orow = tok0 + 128 * b - (NS - Wg)
                need32 = sample or (ti >= 10 and orow >= 0)
                if _CACHE.get("cstop") == "mm":
                    continue
                if need32:
                    k32, rk32 = kvrow32.next()
                    S.op("act", lambda e, k32=k32, pt=pt: e.activation(out=k32[:, :], in_=pt[:, :], func=AF.Copy),
                         reads=[rp], writes=[rk32])
                if _CACHE.get("cstop") == "k32":
                    continue
                if sample:
                    S.op("pool", lambda e, kr=kr: e.memset(kr[0:nb, 512:576], 1.0), writes=[rkr])
                    for n in range(NSEQ if _CACHE.get("sdma") != "0" else 0):
                        S.dma("pool", ext[l][g][n, Wg:Wg + LS, :], kr[LS * n:LS * n + LS, :],
                              reads=[rkr], writes=[r_ext[l][g]], merge=True)
                        S.dma("pool", nks[g][l, n, Wg - LS:Wg, :], k32[LS * n:LS * n + LS, 0:256],
                              reads=[rk32], writes=[out_res], merge=True)
                        S.dma("pool", nvs[g][l, n, Wg - LS:Wg, :], k32[LS * n:LS * n + LS, 256:512],
                              reads=[rk32], writes=[out_res], merge=True)
                else:
                    blk = ti * 4 + b
                    S.op("pool", lambda e, kr=kr, blk=blk: e.tensor_scalar(
                        out=kr[0:nb, 512:576], in0=ones_t[0:nb, 0:64], scalar1=valtm[0:nb, blk:blk + 1],
                        scalar2=None, op0=ALU.mult), reads=[r_ones, r_valtm], writes=[rkr])
                    S.dma("pool", kvscr[l][g][tok0 + 128 * b:tok0 + 128 * b + nb, :], kr[0:nb, :],
                          reads=[rkr], writes=[r_kv[l][g][ti]], merge=True)
                    if need32:
                        S.dma("pool", nkp[g][l, orow:orow + nb, :], k32[0:nb, 0:256], reads=[rk32],
                              writes=[out_res], merge=True)
                        S.dma("pool", nvp[g][l, orow:orow + nb, :], k32[0:nb, 256:512], reads=[rk32],
                              writes=[out_res], merge=True)
        if kind == "kv":
            return
        if _CACHE.get("stop") == "C" and kind != "kv":
            return
        for e2 in range(3):
            wq, rwq = load_w("w_in", l, QOFF + 256 * e2)
            for k in range(2):
                e6 = 2 * e2 + k
                pt, rp = proj_fm(wq, rwq, 128 * k, h_sb, r_h, 0, tn)
                S.op("act", lambda e, e6=e6, pt=pt: e.activation(out=q_sb[:, e6, 0:tn], in_=pt[:, 0:tn],
                                                                 func=AF.Copy, scale=0.125), reads=[rp], writes=[r_q])
        if _CACHE.get("stop") == "D" and kind != "kv":
            return
        units = []
        for g in range(int(_CACHE.get("ng", 3))):
            Wg, d = GROUPS[g]
            for n in range(nseq):
                if sample:
                    keysrc = ext[l][g][n]
                    rk = lambda a_, b_, l=l, g=g: [r_ext[l][g]]
                    qrow0, qcol0 = Wg, LS * n
                else:
                    keysrc = kvscr[l][g]
                    rk = lambda a_, b_, l=l, g=g: r_kv[l][g][a_ // T:b_ // T + 1]
                    qrow0, qcol0 = tok0, 0
                for r in range(min(d, ls)):
                    nsub = (ls - r + d - 1) // d
                    for i0 in range(0, nsub, 128):
                        nq = min(128, nsub - i0)
                        units.append((g, keysrc, rk, qrow0, qcol0, i0, nq, r, d))
        uh = uhist[:, :, 0:nseq * 30].rearrange("p c (n t) -> p c n t", n=nseq)
        if sample:
            S.dma("sp", uh, sconv[l].rearrange("(c p) n t -> p c n t", p=128), writes=[r_uhist])
        elif warm:
            S.op("pool", lambda e: e.memset(uhist[:], 0.0), writes=[r_uhist])
        glu_w = {}

        def conv_chunk(i):
            i2, k = i // 2, i % 2
            if k == 0:
                glu_w["a"] = load_w("w_in", l, 256 * i2)
                glu_w["b"] = load_w("w_in", l, 1024 + 256 * i2)
            wa, rwa = glu_w["a"]
            wb, rwb = glu_w["b"]
            pa, rpa = proj_fm(wa, rwa, 128 * k, h_sb, r_h, 0, tn)
            pb, rpb = proj_fm(wb, rwb, 128 * k, h_sb, r_h, 0, tn)
            cdst = cbuf[:, i, 0:tn].rearrange("p (n t) -> p n t", n=nseq)
            cw = V_CW + (l * 8 + i) * 31
            S.op("act", lambda e, pb=pb: e.activation(out=tmpA[:, 0:tn], in_=pb[:, 0:tn], func=AF.Sigmoid),
                 reads=[rpb], writes=[r_tmpA])
            if sample:
                uwt, ruw = uws.next()
                ue = uwt[:, 0:nseq * (30 + ls)].rearrange("p (n t) -> p n t", n=nseq)
                S.op("pool", lambda e, ue=ue, i=i: e.tensor_copy(out=ue[:, :, 0:30], in_=uh[:, i, :, :]),
                     reads=[r_uhist], writes=[ruw])
                S.op("dve", lambda e, pa=pa, ue=ue: e.tensor_tensor(
                    out=ue[:, :, 30:30 + ls], in0=pa[:, 0:tn].rearrange("p (n t) -> p n t", n=nseq),
                    in1=tmpA[:, 0:tn].rearrange("p (n t) -> p n t", n=nseq), op=ALU.mult),
                    reads=[rpa, r_tmpA], writes=[ruw])
                S.op("pool", lambda e, ue=ue, i=i: e.tensor_copy(out=uh[:, i, :, :], in_=ue[:, :, ls:ls + 30]),
                     reads=[ruw], writes=[r_uhist])
                S.op("dve", lambda e, ue=ue, cdst=cdst, cw=cw, i=i: e.tensor_scalar(
                    out=cdst, in0=ue[:, :, 0:ls], scalar1=vcol(cw), scalar2=vcol(V_CB + 8 * l + i),
                    op0=ALU.mult, op1=ALU.add), reads=[ruw, r_vecs], writes=[r_cbuf])
                for j in range(1, 31):
                    S.op("dve", lambda e, ue=ue, j=j, cdst=cdst, cw=cw: e.scalar_tensor_tensor(
                        out=cdst, in0=ue[:, :, j:j + ls], scalar=vcol(cw + j), in1=cdst,
                        op0=ALU.mult, op1=ALU.add), reads=[ruw, r_cbuf, r_vecs], writes=[r_cbuf])
            else:
                uwt, ruw = uwork.next()
                S.op("pool", lambda e, uwt=uwt, i=i: e.tensor_copy(out=uwt[:, 0:30], in_=uhist[:, i, 0:30]),
                     reads=[r_uhist], writes=[ruw])
                S.op("dve", lambda e, pa=pa, uwt=uwt: e.tensor_tensor(out=uwt[:, 30:30 + tn], in0=pa[:, 0:tn],
                                                                      in1=tmpA[:, 0:tn], op=ALU.mult),
                     reads=[rpa, r_tmpA], writes=[ruw])
                S.op("dve", lambda e, pa=pa, i=i: e.tensor_tensor(out=uhist[:, i, 0:30], in0=pa[:, tn - 30:tn],
                                                                   in1=tmpA[:, tn - 30:tn], op=ALU.mult),
                     reads=[rpa, r_tmpA, ruw], writes=[r_uhist])
                dgt, rdg = dgbuf.next()
                S.dma("sp", dgt[:].rearrange("p j k -> p (j k)"), dgscr[l][i], reads=[r_dg], writes=[rdg])
                pc, rpc = mainps.next()
                for j in range(31):
                    S.op("pe", lambda e, j=j, dgt=dgt, uwt=uwt, pc=pc: e.matmul(
                        pc[:, 0:tn], dgt[:, j, :], uwt[:, j:j + tn], start=(j == 0), stop=(j == 30)),
                        reads=[rdg, ruw], writes=[rpc], inc=(j == 30))
                S.op("act", lambda e, pc=pc, i=i: e.activation(out=cbuf[:, i, 0:tn], in_=pc[:, 0:tn],
                                                               func=AF.Identity, bias=vcol(V_CB + 8 * l + i),
                                                               scale=1.0), reads=[rpc, r_vecs], writes=[r_cbuf])
        stride = max(1, (len(units) + 7) // 8)
        next_chunk = 0
        pend_back = None
        for ui, (g, keysrc, rk, qrow0, qcol0, i0, nq, r, d) in enumerate(units):
            bk = attention_unit(l, g, keysrc, rk, qrow0, qcol0, i0, nq, r, d, sample)
            if pend_back is not None:
                pend_back()
            pend_back = bk
            if ui % stride == stride - 1 and next_chunk < 8:
                conv_chunk(next_chunk)
                next_chunk += 1
        if pend_back is not None:
            pend_back()
        while next_chunk < 8:
            conv_chunk(next_chunk)
            next_chunk += 1
        S.op("dve", lambda e: e.tensor_scalar(out=acc[64:128, :, 0:tn], in0=acc[64:128, :, 0:tn], scalar1=1e-30,
                                              scalar2=None, op0=ALU.max), reads=[r_acc], writes=[r_acc])
        for h, (tt_, rt_) in enumerate(((tmpA, r_tmpA), (tmpB, r_tmpB), (rstd, r_rstd), (mean, r_mean))):
            S.op("dve", lambda e, h=h, tt_=tt_: e.reciprocal(out=tt_[0:64, 0:tn], in_=acc[64:128, h, 0:tn]),
                 reads=[r_acc], writes=[rt_])
            S.op("dve", lambda e, h=h, tt_=tt_: e.tensor_tensor(out=attn_sb[:, h, 0:tn], in0=acc[0:64, h, 0:tn],
                                                                in1=tt_[0:64, 0:tn], op=ALU.mult),
                 reads=[r_acc, rt_], writes=[r_attn])
        if sample:
            S.dma("pool", ncs[l].rearrange("(c p) n t -> p c n t", p=128), uh, reads=[r_uhist], writes=[out_res], merge=True)
        elif ti == NT - 1:
            S.dma("pool", ncp[l].rearrange("(c p) t -> p c t", p=128), uhist[:, :, 0:30],
                  reads=[r_uhist], writes=[out_res], merge=True)
        psum_sum, r_psum = mainps.next()
        for i in range(8):
            S.op("pe", lambda e, i=i: e.matmul(psum_sum[:, 0:tn], ones_t[:], cbuf[:, i, 0:tn],
                                                start=(i == 0), stop=(i == 7)),
                 reads=[r_cbuf, r_ones], writes=[r_psum], inc=(i == 7))
        psq, r_psq = mainps.next()
        for i in range(8):
            tt, rtt = (tmpA, r_tmpA) if i % 2 == 0 else (tmpB, r_tmpB)
            S.op("act", lambda e, i=i, tt=tt: e.activation(out=tt[:, 0:tn], in_=cbuf[:, i, 0:tn], func=AF.Square),
                 reads=[r_cbuf], writes=[rtt])
            S.op("pe", lambda e, i=i, tt=tt: e.matmul(psq[:, 0:tn], ones_t[:], tt[:, 0:tn],
                                                       start=(i == 0), stop=(i == 7)),
                 reads=[rtt, r_ones], writes=[r_psq])
        S.op("act", lambda e: e.activation(out=mean[:, 0:tn], in_=psum_sum[:, 0:tn], func=AF.Copy, scale=1.0 / D),
             reads=[r_psum], writes=[r_mean])
        S.op("dve", lambda e: e.tensor_tensor(out=tmpA[:, 0:tn], in0=mean[:, 0:tn], in1=mean[:, 0:tn], op=ALU.mult),
             reads=[r_mean], writes=[r_tmpA])
        S.op("dve", lambda e: e.scalar_tensor_tensor(out=tmpA[:, 0:tn], in0=psq[:, 0:tn], scalar=1.0 / D,
                                                     in1=tmpA[:, 0:tn], op0=ALU.mult, op1=ALU.subtract),
             reads=[r_psq, r_tmpA], writes=[r_tmpA])
        S.op("dve", lambda e: e.tensor_scalar(out=tmpA[:, 0:tn], in0=tmpA[:, 0:tn], scalar1=0.0, scalar2=None,
                                              op0=ALU.max), reads=[r_tmpA], writes=[r_tmpA])
        S.op("act", lambda e: e.activation(out=tmpA[:, 0:tn], in_=tmpA[:, 0:tn], func=AF.Sqrt, bias=eps_t[:, 0:1],
                                           scale=1.0), reads=[r_tmpA, r_eps], writes=[r_tmpA])
        S.op("dve", lambda e: e.reciprocal(out=tmpA[:, 0:tn], in_=tmpA[:, 0:tn]), reads=[r_tmpA], writes=[r_tmpA])
        for i in range(8):
            S.op("dve", lambda e, i=i: e.tensor_tensor(out=cbuf[:, i, 0:tn], in0=cbuf[:, i, 0:tn],
                                                       in1=mean[:, 0:tn], op=ALU.subtract),
                 reads=[r_cbuf, r_mean], writes=[r_cbuf])
            S.op("pool", lambda e, i=i: e.tensor_tensor(out=cbuf[:, i, 0:tn], in0=cbuf[:, i, 0:tn],
                                                        in1=tmpA[:, 0:tn], op=ALU.mult),
                 reads=[r_cbuf, r_tmpA], writes=[r_cbuf])
            S.op("act", lambda e, i=i: e.activation(out=r_(bufR[:, i, 0:tn]), in_=cbuf[:, i, 0:tn], func=AF.Silu,
                                                    bias=vcol(V_LNB + 8 * l + i), scale=vcol(V_LNG + 8 * l + i)),
                 reads=[r_cbuf, r_vecs], writes=[r_b0])
        if _CACHE.get("stop") == "G" and kind != "kv":
            return
        for j2 in range(4):
            if j2 % 2 == 0:
                S.dma("sp", wao[:], wbf["w_attn_out"][l][:, 256 * j2:256 * j2 + 512].rearrange(
                    "(h p) e -> p h e", p=64), reads=[r_wb["w_attn_out"][l]], writes=[r_wao])
            wgc, rwgc = load_w("w_in", l, GCOFF + 256 * j2)
            wga, rwga = load_w("w_in", l, GAOFF + 256 * j2)
            wco, rwco = load_w("w_conv_out", l, 256 * j2)
            for k in range(2):
                j = 2 * j2 + k
                pgc, rpgc = proj_fm(wgc, rwgc, 128 * k, h_sb, r_h, 0, tn)
                S.op("act", lambda e, pgc=pgc: e.activation(out=tmpA[:, 0:tn], in_=pgc[:, 0:tn], func=AF.Sigmoid),
                     reads=[rpgc], writes=[r_tmpA])
                pga, rpga = proj_fm(wga, rwga, 128 * k, h_sb, r_h, 0, tn)
                S.op("act", lambda e, pga=pga: e.activation(out=tmpB[:, 0:tn], in_=pga[:, 0:tn], func=AF.Sigmoid),
                     reads=[rpga], writes=[r_tmpB])
                pbc, rpbc = proj_fm(wco, rwco, 128 * k, bufR, r_b0, 0, tn)
                S.op("dve", lambda e, pbc=pbc: e.tensor_tensor(out=tmpA[:, 0:tn], in0=pbc[:, 0:tn], in1=tmpA[:, 0:tn],
                                                               op=ALU.mult), reads=[rpbc, r_tmpA], writes=[r_tmpA])
                pba, rpba = mainps.next()
                wc0 = 128 * (j % 4)
                for h in range(4):
                    S.op("pe", lambda e, h=h, wc0=wc0, pba=pba: e.matmul(
                        pba[:, 0:tn], r_(wao[:, h, wc0:wc0 + 128]), r_(attn_sb[:, h, 0:tn]),
                        start=(h == 0), stop=(h == 3)), reads=[r_wao, r_attn], writes=[rpba], inc=(h == 3))
                S.op("dve", lambda e, pba=pba: e.tensor_tensor(out=tmpB[:, 0:tn], in0=pba[:, 0:tn], in1=tmpB[:, 0:tn],
                                                               op=ALU.mult), reads=[rpba, r_tmpB], writes=[r_tmpB])
                S.op("pool", lambda e, j=j: e.tensor_tensor(out=r_(bufR[:, 8 + j, 0:tn]), in0=tmpA[:, 0:tn],
                                                            in1=tmpB[:, 0:tn], op=ALU.add),
                     reads=[r_tmpA, r_tmpB], writes=[r_b1])
        if _CACHE.get("stop") == "H" and kind != "kv":
            return
        for j2 in range(4):
            wo, rwo = load_w("w_out", l, 256 * j2)
            for k in range(2):
                j = 2 * j2 + k
                po, rpo = proj_fm(wo, rwo, 128 * k, bufR, r_b1, 8, tn)
                S.op("dve", lambda e, j=j, po=po: e.tensor_tensor(out=x_sb[:, j, 0:tn], in0=po[:, 0:tn],
                                                                  in1=x_sb[:, j, 0:tn], op=ALU.add),
                     reads=[rpo, r_x], writes=[r_x])
                if not sample:
                    S.op("pool", lambda e, j=j: e.tensor_tensor(out=x_sb[:, j, 0:tn], in0=x_sb[:, j, 0:tn],
                                                                in1=vmask[:, 0:tn], op=ALU.mult),
                         reads=[r_x, r_vmask], writes=[r_x])
        if _CACHE.get("stop") == "I" and kind != "kv":
            return
        rms(V_GFFN + 8 * l, tn)
        fh = fhist[:, :, 0:nseq * 2].rearrange("p c (n t) -> p c n t", n=nseq)
        if sample:
            S.dma("sp", fh, sffn[l].rearrange("(c p) n t -> p c n t", p=128), writes=r_fh)
        elif warm:
            S.op("pool", lambda e: e.memset(fhist[:], 0.0), writes=r_fh)
        Wu = w_up[l]
        Wd = w_down[l]
        for ip in range(3):
            for i2 in range(4):
                wv_, rwv_ = load_w("w_up", l, 1024 * ip + 256 * i2)
                wg_, rwg_ = load_w("w_up", l, DFF + 1024 * ip + 256 * i2)
                for k in range(2):
                    il = 2 * i2 + k
                    i = 8 * ip + il
                    res_c = []
                    for (wt, rw, idx) in ((wv_, rwv_, i), (wg_, rwg_, 24 + i)):
                        pu, rpu = proj_fm(wt, rw, 128 * k, h_sb, r_h, 0, tn)
                        uxt, ruxt = upext.next()
                        ux = uxt[:, 0:nseq * (2 + ls)].rearrange("p (n t) -> p n t", n=nseq)
                        S.op("pool", lambda e, ux=ux, idx=idx: e.tensor_copy(out=ux[:, :, 0:2], in_=fh[:, idx, :, :]),
                             reads=[r_fh[idx]], writes=[ruxt])
                        S.op("act", lambda e, ux=ux, pu=pu: e.activation(
                            out=ux[:, :, 2:2 + ls], in_=pu[:, 0:tn].rearrange("p (n t) -> p n t", n=nseq),
                            func=AF.Copy), reads=[rpu], writes=[ruxt])
                        S.op("pool", lambda e, ux=ux, idx=idx: e.tensor_copy(out=fh[:, idx, :, :],
                                                                             in_=ux[:, :, ls:ls + 2]),
                             reads=[ruxt], writes=[r_fh[idx]])
                        uct, ruc = upc.next()
                        uc = uct[:, 0:tn].rearrange("p (n t) -> p n t", n=nseq)
                        fw = V_FW + (l * 48 + idx) * 3
                        S.op("act", lambda e, ux=ux, uc=uc, fw=fw, idx=idx: e.activation(
                            out=uc, in_=ux[:, :, 0:ls], func=AF.Identity, scale=vcol(fw),
                            bias=vcol(V_FB + 48 * l + idx)), reads=[ruxt, r_vecs], writes=[ruc])
                        for j in (1, 2):
                            S.op("dve", lambda e, ux=ux, uc=uc, fw=fw, j=j: e.scalar_tensor_tensor(
                                out=uc, in0=ux[:, :, j:j + ls], scalar=vcol(fw + j), in1=uc,
                                op0=ALU.mult, op1=ALU.add), reads=[ruxt, ruc, r_vecs], writes=[ruc])
                        res_c.append((uct, ruc))
                    (vt, rv), (gt, rg) = res_c
                    S.op("act", lambda e, gt=gt: e.activation(out=gt[:, 0:tn], in_=gt[:, 0:tn], func=AF.Silu),
                         reads=[rg], writes=[rg])
                    S.op("dve", lambda e, vt=vt, gt=gt, il=il: e.tensor_tensor(
                        out=r_(bufR[:, il, 0:tn]), in0=vt[:, 0:tn], in1=gt[:, 0:tn], op=ALU.mult),
                        reads=[rv, rg], writes=[r_b0])
            for j2 in range(4):
                wd_, rwd_ = load_w("w_down", l, 256 * j2, r0=1024 * ip)
                for k in range(2):
                    j = 2 * j2 + k
                    pd, rpd = proj_fm(wd_, rwd_, 128 * k, bufR, r_b0, 0, tn)
                    S.op("dve", lambda e, j=j, pd=pd: e.tensor_tensor(out=x_sb[:, j, 0:tn], in0=pd[:, 0:tn],
                                                                      in1=x_sb[:, j, 0:tn], op=ALU.add),
                         reads=[rpd, r_x], writes=[r_x])
                    if ip == 2 and not sample:
                        S.op("pool", lambda e, j=j: e.tensor_tensor(out=x_sb[:, j, 0:tn], in0=x_sb[:, j, 0:tn],
                                                                    in1=vmask[:, 0:tn], op=ALU.mult),
                             reads=[r_x, r_vmask], writes=[r_x])
        if sample:
            S.dma("pool", nfs[l].rearrange("(c p) n t -> p c n t", p=128), fh, reads=r_fh, writes=[out_res], merge=True)
        elif ti == NT - 1:
            S.dma("pool", nfp[l].rearrange("(c p) t -> p c t", p=128), fhist[:, :, 0:2],
                  reads=r_fh, writes=[out_res], merge=True)
        if _CACHE.get("stop") == "J" and kind != "kv":
            return
        if warm:
            return
        if not last_layer:
            if sample:
                S.dma("pool", xs1scr.rearrange("(c p) t -> p c t", p=128), x_sb[:, :, 0:tn],
                      reads=[r_x], writes=[r_xs1])
            else:
                S.dma("pool", x1scr[:, tok0:tok0 + T].rearrange("(c p) t -> p c t", p=128), x_sb[:, :, :],
                      reads=[r_x], writes=[r_x1[ti]])
        else:
            if sample or tok0 >= OWN0:
                rms(V_GFIN, tn)
                if sample:
                    S.dma("pool", ysT.rearrange("(c p) t -> p c t", p=128), h_sb[:, :, 0:tn],
                          reads=[r_h], writes=[out_res], merge=True)
                else:
                    S.dma("pool", yT[:, tok0 - OWN0:tok0 - OWN0 + T].rearrange("(c p) t -> p c t", p=128),
                          h_sb[:, :, :], reads=[r_h], writes=[out_res], merge=True)

    xs1scr = dscr("xs1scr", [D, TS])
    r_xs1 = Res("xs1")

    ntile_done = 0
    for l in range(2):
        first_full = 4 if l == 0 else 9
        first_kv = 0 if l == 0 else 5
        sched = ([(ti, "kv") for ti in range(first_kv, first_full + 1)] + [(first_full, "warm")]
                 + [(ti, "full") for ti in range(first_full + 1, NT)] + [(0, "sample")])
        for (ti, kind) in sched:
            if max_tiles is not None and ntile_done >= max_tiles:
                continue
            if _CACHE.get("only") == "sample" and kind != "sample":
                ntile_done += 1
                continue
            run_tile(l, ti, kind)
            ntile_done += 1
            if dbg_tile == (l, ti, kind):
                tn_ = TS if kind == "sample" else T
                S.dma("pool", dbgx[:, 0:tn_].rearrange("(c p) t -> p c t", p=128), x_sb[:, :, 0:tn_],
                      reads=[r_x], writes=[out_res], merge=True)
    S.finish("pool", [out_res])
    S.finish("sp", [out_res])
    return nc, S


_CACHE = {}


def _consts():
    jj = np.arange(128)[:, None]
    ii = np.arange(128)[None, :]
    mprev = np.tile((jj >= ii).astype(np.float32), (1, 4))
    mcur = np.tile((jj <= ii).astype(np.float32), (1, 4))
    return mprev, mcur


def _pack_vecs(norm_attn_g, norm_ffn_g, norm_final_g, conv_dw_b, conv_ln_g, conv_ln_b, conv_dw_w, ffn_dw_b, ffn_dw_w):
    v = np.zeros((128, NV), np.float32)

    def pc(a):
        a = np.asarray(a, np.float32)
        c = a.shape[-1] // 128
        a = a.reshape(a.shape[:-1] + (c, 128))
        return np.moveaxis(a, -1, 0)

    v[:, V_GATT:V_GATT + 16] = pc(norm_attn_g).reshape(128, 16)
    v[:, V_GFFN:V_GFFN + 16] = pc(norm_ffn_g).reshape(128, 16)
    v[:, V_GFIN:V_GFIN + 8] = pc(norm_final_g).reshape(128, 8)
    v[:, V_CB:V_CB + 16] = pc(conv_dw_b).reshape(128, 16)
    v[:, V_LNG:V_LNG + 16] = pc(conv_ln_g).reshape(128, 16)
    v[:, V_LNB:V_LNB + 16] = pc(conv_ln_b).reshape(128, 16)
    cw = pc(conv_dw_w)
    v[:, V_CW:V_CW + 2 * 8 * 31] = np.transpose(cw, (0, 1, 3, 2)).reshape(128, -1)
    v[:, V_FB:V_FB + 96] = pc(ffn_dw_b).reshape(128, 96)
    fw = pc(ffn_dw_w)
    v[:, V_FW:V_FW + 2 * 48 * 3] = np.transpose(fw, (0, 1, 3, 2)).reshape(128, -1)
    return v


def kernel(x_prompt, x_sample, state_conv, cache_k_w128, cache_v_w128, cache_k_w512, cache_v_w512,
           cache_k_w2048, cache_v_w2048, state_ffn_conv, norm_attn_g, w_in, conv_dw_w, conv_dw_b,
           conv_ln_g, conv_ln_b, w_conv_out, w_attn_out, w_out, norm_ffn_g, w_up, ffn_dw_w, ffn_dw_b,
           w_down, norm_final_g):
    f = lambda a: np.ascontiguousarray(np.asarray(a, dtype=np.float32))
    if "nc" not in _CACHE:
        _CACHE["nc"] = build_program()
    nc, S = _CACHE["nc"]
    mprev, mcur = _consts()
    vecs = _pack_vecs(norm_attn_g, norm_ffn_g, norm_final_g, conv_dw_b, conv_ln_g, conv_ln_b, conv_dw_w,
                      ffn_dw_b, ffn_dw_w)
    xp = f(x_prompt)[0]
    xpT_pad = np.zeros((D, OWN0 + SEQ), np.float32)
    xpT_pad[:, OWN0:] = xp.T
    ck = [f(cache_k_w128), f(cache_k_w512), f(cache_k_w2048)]
    cv = [f(cache_v_w128), f(cache_v_w512), f(cache_v_w2048)]
    shared = dict(w_in=f(w_in), w_conv_out=f(w_conv_out), w_attn_out=f(w_attn_out), w_out=f(w_out), w_up=f(w_up),
                  w_down=f(w_down), vecs=vecs, ident=np.eye(128, dtype=np.float32), mprev=mprev, mcur=mcur,
                  )
    xs = f(x_sample)
    sc = f(state_conv)
    sf = f(state_ffn_conv)
    in_maps = []
    for c in range(NCORES):
        s = c * OWN
        m = dict(shared)
        m["xT"] = np.ascontiguousarray(xpT_pad[:, s:s + NS])
        val = (np.arange(NS) + s - OWN0 >= 0).astype(np.float32)
        m["validT"] = np.ascontiguousarray(np.broadcast_to(val[None, :], (128, NS)))
        m["validtm"] = np.ascontiguousarray(val.reshape(NS // 128, 128).T)
        n0 = c * NSEQ
        m["xsT"] = np.ascontiguousarray(xs[n0:n0 + NSEQ].reshape(TS, D).T)
        m["sconv"] = np.ascontiguousarray(np.transpose(sc[:, n0:n0 + NSEQ], (0, 3, 1, 2)))
        m["sffn"] = np.ascontiguousarray(np.transpose(sf[:, n0:n0 + NSEQ], (0, 3, 1, 2)))
        for g in range(3):
            W = GROUPS[g][0]
            m["ck%d" % g] = np.ascontiguousarray(ck[g][:, n0:n0 + NSEQ].reshape(2, NSEQ, W, 256))
            m["cv%d" % g] = np.ascontiguousarray(cv[g][:, n0:n0 + NSEQ].reshape(2, NSEQ, W, 256))
        in_maps.append(m)
    if _CACHE.get("dbg_in_maps_only"):
        return in_maps
    res = run_bass_kernel_spmd(nc, in_maps, core_ids=list(range(NCORES)))
    R = res.results
    y_prompt = np.concatenate([R[c]["yT"].T for c in range(NCORES)], axis=0)[None]
    y_sample = np.concatenate([R[c]["ysT"].T.reshape(NSEQ, LS, D) for c in range(NCORES)], axis=0)
    last = R[NCORES - 1]
    conv_p = np.transpose(last["ncp"], (0, 2, 1))[:, None]
    conv_s = np.concatenate([np.transpose(R[c]["ncs"], (0, 2, 3, 1)) for c in range(NCORES)], axis=1)
    outs = [np.ascontiguousarray(y_prompt), np.ascontiguousarray(y_sample),
            np.ascontiguousarray(conv_p), np.ascontiguousarray(conv_s)]
    for g in range(3):
        W = GROUPS[g][0]
        outs.append(np.ascontiguousarray(last["nkp%d" % g].reshape(2, 1, W, 4, 64)))
        outs.append(np.ascontiguousarray(last["nvp%d" % g].reshape(2, 1, W, 4, 64)))
        outs.append(np.concatenate([R[c]["nks%d" % g].reshape(2, NSEQ, W, 4, 64) for c in range(NCORES)], axis=1))
        outs.append(np.concatenate([R[c]["nvs%d" % g].reshape(2, NSEQ, W, 4, 64) for c in range(NCORES)], axis=1))
    ffn_p = np.transpose(last["nfp"], (0, 2, 1))[:, None]
    ffn_s = np.concatenate([np.transpose(R[c]["nfs"], (0, 2, 3, 1)) for c in range(NCORES)], axis=1)
    outs.append(np.ascontiguousarray(ffn_p))
    outs.append(np.ascontiguousarray(ffn_s))
    return tuple(np.asarray(o, dtype=np.float32) for o in outs)
```

```python
import numpy as np
import concourse.bass as bass
import concourse.mybir as mybir
from concourse.bass_utils import run_bass_kernel_spmd

F32 = mybir.dt.float32
F32R = mybir.dt.float32r
BF16 = mybir.dt.bfloat16
AF = mybir.ActivationFunctionType
ALU = mybir.AluOpType

NCORES = 8
D = 1024
DIN = 6400
DFF = 3072
SEQ = 16384
OWN = SEQ // NCORES
T = 512
NT = 14
NS = NT * T
OWN0 = NS - OWN
NSEQ = 4
LS = 8
TS = NSEQ * LS
GROUPS = ((128, 1), (512, 4), (2048, 16))
EPS = 1e-6
KVW = 576
QOFF, KOFF, VOFF, GCOFF, GAOFF = 2048, 2816, 3584, 4352, 5376

V_GATT = 0
V_GFFN = 16
V_GFIN = 32
V_CB = 40
V_LNG = 56
V_LNB = 72
V_CW = 88
V_FB = V_CW + 2 * 8 * 31
V_FW = V_FB + 96
NV = V_FW + 2 * 48 * 3


class Res:
    __slots__ = ("name", "w", "r")

    def __init__(self, name):
        self.name = name
        self.w = {}
        self.r = {}


class Sched:
    def __init__(self, nc, n_dma_sems=48):
        self.nc = nc
        self.eng = {}
        self.sems = {}
        for name, h in (("pe", nc.tensor), ("act", nc.scalar), ("dve", nc.vector),
                        ("pool", nc.gpsimd), ("sp", nc.sync)):
            self.eng[name] = dict(h=h, sem=nc.alloc_semaphore("s_" + name), cnt=0, seen={})
            self.sems[name] = self.eng[name]["sem"]
        self.dma_sems = []
        for i in range(n_dma_sems):
            k = "d%d" % i
            self.sems[k] = nc.alloc_semaphore("s_" + k)
            self.dma_sems.append(dict(key=k, val=0))
        self.dma_rr = 0
        self.n_inst = 0
        self.n_wait = 0

    def _need(self, reads, writes, skip_key=None):
        need = {}
        for r in reads:
            for k, v in r.w.items():
                if need.get(k, 0) < v:
                    need[k] = v
        for r in writes:
            for k, v in r.w.items():
                if need.get(k, 0) < v:
                    need[k] = v
            for k, v in r.r.items():
                if need.get(k, 0) < v:
                    need[k] = v
        if skip_key is not None:
            need.pop(skip_key, None)
        return need

    def _waits(self, ename, need):
        e = self.eng[ename]
        for k, v in need.items():
            if e["seen"].get(k, 0) >= v:
                continue
            e["h"].wait_ge(self.sems[k], v)
            e["seen"][k] = v
            self.n_wait += 1

    def _commit(self, reads, writes, key, val):
        for r in writes:
            r.w = {key: val}
            r.r = {}
        for r in reads:
            if r.r.get(key, 0) < val:
                r.r[key] = val

    def op(self, ename, fn, reads=(), writes=(), inc=True):
        e = self.eng[ename]
        need = self._need(reads, writes, skip_key=("pe" if ename == "pe" else None))
        self._waits(ename, need)
        ins = fn(e["h"])
        if inc:
            e["cnt"] += 1
            ins.then_inc(e["sem"], 1)
            self._commit(reads, writes, ename, e["cnt"])
        else:
            self._commit(reads, writes, ename, e["cnt"] + 1)
        self.n_inst += 1
        return ins

    def dma(self, qname, out, in_, reads=(), writes=(), merge=False):
        e = self.eng[qname]
        d = self.dma_sems[self.dma_rr]
        self.dma_rr = (self.dma_rr + 1) % len(self.dma_sems)
        if merge:
            need = self._need(reads, ())
            for r in writes:
                for k, v in r.r.items():
                    if need.get(k, 0) < v:
                        need[k] = v
        else:
            need = self._need(reads, writes)
        if d["val"] > 0 and need.get(d["key"], 0) < d["val"]:
            need[d["key"]] = d["val"]
        self._waits(qname, need)
        ins = e["h"].dma_start(out=out, in_=in_)
        d["val"] += 16
        ins.then_inc(self.sems[d["key"]], 16)
        if merge:
            for r in writes:
                r.w[d["key"]] = d["val"]
            self._commit(reads, (), d["key"], d["val"])
        else:
            self._commit(reads, writes, d["key"], d["val"])
        self.n_inst += 1
        return ins

    def finish(self, ename, resources):
        need = self._need(resources, resources)
        self._waits(ename, need)


class Rot:
    def __init__(self, items):
        self.items = items
        self.i = 0

    def next(self):
        it = self.items[self.i]
        self.i = (self.i + 1) % len(self.items)
        return it


def r_(ap):
    return ap


def build_program(max_tiles=None, skip_init=False, skip_out=False, dbg_tile=None):
    nc = bass.Bass("TRN2", target_bir_lowering=False)
    nc.dge_precook = False
    S = Sched(nc)

    def din(name, shape):
        return nc.dram_tensor(name, list(shape), F32, kind="ExternalInput").ap()

    def dout(name, shape):
        return nc.dram_tensor(name, list(shape), F32, kind="ExternalOutput").ap()

    def dscr(name, shape, dt=F32):
        return nc.dram_tensor(name, list(shape), dt, kind="Internal").ap()

    xT = din("xT", [D, NS])
    validT = din("validT", [128, NS])
    validtm_d = din("validtm", [128, NS // 128])
    xsT = din("xsT", [D, TS])
    sconv = din("sconv", [2, D, NSEQ, 30])
    sffn = din("sffn", [2, 2 * DFF, NSEQ, 2])
    ck = [din("ck%d" % g, [2, NSEQ, GROUPS[g][0], 256]) for g in range(3)]
    cv = [din("cv%d" % g, [2, NSEQ, GROUPS[g][0], 256]) for g in range(3)]
    w_in = din("w_in", [2, D, DIN])
    w_conv_out = din("w_conv_out", [2, D, D])
    w_attn_out = din("w_attn_out", [2, 256, D])
    w_out = din("w_out", [2, D, D])
    w_up = din("w_up", [2, D, 2 * DFF])
    w_down = din("w_down", [2, DFF, D])
    vecs_d = din("vecs", [128, NV])
    ident_d = din("ident", [128, 128])
    mprev_d = din("mprev", [128, 4 * 128])
    mcur_d = din("mcur", [128, 4 * 128])

    yT = dout("yT", [D, OWN])
    ysT = dout("ysT", [D, TS])
    ncp = dout("ncp", [2, D, 30])
    ncs = dout("ncs", [2, D, NSEQ, 30])
    nkp = [dout("nkp%d" % g, [2, GROUPS[g][0], 256]) for g in range(3)]
    nvp = [dout("nvp%d" % g, [2, GROUPS[g][0], 256]) for g in range(3)]
    nks = [dout("nks%d" % g, [2, NSEQ, GROUPS[g][0], 256]) for g in range(3)]
    nvs = [dout("nvs%d" % g, [2, NSEQ, GROUPS[g][0], 256]) for g in range(3)]
    nfp = dout("nfp", [2, 2 * DFF, 2])
    nfs = dout("nfs", [2, 2 * DFF, NSEQ, 2])
    out_res = Res("outputs")
    if dbg_tile is not None:
        dbgx = dout("dbgx", [D, T])

    x1scr = dscr("x1scr", [D, NS])
    r_x1 = [Res("x1_%d" % i) for i in range(NT)]
    kvscr = [[dscr("kv_%d_%d" % (l, g), [NS, KVW], BF16) for g in range(3)] for l in range(2)]
    r_kv = [[[Res("kv%d%d_%d" % (l, g, i)) for i in range(NT)] for g in range(3)] for l in range(2)]
    ext = [[dscr("ext_%d_%d" % (l, g), [NSEQ, GROUPS[g][0] + LS, KVW], BF16) for g in range(3)] for l in range(2)]
    r_ext = [[Res("ext%d%d" % (l, g)) for g in range(3)] for l in range(2)]
    WSH = dict(w_in=(D, DIN), w_conv_out=(D, D), w_attn_out=(256, D), w_out=(D, D), w_up=(D, 2 * DFF), w_down=(DFF, D))
    wsrc = dict(w_in=w_in, w_conv_out=w_conv_out, w_attn_out=w_attn_out, w_out=w_out, w_up=w_up, w_down=w_down)
    wbf = {k: [dscr("wb_%s_%d" % (k, l), list(v), BF16) for l in range(2)] for k, v in WSH.items()
           if k == "w_attn_out"}
    wbt = {k: [dscr("wt_%s_%d" % (k, l), [(v[0] // 1024) * (v[1] // 256), 128, 8 * 256], BF16) for l in range(2)]
           for k, v in WSH.items() if k != "w_attn_out"}
    r_wb = {k: [Res("wb_%s_%d" % (k, l)) for l in range(2)] for k in WSH}
    dgscr = [[dscr("dg_%d_%d" % (l, i), [128, 31 * 128], BF16) for i in range(8)] for l in range(2)]
    r_dg = Res("dgscr")

    def sb(name, shape, dt=F32):
        return nc.alloc_sbuf_tensor("sb_" + name, list(shape), dt), Res(name)

    vecs, r_vecs = sb("vecs", [128, NV])
    ident32, r_ident32 = sb("ident32", [128, 128])
    ident, r_ident = sb("ident", [128, 128], BF16)
    mprev, r_mprev = sb("mprev", [128, 4, 128])
    mcur, r_mcur = sb("mcur", [128, 4, 128])
    ones_t, r_ones = sb("ones_t", [128, 128])
    valtm, r_valtm = sb("valtm", [128, NS // 128])
    eps_t, r_eps = sb("eps_t", [128, 1])
    x_sb, r_x = sb("x_sb", [128, 8, T])
    h_sb, r_h = sb("h_sb", [128, 8, T], BF16)
    vmask, r_vmask = sb("vmask", [128, T])
    rstd, r_rstd = sb("rstd", [128, T])
    mean, r_mean = sb("mean", [128, T])
    tmpA, r_tmpA = sb("tmpA", [128, T])
    tmpB, r_tmpB = sb("tmpB", [128, T])
    cbuf, r_cbuf = sb("cbuf", [128, 8, T])
    q_sb, r_q = sb("q_sb", [128, 6, T], BF16)
    acc, r_acc = sb("acc", [128, 4, T])
    attn_sb, r_attn = sb("attn_sb", [64, 4, T], BF16)
    uhist, r_uhist = sb("uhist", [128, 8, NSEQ * 30])
    uwork = Rot([sb("uwork%d" % i, [128, 30 + T], BF16) for i in range(3)])
    uws = Rot([sb("uws%d" % i, [128, NSEQ * (30 + LS)]) for i in range(2)])
    dgbuf = Rot([sb("dg%d" % i, [128, 31, 128], BF16) for i in range(2)])
    bufR, r_bufR = sb("bufR", [128, 16, T], BF16)
    r_b0, r_b1 = Res("bufR0"), Res("bufR1")
    fhist, r_fhist = sb("fhist", [128, 48, NSEQ * 2])
    r_fh = [Res("fh%d" % i) for i in range(48)]
    upext = Rot([sb("upext%d" % i, [128, 2 + T]) for i in range(4)])
    upc = Rot([sb("upc%d" % i, [128, T]) for i in range(6)])
    wslot = Rot([sb("w%d" % i, [128, 8, 256], BF16) for i in range(6)])
    wstage = Rot([sb("wst%d" % i, [128, 8, 256]) for i in range(2)])
    wao, r_wao = sb("wao", [64, 4, 512], BF16)
    kvrow = Rot([sb("kvrow%d" % i, [128, KVW], BF16) for i in range(2)])
    kvrow32 = Rot([sb("kvrow32_%d" % i, [128, 512]) for i in range(2)])
    kvt = Rot([sb("kvt%d" % i, [128, KVW], BF16) for i in range(6)])
    kT = Rot([sb("kT%d" % i, [128, 4, 128], BF16) for i in range(4)])
    p_sb = Rot([sb("p%d" % i, [128, 4, 128], BF16) for i in range(6)])

    def ps(name, shape):
        return nc.alloc_psum_tensor("ps_" + name, list(shape), F32), Res(name)

    mainps = Rot([ps("mps%d" % i, [128, 512]) for i in range(4)])
    sps = Rot([ps("sps%d" % i, [128, 4, 128]) for i in range(2)])
    pvps, r_pvps = ps("pvps", [128, 2, 4, 128])

    S.dma("sp", vecs[:], vecs_d, writes=[r_vecs])
    S.dma("sp", ident32[:], ident_d, writes=[r_ident32])
    S.op("dve", lambda e: e.tensor_copy(out=ident[:], in_=ident32[:]), reads=[r_ident32], writes=[r_ident])
    S.dma("sp", mprev[:], mprev_d.rearrange("p (h q) -> p h q", h=4), writes=[r_mprev])
    S.dma("sp", mcur[:], mcur_d.rearrange("p (h q) -> p h q", h=4), writes=[r_mcur])
    S.op("dve", lambda e: e.memset(ones_t[:], 1.0), writes=[r_ones])
    S.op("dve", lambda e: e.memset(eps_t[:], EPS), writes=[r_eps])
    for rot in (kvt, kT, p_sb):
        for (tt_, rr_) in rot.items:
            S.op("pool", lambda e, tt_=tt_: e.memset(tt_[:], 0.0), writes=[rr_])
    S.dma("sp", valtm[:], validtm_d, writes=[r_valtm])
    cast_rr = [0]

    def cast_op(out_ap, in_ap, reads, writes):
        k = cast_rr[0] % 3
        cast_rr[0] += 1
        if k == 0:
            S.op("act", lambda e: e.activation(out=out_ap, in_=in_ap, func=AF.Copy), reads=reads, writes=writes)
        elif k == 1:
            S.op("pool", lambda e: e.tensor_copy(out=out_ap, in_=in_ap), reads=reads, writes=writes)
        else:
            S.op("dve", lambda e: e.tensor_copy(out=out_ap, in_=in_ap), reads=reads, writes=writes)

    for l in range(2):
        for i in range(8):
            dgt, rdg = dgbuf.next()
            for j in range(31):
                en = "dve" if j % 2 == 0 else "pool"
                S.op(en, lambda e, dgt=dgt, j=j, l=l, i=i: e.tensor_scalar(
                    out=dgt[:, j, :], in0=ident32[:, :], scalar1=vecs[:, V_CW + (l * 8 + i) * 31 + j:V_CW + (l * 8 + i) * 31 + j + 1],
                    scalar2=None, op0=ALU.mult), reads=[r_ident32, r_vecs], writes=[rdg])
            S.dma("pool", dgscr[l][i], dgt[:].rearrange("p j k -> p (j k)"), reads=[rdg], writes=[r_dg], merge=True)
    for l in range(2):
        for name in ("w_in", "w_conv_out", "w_attn_out", "w_out", "w_up", "w_down"):
            R_, C_ = WSH[name]
            src, rdst = wsrc[name][l], r_wb[name][l]
            for r0 in range(0, R_, 1024):
                kch = min(1024, R_ - r0) // 128
                for c0 in range(0, C_, 256):
                    st, rst = wstage.next()
                    S.dma("sp", st[:, 0:kch, :], src[r0:r0 + 128 * kch, c0:c0 + 256].rearrange(
                        "(c p) e -> p c e", p=128), writes=[rst])
                    wt, rw = wslot.next()
                    cast_op(wt[:, 0:kch, :], st[:, 0:kch, :], [rst], [rw])
                    if name == "w_attn_out":
                        S.dma("pool", wbf[name][l][r0:r0 + 128 * kch, c0:c0 + 256].rearrange(
                            "(c p) e -> p c e", p=128), wt[:, 0:kch, :], reads=[rw], writes=[rdst], merge=True)
                    else:
                        bidx = (r0 // 1024) * (C_ // 256) + c0 // 256
                        S.dma("pool", wbt[name][l][bidx].rearrange("p (c e) -> p c e", c=8), wt[:, 0:8, :],
                              reads=[rw], writes=[rdst], merge=True)
    for l in ([] if skip_init else range(2)):
        for g in range(3):
            W = GROUPS[g][0]
            for n in range(NSEQ):
                for r0 in range(0, W, 128):
                    st, rst = wstage.next()
                    stv = st[:, 0:2, :]
                    S.dma("sp", stv[:, 0, :], ck[g][l, n, r0:r0 + 128, :], writes=[rst])
                    S.dma("sp", stv[:, 1, :], cv[g][l, n, r0:r0 + 128, :], writes=[rst], merge=True)
                    kr, rkr = kvrow.next()
                    cast_op(kr[:, 0:512], stv.rearrange("p a b -> p (a b)"), [rst], [rkr])
                    S.op("pool", lambda e, kr=kr: e.memset(kr[:, 512:576], 1.0), writes=[rkr])
                    S.dma("pool", ext[l][g][n, r0:r0 + 128, :], kr[:, :], reads=[rkr], writes=[r_ext[l][g]],
                          merge=True)
                for r0 in range(LS, W, 512):
                    r1 = min(W, r0 + 512)
                    S.dma("pool", nks[g][l, n, r0 - LS:r1 - LS, :], ck[g][l, n, r0:r1, :], writes=[out_res],
                          merge=True)
                    S.dma("pool", nvs[g][l, n, r0 - LS:r1 - LS, :], cv[g][l, n, r0:r1, :], writes=[out_res],
                          merge=True)

    for (st, rst) in wstage.items:
        v16 = st[:, :, :].bitcast(BF16)
        for half in range(2):
            rr = Res("wx")
            rr.w = dict(rst.w)
            rr.r = dict(rst.r)
            wslot.items.append((v16[:, :, 256 * half:256 * half + 256], rr))

    def vcol(off, n=1):
        return vecs[:, off:off + n]

    def load_w(name, l, c0, r0=0, ncols=256, kch=8):
        wt, rw = wslot.next()
        bidx = (r0 // 1024) * (WSH[name][1] // 256) + c0 // 256
        S.dma("sp", wt[:, 0:8, 0:256], wbt[name][l][bidx].rearrange("p (c e) -> p c e", c=8),
              reads=[r_wb[name][l]], writes=[rw])
        return wt, rw

    def rms(l_gain_off, tn, src=x_sb, r_src=None):
        r_src = r_src or r_x
        pt, rp = mainps.next()
        for c in range(8):
            tt, rtt = (tmpA, r_tmpA) if c % 2 == 0 else (tmpB, r_tmpB)
            S.op("act", lambda e, c=c, tt=tt: e.activation(out=tt[:, 0:tn], in_=src[:, c, 0:tn], func=AF.Square),
                 reads=[r_src], writes=[rtt])
            S.op("pe", lambda e, c=c, tt=tt: e.matmul(pt[:, 0:tn], ones_t[:], tt[:, 0:tn], start=(c == 0),
                                                       stop=(c == 7)), reads=[rtt, r_ones], writes=[rp])
        S.op("act", lambda e: e.activation(out=rstd[:, 0:tn], in_=pt[:, 0:tn], func=AF.Sqrt, bias=eps_t[:, 0:1],
                                           scale=1.0 / D), reads=[rp, r_eps], writes=[r_rstd])
        S.op("dve", lambda e: e.reciprocal(out=rstd[:, 0:tn], in_=rstd[:, 0:tn]), reads=[r_rstd], writes=[r_rstd])
        for c in range(8):
            S.op("dve", lambda e, c=c: e.scalar_tensor_tensor(
                out=r_(h_sb[:, c, 0:tn]), in0=src[:, c, 0:tn], scalar=vcol(l_gain_off + c), in1=rstd[:, 0:tn],
                op0=ALU.mult, op1=ALU.mult), reads=[r_src, r_rstd, r_vecs], writes=[r_h])

    def proj_fm(wt, rw, wcol0, rhs_t, r_rhs, rhs_chunk0, tn, kch=8, pt=None, rp=None, start=True, stop=True,
                kparts=128):
        if pt is None:
            pt, rp = mainps.next()
        for c in range(kch):
            S.op("pe", lambda e, c=c: e.matmul(
                pt[:, 0:tn], r_(wt[0:kparts, c, wcol0:wcol0 + 128]), r_(rhs_t[0:kparts, rhs_chunk0 + c, 0:tn]),
                start=(start and c == 0), stop=(stop and c == kch - 1)),
                reads=[rw, r_rhs], writes=[rp], inc=(c == kch - 1))
        return pt, rp

    def attention_unit(l, g, keysrc, r_keys_of_row, qrow0, qcol0, i0, nq, r, d, sample, have_prev=True):
        tiles = []
        if have_prev:
            tiles.append((qrow0 + r + d * (i0 - 128), 128, mprev))
        tiles.append((qrow0 + r + d * i0, nq, mcur))
        qs = qcol0 + r + d * i0
        qsl = slice(qs, qs + d * (nq - 1) + 1, d)
        nqm = nq + (nq % 2)
        qslm = slice(qs, qs + d * (nqm - 1) + 1, d)
        ptiles = []
        for (row0, nk, mk) in tiles:
            kt, rkt = kvt.next()
            rows = keysrc[row0:row0 + d * (nk - 1) + 1:d, :]
            S.dma("sp", kt[0:nk, :], rows, reads=r_keys_of_row(row0, row0 + d * (nk - 1)), writes=[rkt])
            if _CACHE.get("att") == "dma":
                continue
            tp, rtp = mainps.next()
            tp16 = tp[:, :].bitcast(BF16)
            for j in range(2):
                S.op("pe", lambda e, j=j: e.transpose(tp16[:, j * 128:j * 128 + 128], kt[:, j * 128:(j + 1) * 128],
                                                       ident[:, :]),
                     reads=[rkt, r_ident], writes=[rtp], inc=(j == 1))
            ktt, rktt = kT.next()
            tpv = tp16[:, 0:256].rearrange("p (j k) -> p j k", j=2)
            S.op("act", lambda e: e.activation(out=ktt[0:64, 0:4:2, :], in_=tpv[0:64, :, :], func=AF.Copy),
                 reads=[rtp], writes=[rktt])
            S.op("act", lambda e: e.activation(out=ktt[64:128, 1:4:2, :], in_=tpv[64:128, :, :], func=AF.Copy),
                 reads=[rtp], writes=[rktt])
            if _CACHE.get("att") == "tr":
                continue
            sp_, rsp = sps.next()
            for h in range(4):
                S.op("pe", lambda e, h=h: e.matmul(
                    sp_[:, h, 0:nqm], ktt[:, h, :], q_sb[:, 2 * g + h // 2, qslm],
                    start=True, stop=True), reads=[rktt, r_q], writes=[rsp], inc=(h == 3))
            if _CACHE.get("att") == "s":
                continue
            pp, rpp = p_sb.next()
            S.op("act", lambda e: e.activation(out=pp[0:nk, :, 0:nqm], in_=sp_[0:nk, :, 0:nqm], func=AF.Exp),
                 reads=[rsp], writes=[rpp])
            S.op("pool", lambda e, mk=mk: e.tensor_tensor(out=pp[:, :, 0:nqm], in0=pp[:, :, 0:nqm],
                                                           in1=mk[:, :, 0:nqm], op=ALU.mult),
                 reads=[rpp, r_mprev, r_mcur], writes=[rpp])
            ptiles.append((kt, rkt, pp, rpp, nk))
        def back():
            ntile = len(ptiles)
            if _CACHE.get("att") in ("dma", "tr", "s", "exp"):
                return
            for h in range(4):
                for part in range(2):
                    for ti, (kt, rkt, pp, rpp, nk) in enumerate(ptiles):
                        if part == 0:
                            lt = kt[:, 256 + 64 * h:256 + 64 * h + 128]
                        else:
                            lt = kt[:, 448:576]
                        S.op("pe", lambda e, h=h, lt=lt, pp=pp, ti=ti, part=part: e.matmul(
                            pvps[:, part, h, 0:nqm], lt, pp[:, h, 0:nqm],
                            start=(ti == 0), stop=(ti == ntile - 1)), reads=[rkt, rpp, r_ones], writes=[r_pvps],
                            inc=(h == 3 and part == 1 and ti == ntile - 1))
            if _CACHE.get("att") == "pv":
                return
            if g == 0:
                S.op("dve", lambda e: e.tensor_copy(out=acc[0:64, :, qsl], in_=pvps[0:64, 0, :, 0:nq]),
                     reads=[r_pvps], writes=[r_acc])
                S.op("dve", lambda e: e.tensor_copy(out=acc[64:128, :, qsl], in_=pvps[64:128, 1, :, 0:nq]),
                     reads=[r_pvps], writes=[r_acc])
            else:
                S.op("dve", lambda e: e.tensor_tensor(out=acc[0:64, :, qsl], in0=pvps[0:64, 0, :, 0:nq],
                                                      in1=acc[0:64, :, qsl], op=ALU.add),
                     reads=[r_pvps, r_acc], writes=[r_acc])
                S.op("dve", lambda e: e.tensor_tensor(out=acc[64:128, :, qsl], in0=pvps[64:128, 1, :, 0:nq],
                                                      in1=acc[64:128, :, qsl], op=ALU.add),
                     reads=[r_pvps, r_acc], writes=[r_acc])
        return back

    def run_tile(l, ti, kind):
        sample = kind == "sample"
        warm = kind == "warm"
        tn = TS if sample else (128 if warm else T)
        nseq, ls = (NSEQ, LS) if sample else (1, tn)
        tok0 = ti * T + (T - 128 if warm else 0)
        last_layer = l == 1
        Wl = w_in[l]
        if sample:
            if l == 0:
                S.dma("sp", x_sb[:, :, 0:tn], xsT.rearrange("(c p) t -> p c t", p=128), writes=[r_x])
            else:
                S.dma("sp", x_sb[:, :, 0:tn], xs1scr.rearrange("(c p) t -> p c t", p=128),
                      reads=[r_xs1], writes=[r_x])
        else:
            src = xT if l == 0 else x1scr
            rd = [] if l == 0 else [r_x1[ti]]
            S.dma("sp", x_sb[:, :, 0:tn], src[:, tok0:tok0 + tn].rearrange("(c p) t -> p c t", p=128),
                  reads=rd, writes=[r_x])
            if kind != "kv":
                S.dma("sp", vmask[:, 0:tn], validT[:, tok0:tok0 + tn], writes=[r_vmask])
        if _CACHE.get("stop") == "A":
            return
        rms(V_GATT + 8 * l, tn)
        if _CACHE.get("stop") == "B":
            return
        nblk = (tn + 127) // 128
        for g in ([] if warm else range(3)):
            Wg = GROUPS[g][0]
            wk, rwk = load_w("w_in", l, KOFF + 256 * g)
            wv, rwv = load_w("w_in", l, VOFF + 256 * g)
            for b in range(nblk):
                nb = min(128, tn - 128 * b)
                pt, rp = mainps.next()
                for (wt, rw, co) in ((wk, rwk, 0), (wv, rwv, 256)):
                    for c in range(8):
                        S.op("pe", lambda e, c=c, wt=wt, co=co: e.matmul(
                            pt[:, co:co + 256], h_sb[:, c, 128 * b:128 * b + 128], wt[:, c, 0:256],
                            start=(c == 0), stop=(c == 7)), reads=[r_h, rw], writes=[rp], inc=(c == 7))
                kr, rkr = kvrow.next()
                S.op("act", lambda e, kr=kr, pt=pt: e.activation(out=kr[0:nb, 0:512], in_=pt[0:nb, :], func=AF.Copy),
                     reads=[rp], writes=[rkr])
                orow = tok0 + 128 * b - (NS - Wg)
                need32 = sample or (ti >= 10 and orow >= 0)
                if _CACHE.get("cstop") == "mm":
                    continue
                if need32:
                    k32, rk32 = kvrow32.next()
                    S.op("act", lambda e, k32=k32, pt=pt: e.activation(out=k32[:, :], in_=pt[:, :], func=AF.Copy),
                         reads=[rp], writes=[rk32])
                if _CACHE.get("cstop") == "k32":
                    continue
                if sample:
                    S.op("pool", lambda e, kr=kr: e.memset(kr[0:nb, 512:576], 1.0), writes=[rkr])
                    for n in range(NSEQ if _CACHE.get("sdma") != "0" else 0):
                        S.dma("pool", ext[l][g][n, Wg:Wg + LS, :], kr[LS * n:LS * n + LS, :],
                              reads=[rkr], writes=[r_ext[l][g]], merge=True)
                        S.dma("pool", nks[g][l, n, Wg - LS:Wg, :], k32[LS * n:LS * n + LS, 0:256],
                              reads=[rk32], writes=[out_res], merge=True)
                        S.dma("pool", nvs[g][l, n, Wg - LS:Wg, :], k32[LS * n:LS * n + LS, 256:512],
                              reads=[rk32], writes=[out_res], merge=True)
                else:
                    blk = ti * 4 + b
                    S.op("pool", lambda e, kr=kr, blk=blk: e.tensor_scalar(
                        out=kr[0:nb, 512:576], in0=ones_t[0:nb, 0:64], scalar1=valtm[0:nb, blk:blk + 1],
                        scalar2=None, op0=ALU.mult), reads=[r_ones, r_valtm], writes=[rkr])
                    S.dma("pool", kvscr[l][g][tok0 + 128 * b:tok0 + 128 * b + nb, :], kr[0:nb, :],
                          reads=[rkr], writes=[r_kv[l][g][ti]], merge=True)
                    if need32:
                        S.dma("pool", nkp[g][l, orow:orow + nb, :], k32[0:nb, 0:256], reads=[rk32],
                              writes=[out_res], merge=True)
                        S.dma("pool", nvp[g][l, orow:orow + nb, :], k32[0:nb, 256:512], reads=[rk32],
                              writes=[out_res], merge=True)
        if kind == "kv":
            return
        if _CACHE.get("stop") == "C" and kind != "kv":
            return
        for e2 in range(3):
            wq, rwq = load_w("w_in", l, QOFF + 256 * e2)
            for k in range(2):
                e6 = 2 * e2 + k
                pt, rp = proj_fm(wq, rwq, 128 * k, h_sb, r_h, 0, tn)
                S.op("act", lambda e, e6=e6, pt=pt: e.activation(out=q_sb[:, e6, 0:tn], in_=pt[:, 0:tn],
                                                                 func=AF.Copy, scale=0.125), reads=[rp], writes=[r_q])
        if _CACHE.get("stop") == "D" and kind != "kv":
            return
        units = []
        for g in range(int(_CACHE.get("ng", 3))):
            Wg, d = GROUPS[g]
            for n in range(nseq):
                if sample:
                    keysrc = ext[l][g][n]
                    rk = lambda a_, b_, l=l, g=g: [r_ext[l][g]]
                    qrow0, qcol0 = Wg, LS * n
                else:
                    keysrc = kvscr[l][g]
                    rk = lambda a_, b_, l=l, g=g: r_kv[l][g][a_ // T:b_ // T + 1]
                    qrow0, qcol0 = tok0, 0
                for r in range(min(d, ls)):
                    nsub = (ls - r + d - 1) // d
                    for i0 in range(0, nsub, 128):
                        nq = min(128, nsub - i0)
                        units.append((g, keysrc, rk, qrow0, qcol0, i0, nq, r, d))
        uh = uhist[:, :, 0:nseq * 30].rearrange("p c (n t) -> p c n t", n=nseq)
        if sample:
            S.dma("sp", uh, sconv[l].rearrange("(c p) n t -> p c n t", p=128), writes=[r_uhist])
        elif warm:
            S.op("pool", lambda e: e.memset(uhist[:], 0.0), writes=[r_uhist])
        glu_w = {}

        def conv_chunk(i):
            i2, k = i // 2, i % 2
            if k == 0:
                glu_w["a"] = load_w("w_in", l, 256 * i2)
                glu_w["b"] = load_w("w_in", l, 1024 + 256 * i2)
            wa, rwa = glu_w["a"]
            wb, rwb = glu_w["b"]
            pa, rpa = proj_fm(wa, rwa, 128 * k, h_sb, r_h, 0, tn)
            pb, rpb = proj_fm(wb, rwb, 128 * k, h_sb, r_h, 0, tn)
            cdst = cbuf[:, i, 0:tn].rearrange("p (n t) -> p n t", n=nseq)
            cw = V_CW + (l * 8 + i) * 31
            S.op("act", lambda e, pb=pb: e.activation(out=tmpA[:, 0:tn], in_=pb[:, 0:tn], func=AF.Sigmoid),
                 reads=[rpb], writes=[r_tmpA])
            if sample:
                uwt, ruw = uws.next()
                ue = uwt[:, 0:nseq * (30 + ls)].rearrange("p (n t) -> p n t", n=nseq)
                S.op("pool", lambda e, ue=ue, i=i: e.tensor_copy(out=ue[:, :, 0:30], in_=uh[:, i, :, :]),
                     reads=[r_uhist], writes=[ruw])
                S.op("dve", lambda e, pa=pa, ue=ue: e.tensor_tensor(
                    out=ue[:, :, 30:30 + ls], in0=pa[:, 0:tn].rearrange("p (n t) -> p n t", n=nseq),
                    in1=tmpA[:, 0:tn].rearrange("p (n t) -> p n t", n=nseq), op=ALU.mult),
                    reads=[rpa, r_tmpA], writes=[ruw])
                S.op("pool", lambda e, ue=ue, i=i: e.tensor_copy(out=uh[:, i, :, :], in_=ue[:, :, ls:ls + 30]),
                     reads=[ruw], writes=[r_uhist])
                S.op("dve", lambda e, ue=ue, cdst=cdst, cw=cw, i=i: e.tensor_scalar(
                    out=cdst, in0=ue[:, :, 0:ls], scalar1=vcol(cw), scalar2=vcol(V_CB + 8 * l + i),
                    op0=ALU.mult, op1=ALU.add), reads=[ruw, r_vecs], writes=[r_cbuf])
                for j in range(1, 31):
                    S.op("dve", lambda e, ue=ue, j=j, cdst=cdst, cw=cw: e.scalar_tensor_tensor(
                        out=cdst, in0=ue[:, :, j:j + ls], scalar=vcol(cw + j), in1=cdst,
                        op0=ALU.mult, op1=ALU.add), reads=[ruw, r_cbuf, r_vecs], writes=[r_cbuf])
            else:
                uwt, ruw = uwork.next()
                S.op("pool", lambda e, uwt=uwt, i=i: e.tensor_copy(out=uwt[:, 0:30], in_=uhist[:, i, 0:30]),
                     reads=[r_uhist], writes=[ruw])
                S.op("dve", lambda e, pa=pa, uwt=uwt: e.tensor_tensor(out=uwt[:, 30:30 + tn], in0=pa[:, 0:tn],
                                                                      in1=tmpA[:, 0:tn], op=ALU.mult),
                     reads=[rpa, r_tmpA], writes=[ruw])
                S.op("dve", lambda e, pa=pa, i=i: e.tensor_tensor(out=uhist[:, i, 0:30], in0=pa[:, tn - 30:tn],
                                                                   in1=tmpA[:, tn - 30:tn], op=ALU.mult),
                     reads=[rpa, r_tmpA, ruw], writes=[r_uhist])
                dgt, rdg = dgbuf.next()
                S.dma("sp", dgt[:].rearrange("p j k -> p (j k)"), dgscr[l][i], reads=[r_dg], writes=[rdg])
                pc, rpc = mainps.next()
                for j in range(31):
                    S.op("pe", lambda e, j=j, dgt=dgt, uwt=uwt, pc=pc: e.matmul(
                        pc[:, 0:tn], dgt[:, j, :], uwt[:, j:j + tn], start=(j == 0), stop=(j == 30)),
                        reads=[rdg, ruw], writes=[rpc], inc=(j == 30))
                S.op("act", lambda e, pc=pc, i=i: e.activation(out=cbuf[:, i, 0:tn], in_=pc[:, 0:tn],
                                                               func=AF.Identity, bias=vcol(V_CB + 8 * l + i),
                                                               scale=1.0), reads=[rpc, r_vecs], writes=[r_cbuf])
        stride = max(1, (len(units) + 7) // 8)
        next_chunk = 0
        pend_back = None
        for ui, (g, keysrc, rk, qrow0, qcol0, i0, nq, r, d) in enumerate(units):
            bk = attention_unit(l, g, keysrc, rk, qrow0, qcol0, i0, nq, r, d, sample)
            if pend_back is not None:
                pend_back()
            pend_back = bk
            if ui % stride == stride - 1 and next_chunk < 8:
                conv_chunk(next_chunk)
                next_chunk += 1
        if pend_back is not None:
            pend_back()
        while next_chunk < 8:
            conv_chunk(next_chunk)
            next_chunk += 1
        S.op("dve", lambda e: e.tensor_scalar(out=acc[64:128, :, 0:tn], in0=acc[64:128, :, 0:tn], scalar1=1e-30,
                                              scalar2=None, op0=ALU.max), reads=[r_acc], writes=[r_acc])
        for h, (tt_, rt_) in enumerate(((tmpA, r_tmpA), (tmpB, r_tmpB), (rstd, r_rstd), (mean, r_mean))):
            S.op("dve", lambda e, h=h, tt_=tt_: e.reciprocal(out=tt_[0:64, 0:tn], in_=acc[64:128, h, 0:tn]),
                 reads=[r_acc], writes=[rt_])
            S.op("dve", lambda e, h=h, tt_=tt_: e.tensor_tensor(out=attn_sb[:, h, 0:tn], in0=acc[0:64, h, 0:tn],
                                                                in1=tt_[0:64, 0:tn], op=ALU.mult),
                 reads=[r_acc, rt_], writes=[r_attn])
        if sample:
            S.dma("pool", ncs[l].rearrange("(c p) n t -> p c n t", p=128), uh, reads=[r_uhist], writes=[out_res], merge=True)
        elif ti == NT - 1:
            S.dma("pool", ncp[l].rearrange("(c p) t -> p c t", p=128), uhist[:, :, 0:30],
                  reads=[r_uhist], writes=[out_res], merge=True)
        psum_sum, r_psum = mainps.next()
        for i in range(8):
            S.op("pe", lambda e, i=i: e.matmul(psum_sum[:, 0:tn], ones_t[:], cbuf[:, i, 0:tn],
                                                start=(i == 0), stop=(i == 7)),
                 reads=[r_cbuf, r_ones], writes=[r_psum], inc=(i == 7))
        psq, r_psq = mainps.next()
        for i in range(8):
            tt, rtt = (tmpA, r_tmpA) if i % 2 == 0 else (tmpB, r_tmpB)
            S.op("act", lambda e, i=i, tt=tt: e.activation(out=tt[:, 0:tn], in_=cbuf[:, i, 0:tn], func=AF.Square),
                 reads=[r_cbuf], writes=[rtt])
            S.op("pe", lambda e, i=i, tt=tt: e.matmul(psq[:, 0:tn], ones_t[:], tt[:, 0:tn],
                                                       start=(i == 0), stop=(i == 7)),
                 reads=[rtt, r_ones], writes=[r_psq])
        S.op("act", lambda e: e.activation(out=mean[:, 0:tn], in_=psum_sum[:, 0:tn], func=AF.Copy, scale=1.0 / D),
             reads=[r_psum], writes=[r_mean])
        S.op("dve", lambda e: e.tensor_tensor(out=tmpA[:, 0:tn], in0=mean[:, 0:tn], in1=mean[:, 0:tn], op=ALU.mult),
             reads=[r_mean], writes=[r_tmpA])
        S.op("dve", lambda e: e.scalar_tensor_tensor(out=tmpA[:, 0:tn], in0=psq[:, 0:tn], scalar=1.0 / D,
                                                     in1=tmpA[:, 0:tn], op0=ALU.mult, op1=ALU.subtract),
             reads=[r_psq, r_tmpA], writes=[r_tmpA])
        S.op("dve", lambda e: e.tensor_scalar(out=tmpA[:, 0:tn], in0=tmpA[:, 0:tn], scalar1=0.0, scalar2=None,
                                              op0=ALU.max), reads=[r_tmpA], writes=[r_tmpA])
        S.op("act", lambda e: e.activation(out=tmpA[:, 0:tn], in_=tmpA[:, 0:tn], func=AF.Sqrt, bias=eps_t[:, 0:1],
                                           scale=1.0), reads=[r_tmpA, r_eps], writes=[r_tmpA])
        S.op("dve", lambda e: e.reciprocal(out=tmpA[:, 0:tn], in_=tmpA[:, 0:tn]), reads=[r_tmpA], writes=[r_tmpA])
        for i in range(8):
            S.op("dve", lambda e, i=i: e.tensor_tensor(out=cbuf[:, i, 0:tn], in0=cbuf[:, i, 0:tn],
                                                       in1=mean[:, 0:tn], op=ALU.subtract),
                 reads=[r_cbuf, r_mean], writes=[r_cbuf])
            S.op("pool", lambda e, i=i: e.tensor_tensor(out=cbuf[:, i, 0:tn], in0=cbuf[:, i, 0:tn],
                                                        in1=tmpA[:, 0:tn], op=ALU.mult),
                 reads=[r_cbuf, r_tmpA], writes=[r_cbuf])
            S.op("act", lambda e, i=i: e.activation(out=r_(bufR[:, i, 0:tn]), in_=cbuf[:, i, 0:tn], func=AF.Silu,
                                                    bias=vcol(V_LNB + 8 * l + i), scale=vcol(V_LNG + 8 * l + i)),
                 reads=[r_cbuf, r_vecs], writes=[r_b0])
        if _CACHE.get("stop") == "G" and kind != "kv":
            return
        for j2 in range(4):
            if j2 % 2 == 0:
                S.dma("sp", wao[:], wbf["w_attn_out"][l][:, 256 * j2:256 * j2 + 512].rearrange(
                    "(h p) e -> p h e", p=64), reads=[r_wb["w_attn_out"][l]], writes=[r_wao])
            wgc, rwgc = load_w("w_in", l, GCOFF + 256 * j2)
            wga, rwga = load_w("w_in", l, GAOFF + 256 * j2)
            wco, rwco = load_w("w_conv_out", l, 256 * j2)
            for k in range(2):
                j = 2 * j2 + k
                pgc, rpgc = proj_fm(wgc, rwgc, 128 * k, h_sb, r_h, 0, tn)
                S.op("act", lambda e, pgc=pgc: e.activation(out=tmpA[:, 0:tn], in_=pgc[:, 0:tn], func=AF.Sigmoid),
                     reads=[rpgc], writes=[r_tmpA])
                pga, rpga = proj_fm(wga, rwga, 128 * k, h_sb, r_h, 0, tn)
                S.op("act", lambda e, pga=pga: e.activation(out=tmpB[:, 0:tn], in_=pga[:, 0:tn], func=AF.Sigmoid),
                     reads=[rpga], writes=[r_tmpB])
                pbc, rpbc = proj_fm(wco, rwco, 128 * k, bufR, r_b0, 0, tn)
                S.op("dve", lambda e, pbc=pbc: e.tensor_tensor(out=tmpA[:, 0:tn], in0=pbc[:, 0:tn], in1=tmpA[:, 0:tn],
                                                               op=ALU.mult), reads=[rpbc, r_tmpA], writes=[r_tmpA])
                pba, rpba = mainps.next()
                wc0 = 128 * (j % 4)
                for h in range(4):
                    S.op("pe", lambda e, h=h, wc0=wc0, pba=pba: e.matmul(
                        pba[:, 0:tn], r_(wao[:, h, wc0:wc0 + 128]), r_(attn_sb[:, h, 0:tn]),
                        start=(h == 0), stop=(h == 3)), reads=[r_wao, r_attn], writes=[rpba], inc=(h == 3))
                S.op("dve", lambda e, pba=pba: e.tensor_tensor(out=tmpB[:, 0:tn], in0=pba[:, 0:tn], in1=tmpB[:, 0:tn],
                                                               op=ALU.mult), reads=[rpba, r_tmpB], writes=[r_tmpB])
                S.op("pool", lambda e, j=j: e.tensor_tensor(out=r_(bufR[:, 8 + j, 0:tn]), in0=tmpA[:, 0:tn],
                                                            in1=tmpB[:, 0:tn], op=ALU.add),
                     reads=[r_tmpA, r_tmpB], writes=[r_b1])
        if _CACHE.get("stop") == "H" and kind != "kv":
            return
        for j2 in range(4):
            wo, rwo = load_w("w_out", l, 256 * j2)
            for k in range(2):
                j = 2 * j2 + k
                po, rpo = proj_fm(wo, rwo, 128 * k, bufR, r_b1, 8, tn)
                S.op("dve", lambda e, j=j, po=po: e.tensor_tensor(out=x_sb[:, j, 0:tn], in0=po[:, 0:tn],
                                                                  in1=x_sb[:, j, 0:tn], op=ALU.add),
                     reads=[rpo, r_x], writes=[r_x])
                if not sample:
                    S.op("pool", lambda e, j=j: e.tensor_tensor(out=x_sb[:, j, 0:tn], in0=x_sb[:, j, 0:tn],
                                                                in1=vmask[:, 0:tn], op=ALU.mult),
                         reads=[r_x, r_vmask], writes=[r_x])
        if _CACHE.get("stop") == "I" and kind != "kv":
            return
        rms(V_GFFN + 8 * l, tn)
        fh = fhist[:, :, 0:nseq * 2].rearrange("p c (n t) -> p c n t", n=nseq)
        if sample:
            S.dma("sp", fh, sffn[l].rearrange("(c p) n t -> p c n t", p=128), writes=r_fh)
        elif warm:
            S.op("pool", lambda e: e.memset(fhist[:], 0.0), writes=r_fh)
        Wu = w_up[l]
        Wd = w_down[l]
        ffn_pend = [None]
        for ip in range(3):
            for i2 in range(4):
                wv_, rwv_ = load_w("w_up", l, 1024 * ip + 256 * i2)
                wg_, rwg_ = load_w("w_up", l, DFF + 1024 * ip + 256 * i2)
                for k in range(2):
                    il = 2 * i2 + k
                    i = 8 * ip + il
                    res_c = []
                    for (wt, rw, idx) in ((wv_, rwv_, i), (wg_, rwg_, 24 + i)):
                        pu, rpu = proj_fm(wt, rw, 128 * k, h_sb, r_h, 0, tn)
                        uxt, ruxt = upext.next()
                        ux = uxt[:, 0:nseq * (2 + ls)].rearrange("p (n t) -> p n t", n=nseq)
                        S.op("pool", lambda e, ux=ux, idx=idx: e.tensor_copy(out=ux[:, :, 0:2], in_=fh[:, idx, :, :]),
                             reads=[r_fh[idx]], writes=[ruxt])
                        S.op("act", lambda e, ux=ux, pu=pu: e.activation(
                            out=ux[:, :, 2:2 + ls], in_=pu[:, 0:tn].rearrange("p (n t) -> p n t", n=nseq),
                            func=AF.Copy), reads=[rpu], writes=[ruxt])
                        S.op("pool", lambda e, ux=ux, idx=idx: e.tensor_copy(out=fh[:, idx, :, :],
                                                                             in_=ux[:, :, ls:ls + 2]),
                             reads=[ruxt], writes=[r_fh[idx]])
                        uct, ruc = upc.next()
                        uc = uct[:, 0:tn].rearrange("p (n t) -> p n t", n=nseq)
                        fw = V_FW + (l * 48 + idx) * 3
                        S.op("act", lambda e, ux=ux, uc=uc, fw=fw, idx=idx: e.activation(
                            out=uc, in_=ux[:, :, 0:ls], func=AF.Identity, scale=vcol(fw),
                            bias=vcol(V_FB + 48 * l + idx)), reads=[ruxt, r_vecs], writes=[ruc])
                        for j in (1, 2):
                            S.op("dve", lambda e, ux=ux, uc=uc, fw=fw, j=j: e.scalar_tensor_tensor(
                                out=uc, in0=ux[:, :, j:j + ls], scalar=vcol(fw + j), in1=uc,
                                op0=ALU.mult, op1=ALU.add), reads=[ruxt, ruc, r_vecs], writes=[ruc])
                        res_c.append((uct, ruc))
                    (vt, rv), (gt, rg) = res_c

                    def pair_back(vt=vt, rv=rv, gt=gt, rg=rg, il=il):
                        S.op("act", lambda e: e.activation(out=gt[:, 0:tn], in_=gt[:, 0:tn], func=AF.Silu),
                             reads=[rg], writes=[rg])
                        S.op("dve", lambda e: e.tensor_tensor(
                            out=r_(bufR[:, il, 0:tn]), in0=vt[:, 0:tn], in1=gt[:, 0:tn], op=ALU.mult),
                            reads=[rv, rg], writes=[r_b0])

                    if ffn_pend[0] is not None:
                        ffn_pend[0]()
                    ffn_pend[0] = pair_back
            if ffn_pend[0] is not None:
                ffn_pend[0]()
                ffn_pend[0] = None
            for j2 in range(4):
                wd_, rwd_ = load_w("w_down", l, 256 * j2, r0=1024 * ip)
                for k in range(2):
                    j = 2 * j2 + k
                    pd, rpd = proj_fm(wd_, rwd_, 128 * k, bufR, r_b0, 0, tn)
                    S.op("dve", lambda e, j=j, pd=pd: e.tensor_tensor(out=x_sb[:, j, 0:tn], in0=pd[:, 0:tn],
                                                                      in1=x_sb[:, j, 0:tn], op=ALU.add),
                         reads=[rpd, r_x], writes=[r_x])
                    if ip == 2 and not sample:
                        S.op("pool", lambda e, j=j: e.tensor_tensor(out=x_sb[:, j, 0:tn], in0=x_sb[:, j, 0:tn],
                                                                    in1=vmask[:, 0:tn], op=ALU.mult),
                             reads=[r_x, r_vmask], writes=[r_x])
        if sample:
            S.dma("pool", nfs[l].rearrange("(c p) n t -> p c n t", p=128), fh, reads=r_fh, writes=[out_res], merge=True)
        elif ti == NT - 1:
            S.dma("pool", nfp[l].rearrange("(c p) t -> p c t", p=128), fhist[:, :, 0:2],
                  reads=r_fh, writes=[out_res], merge=True)
        if _CACHE.get("stop") == "J" and kind != "kv":
            return
        if warm:
            return
        if not last_layer:
            if sample:
                S.dma("pool", xs1scr.rearrange("(c p) t -> p c t", p=128), x_sb[:, :, 0:tn],
                      reads=[r_x], writes=[r_xs1])
            else:
                S.dma("pool", x1scr[:, tok0:tok0 + T].rearrange("(c p) t -> p c t", p=128), x_sb[:, :, :],
                      reads=[r_x], writes=[r_x1[ti]])
        else:
            if sample or tok0 >= OWN0:
                rms(V_GFIN, tn)
                if sample:
                    S.dma("pool", ysT.rearrange("(c p) t -> p c t", p=128), h_sb[:, :, 0:tn],
                          reads=[r_h], writes=[out_res], merge=True)
                else:
                    S.dma("pool", yT[:, tok0 - OWN0:tok0 - OWN0 + T].rearrange("(c p) t -> p c t", p=128),
                          h_sb[:, :, :], reads=[r_h], writes=[out_res], merge=True)

    xs1scr = dscr("xs1scr", [D, TS])
    r_xs1 = Res("xs1")

    ntile_done = 0
    for l in range(2):
        first_full = 4 if l == 0 else 9
        first_kv = 0 if l == 0 else 5
        sched = ([(ti, "kv") for ti in range(first_kv, first_full + 1)] + [(first_full, "warm")]
                 + [(ti, "full") for ti in range(first_full + 1, NT)] + [(0, "sample")])
        for (ti, kind) in sched:
            if max_tiles is not None and ntile_done >= max_tiles:
                continue
            if _CACHE.get("only") == "sample" and kind != "sample":
                ntile_done += 1
                continue
            run_tile(l, ti, kind)
            ntile_done += 1
            if dbg_tile == (l, ti, kind):
                tn_ = TS if kind == "sample" else T
                S.dma("pool", dbgx[:, 0:tn_].rearrange("(c p) t -> p c t", p=128), x_sb[:, :, 0:tn_],
                      reads=[r_x], writes=[out_res], merge=True)
    S.finish("pool", [out_res])
    S.finish("sp", [out_res])
    return nc, S


_CACHE = {}


def _consts():
    jj = np.arange(128)[:, None]
    ii = np.arange(128)[None, :]
    mprev = np.tile((jj >= ii).astype(np.float32), (1, 4))
    mcur = np.tile((jj <= ii).astype(np.float32), (1, 4))
    return mprev, mcur


def _pack_vecs(norm_attn_g, norm_ffn_g, norm_final_g, conv_dw_b, conv_ln_g, conv_ln_b, conv_dw_w, ffn_dw_b, ffn_dw_w):
    v = np.zeros((128, NV), np.float32)

    def pc(a):
        a = np.asarray(a, np.float32)
        c = a.shape[-1] // 128
        a = a.reshape(a.shape[:-1] + (c, 128))
        return np.moveaxis(a, -1, 0)

    v[:, V_GATT:V_GATT + 16] = pc(norm_attn_g).reshape(128, 16)
    v[:, V_GFFN:V_GFFN + 16] = pc(norm_ffn_g).reshape(128, 16)
    v[:, V_GFIN:V_GFIN + 8] = pc(norm_final_g).reshape(128, 8)
    v[:, V_CB:V_CB + 16] = pc(conv_dw_b).reshape(128, 16)
    v[:, V_LNG:V_LNG + 16] = pc(conv_ln_g).reshape(128, 16)
    v[:, V_LNB:V_LNB + 16] = pc(conv_ln_b).reshape(128, 16)
    cw = pc(conv_dw_w)
    v[:, V_CW:V_CW + 2 * 8 * 31] = np.transpose(cw, (0, 1, 3, 2)).reshape(128, -1)
    v[:, V_FB:V_FB + 96] = pc(ffn_dw_b).reshape(128, 96)
    fw = pc(ffn_dw_w)
    v[:, V_FW:V_FW + 2 * 48 * 3] = np.transpose(fw, (0, 1, 3, 2)).reshape(128, -1)
    return v


def kernel(x_prompt, x_sample, state_conv, cache_k_w128, cache_v_w128, cache_k_w512, cache_v_w512,
           cache_k_w2048, cache_v_w2048, state_ffn_conv, norm_attn_g, w_in, conv_dw_w, conv_dw_b,
           conv_ln_g, conv_ln_b, w_conv_out, w_attn_out, w_out, norm_ffn_g, w_up, ffn_dw_w, ffn_dw_b,
           w_down, norm_final_g):
    f = lambda a: np.ascontiguousarray(np.asarray(a, dtype=np.float32))
    if "nc" not in _CACHE:
        _CACHE["nc"] = build_program()
    nc, S = _CACHE["nc"]
    mprev, mcur = _consts()
    vecs = _pack_vecs(norm_attn_g, norm_ffn_g, norm_final_g, conv_dw_b, conv_ln_g, conv_ln_b, conv_dw_w,
                      ffn_dw_b, ffn_dw_w)
    xp = f(x_prompt)[0]
    xpT_pad = np.zeros((D, OWN0 + SEQ), np.float32)
    xpT_pad[:, OWN0:] = xp.T
    ck = [f(cache_k_w128), f(cache_k_w512), f(cache_k_w2048)]
    cv = [f(cache_v_w128), f(cache_v_w512), f(cache_v_w2048)]
    shared = dict(w_in=f(w_in), w_conv_out=f(w_conv_out), w_attn_out=f(w_attn_out), w_out=f(w_out), w_up=f(w_up),
                  w_down=f(w_down), vecs=vecs, ident=np.eye(128, dtype=np.float32), mprev=mprev, mcur=mcur,
                  )
    xs = f(x_sample)
    sc = f(state_conv)
    sf = f(state_ffn_conv)
    in_maps = []
    for c in range(NCORES):
        s = c * OWN
        m = dict(shared)
        m["xT"] = np.ascontiguousarray(xpT_pad[:, s:s + NS])
        val = (np.arange(NS) + s - OWN0 >= 0).astype(np.float32)
        m["validT"] = np.ascontiguousarray(np.broadcast_to(val[None, :], (128, NS)))
        m["validtm"] = np.ascontiguousarray(val.reshape(NS // 128, 128).T)
        n0 = c * NSEQ
        m["xsT"] = np.ascontiguousarray(xs[n0:n0 + NSEQ].reshape(TS, D).T)
        m["sconv"] = np.ascontiguousarray(np.transpose(sc[:, n0:n0 + NSEQ], (0, 3, 1, 2)))
        m["sffn"] = np.ascontiguousarray(np.transpose(sf[:, n0:n0 + NSEQ], (0, 3, 1, 2)))
        for g in range(3):
            W = GROUPS[g][0]
            m["ck%d" % g] = np.ascontiguousarray(ck[g][:, n0:n0 + NSEQ].reshape(2, NSEQ, W, 256))
            m["cv%d" % g] = np.ascontiguousarray(cv[g][:, n0:n0 + NSEQ].reshape(2, NSEQ, W, 256))
        in_maps.append(m)
    if _CACHE.get("dbg_in_maps_only"):
        return in_maps
    res = run_bass_kernel_spmd(nc, in_maps, core_ids=list(range(NCORES)))
    R = res.results
    y_prompt = np.concatenate([R[c]["yT"].T for c in range(NCORES)], axis=0)[None]
    y_sample = np.concatenate([R[c]["ysT"].T.reshape(NSEQ, LS, D) for c in range(NCORES)], axis=0)
    last = R[NCORES - 1]
    conv_p = np.transpose(last["ncp"], (0, 2, 1))[:, None]
    conv_s = np.concatenate([np.transpose(R[c]["ncs"], (0, 2, 3, 1)) for c in range(NCORES)], axis=1)
    outs = [np.ascontiguousarray(y_prompt), np.ascontiguousarray(y_sample),
            np.ascontiguousarray(conv_p), np.ascontiguousarray(conv_s)]
    for g in range(3):
        W = GROUPS[g][0]
        outs.append(np.ascontiguousarray(last["nkp%d" % g].reshape(2, 1, W, 4, 64)))
        outs.append(np.ascontiguousarray(last["nvp%d" % g].reshape(2, 1, W, 4, 64)))
        outs.append(np.concatenate([R[c]["nks%d" % g].reshape(2, NSEQ, W, 4, 64) for c in range(NCORES)], axis=1))
        outs.append(np.concatenate([R[c]["nvs%d" % g].reshape(2, NSEQ, W, 4, 64) for c in range(NCORES)], axis=1))
    ffn_p = np.transpose(last["nfp"], (0, 2, 1))[:, None]
    ffn_s = np.concatenate([np.transpose(R[c]["nfs"], (0, 2, 3, 1)) for c in range(NCORES)], axis=1)
    outs.append(np.ascontiguousarray(ffn_p))
    outs.append(np.ascontiguousarray(ffn_s))
    return tuple(np.asarray(o, dtype=np.float32) for o in outs)
```

```python
import numpy as np
import concourse.bass as bass
import concourse.mybir as mybir
from concourse.bass_utils import run_bass_kernel_spmd

F32 = mybir.dt.float32
F32R = mybir.dt.float32r
BF16 = mybir.dt.bfloat16
AF = mybir.ActivationFunctionType
ALU = mybir.AluOpType

NCORES = 8
D = 1024
DIN = 6400
DFF = 3072
SEQ = 16384
OWN = SEQ // NCORES
T = 512
NT = 14
NS = NT * T
OWN0 = NS - OWN
NSEQ = 4
LS = 8
TS = NSEQ * LS
GROUPS = ((128, 1), (512, 4), (2048, 16))
EPS = 1e-6
KVW = 576
QOFF, KOFF, VOFF, GCOFF, GAOFF = 2048, 2816, 3584, 4352, 5376

V_GATT = 0
V_GFFN = 16
V_GFIN = 32
V_CB = 40
V_LNG = 56
V_LNB = 72
V_CW = 88
V_FB = V_CW + 2 * 8 * 31
V_FW = V_FB + 96
NV = V_FW + 2 * 48 * 3


class Res:
    __slots__ = ("name", "w", "r")

    def __init__(self, name):
        self.name = name
        self.w = {}
        self.r = {}


class Sched:
    def __init__(self, nc, n_dma_sems=48):
        self.nc = nc
        self.eng = {}
        self.sems = {}
        for name, h in (("pe", nc.tensor), ("act", nc.scalar), ("dve", nc.vector),
                        ("pool", nc.gpsimd), ("sp", nc.sync)):
            self.eng[name] = dict(h=h, sem=nc.alloc_semaphore("s_" + name), cnt=0, seen={})
            self.sems[name] = self.eng[name]["sem"]
        self.dma_sems = []
        for i in range(n_dma_sems):
            k = "d%d" % i
            self.sems[k] = nc.alloc_semaphore("s_" + k)
            self.dma_sems.append(dict(key=k, val=0))
        self.dma_rr = 0
        self.n_inst = 0
        self.n_wait = 0

    def _need(self, reads, writes, skip_key=None):
        need = {}
        for r in reads:
            for k, v in r.w.items():
                if need.get(k, 0) < v:
                    need[k] = v
        for r in writes:
            for k, v in r.w.items():
                if need.get(k, 0) < v:
                    need[k] = v
            for k, v in r.r.items():
                if need.get(k, 0) < v:
                    need[k] = v
        if skip_key is not None:
            need.pop(skip_key, None)
        return need

    def _waits(self, ename, need):
        e = self.eng[ename]
        for k, v in need.items():
            if e["seen"].get(k, 0) >= v:
                continue
            e["h"].wait_ge(self.sems[k], v)
            e["seen"][k] = v
            self.n_wait += 1

    def _commit(self, reads, writes, key, val):
        for r in writes:
            r.w = {key: val}
            r.r = {}
        for r in reads:
            if r.r.get(key, 0) < val:
                r.r[key] = val

    def op(self, ename, fn, reads=(), writes=(), inc=True):
        e = self.eng[ename]
        need = self._need(reads, writes, skip_key=("pe" if ename == "pe" else None))
        self._waits(ename, need)
        ins = fn(e["h"])
        if inc:
            e["cnt"] += 1
            ins.then_inc(e["sem"], 1)
            self._commit(reads, writes, ename, e["cnt"])
        else:
            self._commit(reads, writes, ename, e["cnt"] + 1)
        self.n_inst += 1
        return ins

    def dma(self, qname, out, in_, reads=(), writes=(), merge=False):
        e = self.eng[qname]
        d = self.dma_sems[self.dma_rr]
        self.dma_rr = (self.dma_rr + 1) % len(self.dma_sems)
        if merge:
            need = self._need(reads, ())
            for r in writes:
                for k, v in r.r.items():
                    if need.get(k, 0) < v:
                        need[k] = v
        else:
            need = self._need(reads, writes)
        if d["val"] > 0 and need.get(d["key"], 0) < d["val"]:
            need[d["key"]] = d["val"]
        self._waits(qname, need)
        ins = e["h"].dma_start(out=out, in_=in_)
        d["val"] += 16
        ins.then_inc(self.sems[d["key"]], 16)
        if merge:
            for r in writes:
                r.w[d["key"]] = d["val"]
            self._commit(reads, (), d["key"], d["val"])
        else:
            self._commit(reads, writes, d["key"], d["val"])
        self.n_inst += 1
        return ins

    def finish(self, ename, resources):
        need = self._need(resources, resources)
        self._waits(ename, need)


class Rot:
    def __init__(self, items):
        self.items = items
        self.i = 0

    def next(self):
        it = self.items[self.i]
        self.i = (self.i + 1) % len(self.items)
        return it


def r_(ap):
    return ap


def build_program(max_tiles=None, skip_init=False, skip_out=False, dbg_tile=None):
    nc = bass.Bass("TRN2", target_bir_lowering=False)
    nc.dge_precook = False
    S = Sched(nc)

    def din(name, shape):
        return nc.dram_tensor(name, list(shape), F32, kind="ExternalInput").ap()

    def dout(name, shape):
        return nc.dram_tensor(name, list(shape), F32, kind="ExternalOutput").ap()

    def dscr(name, shape, dt=F32):
        return nc.dram_tensor(name, list(shape), dt, kind="Internal").ap()

    xT = din("xT", [D, NS])
    validT = din("validT", [128, NS])
    validtm_d = din("validtm", [128, NS // 128])
    xsT = din("xsT", [D, TS])
    sconv = din("sconv", [2, D, NSEQ, 30])
    sffn = din("sffn", [2, 2 * DFF, NSEQ, 2])
    ck = [din("ck%d" % g, [2, NSEQ, GROUPS[g][0], 256]) for g in range(3)]
    cv = [din("cv%d" % g, [2, NSEQ, GROUPS[g][0], 256]) for g in range(3)]
    w_in = din("w_in", [2, D, DIN])
    w_conv_out = din("w_conv_out", [2, D, D])
    w_attn_out = din("w_attn_out", [2, 256, D])
    w_out = din("w_out", [2, D, D])
    w_up = din("w_up", [2, D, 2 * DFF])
    w_down = din("w_down", [2, DFF, D])
    vecs_d = din("vecs", [128, NV])
    ident_d = din("ident", [128, 128])
    mprev_d = din("mprev", [128, 4 * 128])
    mcur_d = din("mcur", [128, 4 * 128])

    yT = dout("yT", [D, OWN])
    ysT = dout("ysT", [D, TS])
    ncp = dout("ncp", [2, D, 30])
    ncs = dout("ncs", [2, D, NSEQ, 30])
    nkp = [dout("nkp%d" % g, [2, GROUPS[g][0], 256]) for g in range(3)]
    nvp = [dout("nvp%d" % g, [2, GROUPS[g][0], 256]) for g in range(3)]
    nks = [dout("nks%d" % g, [2, NSEQ, GROUPS[g][0], 256]) for g in range(3)]
    nvs = [dout("nvs%d" % g, [2, NSEQ, GROUPS[g][0], 256]) for g in range(3)]
    nfp = dout("nfp", [2, 2 * DFF, 2])
    nfs = dout("nfs", [2, 2 * DFF, NSEQ, 2])
    out_res = Res("outputs")
    if dbg_tile is not None:
        dbgx = dout("dbgx", [D, T])

    x1scr = dscr("x1scr", [D, NS])
    r_x1 = [Res("x1_%d" % i) for i in range(NT)]
    kvscr = [[dscr("kv_%d_%d" % (l, g), [NS, KVW], BF16) for g in range(3)] for l in range(2)]
    r_kv = [[[Res("kv%d%d_%d" % (l, g, i)) for i in range(NT)] for g in range(3)] for l in range(2)]
    ext = [[dscr("ext_%d_%d" % (l, g), [NSEQ, GROUPS[g][0] + LS, KVW], BF16) for g in range(3)] for l in range(2)]
    r_ext = [[Res("ext%d%d" % (l, g)) for g in range(3)] for l in range(2)]
    WSH = dict(w_in=(D, DIN), w_conv_out=(D, D), w_attn_out=(256, D), w_out=(D, D), w_up=(D, 2 * DFF), w_down=(DFF, D))
    wsrc = dict(w_in=w_in, w_conv_out=w_conv_out, w_attn_out=w_attn_out, w_out=w_out, w_up=w_up, w_down=w_down)
    wbf = {k: [dscr("wb_%s_%d" % (k, l), list(v), BF16) for l in range(2)] for k, v in WSH.items()
           if k == "w_attn_out"}
    wbt = {k: [dscr("wt_%s_%d" % (k, l), [(v[0] // 1024) * (v[1] // 256), 128, 8 * 256], BF16) for l in range(2)]
           for k, v in WSH.items() if k != "w_attn_out"}
    r_wb = {k: [Res("wb_%s_%d" % (k, l)) for l in range(2)] for k in WSH}
    dgscr = [[dscr("dg_%d_%d" % (l, i), [128, 31 * 128], BF16) for i in range(8)] for l in range(2)]
    r_dg = Res("dgscr")

    def sb(name, shape, dt=F32):
        return nc.alloc_sbuf_tensor("sb_" + name, list(shape), dt), Res(name)

    vecs, r_vecs = sb("vecs", [128, NV])
    ident32, r_ident32 = sb("ident32", [128, 128])
    ident, r_ident = sb("ident", [128, 128], BF16)
    mprev, r_mprev = sb("mprev", [128, 4, 128])
    mcur, r_mcur = sb("mcur", [128, 4, 128])
    ones_t, r_ones = sb("ones_t", [128, 128])
    valtm, r_valtm = sb("valtm", [128, NS // 128])
    eps_t, r_eps = sb("eps_t", [128, 1])
    x_sb, r_x = sb("x_sb", [128, 8, T])
    h_sb, r_h = sb("h_sb", [128, 8, T], BF16)
    vmask, r_vmask = sb("vmask", [128, T])
    rstd, r_rstd = sb("rstd", [128, T])
    mean, r_mean = sb("mean", [128, T])
    tmpA, r_tmpA = sb("tmpA", [128, T])
    tmpB, r_tmpB = sb("tmpB", [128, T])
    cbuf, r_cbuf = sb("cbuf", [128, 8, T])
    q_sb, r_q = sb("q_sb", [128, 6, T], BF16)
    acc, r_acc = sb("acc", [128, 4, T])
    attn_sb, r_attn = sb("attn_sb", [64, 4, T], BF16)
    uhist, r_uhist = sb("uhist", [128, 8, NSEQ * 30])
    uwork = Rot([sb("uwork%d" % i, [128, 30 + T], BF16) for i in range(3)])
    uws = Rot([sb("uws%d" % i, [128, NSEQ * (30 + LS)]) for i in range(2)])
    dgbuf = Rot([sb("dg%d" % i, [128, 31, 128], BF16) for i in range(2)])
    bufR, r_bufR = sb("bufR", [128, 16, T], BF16)
    r_b0, r_b1 = Res("bufR0"), Res("bufR1")
    fhist, r_fhist = sb("fhist", [128, 48, NSEQ * 2])
    r_fh = [Res("fh%d" % i) for i in range(48)]
    upext = Rot([sb("upext%d" % i, [128, 2 + T]) for i in range(4)])
    upc = Rot([sb("upc%d" % i, [128, T]) for i in range(6)])
    wslot = Rot([sb("w%d" % i, [128, 8, 256], BF16) for i in range(6)])
    wstage = Rot([sb("wst%d" % i, [128, 8, 256]) for i in range(2)])
    wao, r_wao = sb("wao", [64, 4, 512], BF16)
    kvrow = Rot([sb("kvrow%d" % i, [128, KVW], BF16) for i in range(2)])
    kvrow32 = Rot([sb("kvrow32_%d" % i, [128, 512]) for i in range(2)])
    kvt = Rot([sb("kvt%d" % i, [128, KVW], BF16) for i in range(6)])
    kT = Rot([sb("kT%d" % i, [128, 4, 128], BF16) for i in range(4)])
    p_sb = Rot([sb("p%d" % i, [128, 4, 128], BF16) for i in range(6)])

    def ps(name, shape):
        return nc.alloc_psum_tensor("ps_" + name, list(shape), F32), Res(name)

    mainps = Rot([ps("mps%d" % i, [128, 512]) for i in range(4)])
    sps = Rot([ps("sps%d" % i, [128, 4, 128]) for i in range(2)])
    pvps, r_pvps = ps("pvps", [128, 2, 4, 128])

    S.dma("sp", vecs[:], vecs_d, writes=[r_vecs])
    S.dma("sp", ident32[:], ident_d, writes=[r_ident32])
    S.op("dve", lambda e: e.tensor_copy(out=ident[:], in_=ident32[:]), reads=[r_ident32], writes=[r_ident])
    S.dma("sp", mprev[:], mprev_d.rearrange("p (h q) -> p h q", h=4), writes=[r_mprev])
    S.dma("sp", mcur[:], mcur_d.rearrange("p (h q) -> p h q", h=4), writes=[r_mcur])
    S.op("dve", lambda e: e.memset(ones_t[:], 1.0), writes=[r_ones])
    S.op("dve", lambda e: e.memset(eps_t[:], EPS), writes=[r_eps])
    for rot in (kvt, kT, p_sb):
        for (tt_, rr_) in rot.items:
            S.op("pool", lambda e, tt_=tt_: e.memset(tt_[:], 0.0), writes=[rr_])
    S.dma("sp", valtm[:], validtm_d, writes=[r_valtm])
    cast_rr = [0]

    def cast_op(out_ap, in_ap, reads, writes):
        k = cast_rr[0] % 3
        cast_rr[0] += 1
        if k == 0:
            S.op("act", lambda e: e.activation(out=out_ap, in_=in_ap, func=AF.Copy), reads=reads, writes=writes)
        elif k == 1:
            S.op("pool", lambda e: e.tensor_copy(out=out_ap, in_=in_ap), reads=reads, writes=writes)
        else:
            S.op("dve", lambda e: e.tensor_copy(out=out_ap, in_=in_ap), reads=reads, writes=writes)

    for l in range(2):
        for i in range(8):
            dgt, rdg = dgbuf.next()
            for j in range(31):
                en = "dve" if j % 2 == 0 else "pool"
                S.op(en, lambda e, dgt=dgt, j=j, l=l, i=i: e.tensor_scalar(
                    out=dgt[:, j, :], in0=ident32[:, :], scalar1=vecs[:, V_CW + (l * 8 + i) * 31 + j:V_CW + (l * 8 + i) * 31 + j + 1],
                    scalar2=None, op0=ALU.mult), reads=[r_ident32, r_vecs], writes=[rdg])
            S.dma("pool", dgscr[l][i], dgt[:].rearrange("p j k -> p (j k)"), reads=[rdg], writes=[r_dg], merge=True)
    for l in range(2):
        for name in ("w_in", "w_conv_out", "w_attn_out", "w_out", "w_up", "w_down"):
            R_, C_ = WSH[name]
            src, rdst = wsrc[name][l], r_wb[name][l]
            for r0 in range(0, R_, 1024):
                kch = min(1024, R_ - r0) // 128
                for c0 in range(0, C_, 256):
                    st, rst = wstage.next()
                    S.dma("sp", st[:, 0:kch, :], src[r0:r0 + 128 * kch, c0:c0 + 256].rearrange(
                        "(c p) e -> p c e", p=128), writes=[rst])
                    wt, rw = wslot.next()
                    cast_op(wt[:, 0:kch, :], st[:, 0:kch, :], [rst], [rw])
                    if name == "w_attn_out":
                        S.dma("pool", wbf[name][l][r0:r0 + 128 * kch, c0:c0 + 256].rearrange(
                            "(c p) e -> p c e", p=128), wt[:, 0:kch, :], reads=[rw], writes=[rdst], merge=True)
                    else:
                        bidx = (r0 // 1024) * (C_ // 256) + c0 // 256
                        S.dma("pool", wbt[name][l][bidx].rearrange("p (c e) -> p c e", c=8), wt[:, 0:8, :],
                              reads=[rw], writes=[rdst], merge=True)
    for l in ([] if skip_init else range(2)):
        for g in range(3):
            W = GROUPS[g][0]
            for n in range(NSEQ):
                for r0 in range(0, W, 128):
                    st, rst = wstage.next()
                    stv = st[:, 0:2, :]
                    S.dma("sp", stv[:, 0, :], ck[g][l, n, r0:r0 + 128, :], writes=[rst])
                    S.dma("sp", stv[:, 1, :], cv[g][l, n, r0:r0 + 128, :], writes=[rst], merge=True)
                    kr, rkr = kvrow.next()
                    cast_op(kr[:, 0:512], stv.rearrange("p a b -> p (a b)"), [rst], [rkr])
                    S.op("pool", lambda e, kr=kr: e.memset(kr[:, 512:576], 1.0), writes=[rkr])
                    S.dma("pool", ext[l][g][n, r0:r0 + 128, :], kr[:, :], reads=[rkr], writes=[r_ext[l][g]],
                          merge=True)
                for r0 in range(LS, W, 512):
                    r1 = min(W, r0 + 512)
                    S.dma("pool", nks[g][l, n, r0 - LS:r1 - LS, :], ck[g][l, n, r0:r1, :], writes=[out_res],
                          merge=True)
                    S.dma("pool", nvs[g][l, n, r0 - LS:r1 - LS, :], cv[g][l, n, r0:r1, :], writes=[out_res],
                          merge=True)

    for (st, rst) in wstage.items:
        v16 = st[:, :, :].bitcast(BF16)
        for half in range(2):
            rr = Res("wx")
            rr.w = dict(rst.w)
            rr.r = dict(rst.r)
            wslot.items.append((v16[:, :, 256 * half:256 * half + 256], rr))

    def vcol(off, n=1):
        return vecs[:, off:off + n]

    def load_w(name, l, c0, r0=0, ncols=256, kch=8):
        wt, rw = wslot.next()
        bidx = (r0 // 1024) * (WSH[name][1] // 256) + c0 // 256
        S.dma("sp", wt[:, 0:8, 0:256], wbt[name][l][bidx].rearrange("p (c e) -> p c e", c=8),
              reads=[r_wb[name][l]], writes=[rw])
        return wt, rw

    def rms(l_gain_off, tn, src=x_sb, r_src=None):
        r_src = r_src or r_x
        pt, rp = mainps.next()
        for c in range(8):
            tt, rtt = (tmpA, r_tmpA) if c % 2 == 0 else (tmpB, r_tmpB)
            S.op("act", lambda e, c=c, tt=tt: e.activation(out=tt[:, 0:tn], in_=src[:, c, 0:tn], func=AF.Square),
                 reads=[r_src], writes=[rtt])
            S.op("pe", lambda e, c=c, tt=tt: e.matmul(pt[:, 0:tn], ones_t[:], tt[:, 0:tn], start=(c == 0),
                                                       stop=(c == 7)), reads=[rtt, r_ones], writes=[rp])
        S.op("act", lambda e: e.activation(out=rstd[:, 0:tn], in_=pt[:, 0:tn], func=AF.Sqrt, bias=eps_t[:, 0:1],
                                           scale=1.0 / D), reads=[rp, r_eps], writes=[r_rstd])
        S.op("dve", lambda e: e.reciprocal(out=rstd[:, 0:tn], in_=rstd[:, 0:tn]), reads=[r_rstd], writes=[r_rstd])
        for c in range(8):
            S.op("dve", lambda e, c=c: e.scalar_tensor_tensor(
                out=r_(h_sb[:, c, 0:tn]), in0=src[:, c, 0:tn], scalar=vcol(l_gain_off + c), in1=rstd[:, 0:tn],
                op0=ALU.mult, op1=ALU.mult), reads=[r_src, r_rstd, r_vecs], writes=[r_h])

    def proj_fm(wt, rw, wcol0, rhs_t, r_rhs, rhs_chunk0, tn, kch=8, pt=None, rp=None, start=True, stop=True,
                kparts=128):
        if pt is None:
            pt, rp = mainps.next()
        for c in range(kch):
            S.op("pe", lambda e, c=c: e.matmul(
                pt[:, 0:tn], r_(wt[0:kparts, c, wcol0:wcol0 + 128]), r_(rhs_t[0:kparts, rhs_chunk0 + c, 0:tn]),
                start=(start and c == 0), stop=(stop and c == kch - 1)),
                reads=[rw, r_rhs], writes=[rp], inc=(c == kch - 1))
        return pt, rp

    def attention_unit(l, g, keysrc, r_keys_of_row, qrow0, qcol0, i0, nq, r, d, sample, have_prev=True):
        tiles = []
        if have_prev:
            tiles.append((qrow0 + r + d * (i0 - 128), 128, mprev))
        tiles.append((qrow0 + r + d * i0, nq, mcur))
        qs = qcol0 + r + d * i0
        qsl = slice(qs, qs + d * (nq - 1) + 1, d)
        nqm = nq + (nq % 2)
        qslm = slice(qs, qs + d * (nqm - 1) + 1, d)
        ptiles = []
        for (row0, nk, mk) in tiles:
            kt, rkt = kvt.next()
            rows = keysrc[row0:row0 + d * (nk - 1) + 1:d, :]
            S.dma("sp", kt[0:nk, :], rows, reads=r_keys_of_row(row0, row0 + d * (nk - 1)), writes=[rkt])
            if _CACHE.get("att") == "dma":
                continue
            tp, rtp = mainps.next()
            tp16 = tp[:, :].bitcast(BF16)
            for j in range(2):
                S.op("pe", lambda e, j=j: e.transpose(tp16[:, j * 128:j * 128 + 128], kt[:, j * 128:(j + 1) * 128],
                                                       ident[:, :]),
                     reads=[rkt, r_ident], writes=[rtp], inc=(j == 1))
            ktt, rktt = kT.next()
            tpv = tp16[:, 0:256].rearrange("p (j k) -> p j k", j=2)
            S.op("act", lambda e: e.activation(out=ktt[0:64, 0:4:2, :], in_=tpv[0:64, :, :], func=AF.Copy),
                 reads=[rtp], writes=[rktt])
            S.op("act", lambda e: e.activation(out=ktt[64:128, 1:4:2, :], in_=tpv[64:128, :, :], func=AF.Copy),
                 reads=[rtp], writes=[rktt])
            if _CACHE.get("att") == "tr":
                continue
            sp_, rsp = sps.next()
            for h in range(4):
                S.op("pe", lambda e, h=h: e.matmul(
                    sp_[:, h, 0:nqm], ktt[:, h, :], q_sb[:, 2 * g + h // 2, qslm],
                    start=True, stop=True), reads=[rktt, r_q], writes=[rsp], inc=(h == 3))
            if _CACHE.get("att") == "s":
                continue
            pp, rpp = p_sb.next()
            S.op("act", lambda e: e.activation(out=pp[0:nk, :, 0:nqm], in_=sp_[0:nk, :, 0:nqm], func=AF.Exp),
                 reads=[rsp], writes=[rpp])
            S.op("pool", lambda e, mk=mk: e.tensor_tensor(out=pp[:, :, 0:nqm], in0=pp[:, :, 0:nqm],
                                                           in1=mk[:, :, 0:nqm], op=ALU.mult),
                 reads=[rpp, r_mprev, r_mcur], writes=[rpp])
            ptiles.append((kt, rkt, pp, rpp, nk))
        def back():
            ntile = len(ptiles)
            if _CACHE.get("att") in ("dma", "tr", "s", "exp"):
                return
            for h in range(4):
                for part in range(2):
                    for ti, (kt, rkt, pp, rpp, nk) in enumerate(ptiles):
                        if part == 0:
                            lt = kt[:, 256 + 64 * h:256 + 64 * h + 128]
                        else:
                            lt = kt[:, 448:576]
                        S.op("pe", lambda e, h=h, lt=lt, pp=pp, ti=ti, part=part: e.matmul(
                            pvps[:, part, h, 0:nqm], lt, pp[:, h, 0:nqm],
                            start=(ti == 0), stop=(ti == ntile - 1)), reads=[rkt, rpp, r_ones], writes=[r_pvps],
                            inc=(h == 3 and part == 1 and ti == ntile - 1))
            if _CACHE.get("att") == "pv":
                return
            if g == 0:
                S.op("dve", lambda e: e.tensor_copy(out=acc[0:64, :, qsl], in_=pvps[0:64, 0, :, 0:nq]),
                     reads=[r_pvps], writes=[r_acc])
                S.op("dve", lambda e: e.tensor_copy(out=acc[64:128, :, qsl], in_=pvps[64:128, 1, :, 0:nq]),
                     reads=[r_pvps], writes=[r_acc])
            else:
                S.op("dve", lambda e: e.tensor_tensor(out=acc[0:64, :, qsl], in0=pvps[0:64, 0, :, 0:nq],
                                                      in1=acc[0:64, :, qsl], op=ALU.add),
                     reads=[r_pvps, r_acc], writes=[r_acc])
                S.op("dve", lambda e: e.tensor_tensor(out=acc[64:128, :, qsl], in0=pvps[64:128, 1, :, 0:nq],
                                                      in1=acc[64:128, :, qsl], op=ALU.add),
                     reads=[r_pvps, r_acc], writes=[r_acc])
        return back

    def run_tile(l, ti, kind):
        sample = kind == "sample"
        warm = kind == "warm"
        tn = TS if sample else (128 if warm else T)
        nseq, ls = (NSEQ, LS) if sample else (1, tn)
        tok0 = ti * T + (T - 128 if warm else 0)
        last_layer = l == 1
        Wl = w_in[l]
        if sample:
            if l == 0:
                S.dma("sp", x_sb[:, :, 0:tn], xsT.rearrange("(c p) t -> p c t", p=128), writes=[r_x])
            else:
                S.dma("sp", x_sb[:, :, 0:tn], xs1scr.rearrange("(c p) t -> p c t", p=128),
                      reads=[r_xs1], writes=[r_x])
        else:
            src = xT if l == 0 else x1scr
            rd = [] if l == 0 else [r_x1[ti]]
            S.dma("sp", x_sb[:, :, 0:tn], src[:, tok0:tok0 + tn].rearrange("(c p) t -> p c t", p=128),
                  reads=rd, writes=[r_x])
            if kind != "kv":
                S.dma("sp", vmask[:, 0:tn], validT[:, tok0:tok0 + tn], writes=[r_vmask])
        if _CACHE.get("stop") == "A":
            return
        rms(V_GATT + 8 * l, tn)
        if _CACHE.get("stop") == "B":
            return
        nblk = (tn + 127) // 128
        for g in ([] if warm else range(3)):
            Wg = GROUPS[g][0]
            wk, rwk = load_w("w_in", l, KOFF + 256 * g)
            wv, rwv = load_w("w_in", l, VOFF + 256 * g)
            for b in range(nblk):
                nb = min(128, tn - 128 * b)
                pt, rp = mainps.next()
                for (wt, rw, co) in ((wk, rwk, 0), (wv, rwv, 256)):
                    for c in range(8):
                        S.op("pe", lambda e, c=c, wt=wt, co=co: e.matmul(
                            pt[:, co:co + 256], h_sb[:, c, 128 * b:128 * b + 128], wt[:, c, 0:256],
                            start=(c == 0), stop=(c == 7)), reads=[r_h, rw], writes=[rp], inc=(c == 7))
                kr, rkr = kvrow.next()
                S.op("act", lambda e, kr=kr, pt=pt: e.activation(out=kr[0:nb, 0:512], in_=pt[0:nb, :], func=AF.Copy),
                     reads=[rp], writes=[rkr])
                orow = tok0 + 128 * b - (NS - Wg)
                need32 = sample or (ti >= 10 and orow >= 0)
                if _CACHE.get("cstop") == "mm":
                    continue
                if need32:
                    k32, rk32 = kvrow32.next()
                    S.op("act", lambda e, k32=k32, pt=pt: e.activation(out=k32[:, :], in_=pt[:, :], func=AF.Copy),
                         reads=[rp], writes=[rk32])
                if _CACHE.get("cstop") == "k32":
                    continue
                if sample:
                    S.op("pool", lambda e, kr=kr: e.memset(kr[0:nb, 512:576], 1.0), writes=[rkr])
                    for n in range(NSEQ if _CACHE.get("sdma") != "0" else 0):
                        S.dma("pool", ext[l][g][n, Wg:Wg + LS, :], kr[LS * n:LS * n + LS, :],
                              reads=[rkr], writes=[r_ext[l][g]], merge=True)
                        S.dma("pool", nks[g][l, n, Wg - LS:Wg, :], k32[LS * n:LS * n + LS, 0:256],
                              reads=[rk32], writes=[out_res], merge=True)
                        S.dma("pool", nvs[g][l, n, Wg - LS:Wg, :], k32[LS * n:LS * n + LS, 256:512],
                              reads=[rk32], writes=[out_res], merge=True)
                else:
                    blk = ti * 4 + b
                    S.op("pool", lambda e, kr=kr, blk=blk: e.tensor_scalar(
                        out=kr[0:nb, 512:576], in0=ones_t[0:nb, 0:64], scalar1=valtm[0:nb, blk:blk + 1],
                        scalar2=None, op0=ALU.mult), reads=[r_ones, r_valtm], writes=[rkr])
                    S.dma("pool", kvscr[l][g][tok0 + 128 * b:tok0 + 128 * b + nb, :], kr[0:nb, :],
                          reads=[rkr], writes=[r_kv[l][g][ti]], merge=True)
                    if need32:
                        S.dma("pool", nkp[g][l, orow:orow + nb, :], k32[0:nb, 0:256], reads=[rk32],
                              writes=[out_res], merge=True)
                        S.dma("pool", nvp[g][l, orow:orow + nb, :], k32[0:nb, 256:512], reads=[rk32],
                              writes=[out_res], merge=True)
        if kind == "kv":
            return
        if _CACHE.get("stop") == "C" and kind != "kv":
            return
        for e2 in range(3):
            wq, rwq = load_w("w_in", l, QOFF + 256 * e2)
            for k in range(2):
                e6 = 2 * e2 + k
                pt, rp = proj_fm(wq, rwq, 128 * k, h_sb, r_h, 0, tn)
                S.op("act", lambda e, e6=e6, pt=pt: e.activation(out=q_sb[:, e6, 0:tn], in_=pt[:, 0:tn],
                                                                 func=AF.Copy, scale=0.125), reads=[rp], writes=[r_q])
        if _CACHE.get("stop") == "D" and kind != "kv":
            return
        units = []
        for g in range(int(_CACHE.get("ng", 3))):
            Wg, d = GROUPS[g]
            for n in range(nseq):
                if sample:
                    keysrc = ext[l][g][n]
                    rk = lambda a_, b_, l=l, g=g: [r_ext[l][g]]
                    qrow0, qcol0 = Wg, LS * n
                else:
                    keysrc = kvscr[l][g]
                    rk = lambda a_, b_, l=l, g=g: r_kv[l][g][a_ // T:b_ // T + 1]
                    qrow0, qcol0 = tok0, 0
                for r in range(min(d, ls)):
                    nsub = (ls - r + d - 1) // d
                    for i0 in range(0, nsub, 128):
                        nq = min(128, nsub - i0)
                        units.append((g, keysrc, rk, qrow0, qcol0, i0, nq, r, d))
        uh = uhist[:, :, 0:nseq * 30].rearrange("p c (n t) -> p c n t", n=nseq)
        if sample:
            S.dma("sp", uh, sconv[l].rearrange("(c p) n t -> p c n t", p=128), writes=[r_uhist])
        elif warm:
            S.op("pool", lambda e: e.memset(uhist[:], 0.0), writes=[r_uhist])
        glu_w = {}

        def conv_chunk(i):
            i2, k = i // 2, i % 2
            if k == 0:
                glu_w["a"] = load_w("w_in", l, 256 * i2)
                glu_w["b"] = load_w("w_in", l, 1024 + 256 * i2)
            wa, rwa = glu_w["a"]
            wb, rwb = glu_w["b"]
            pa, rpa = proj_fm(wa, rwa, 128 * k, h_sb, r_h, 0, tn)
            pb, rpb = proj_fm(wb, rwb, 128 * k, h_sb, r_h, 0, tn)
            cdst = cbuf[:, i, 0:tn].rearrange("p (n t) -> p n t", n=nseq)
            cw = V_CW + (l * 8 + i) * 31
            S.op("act", lambda e, pb=pb: e.activation(out=tmpA[:, 0:tn], in_=pb[:, 0:tn], func=AF.Sigmoid),
                 reads=[rpb], writes=[r_tmpA])
            if sample:
                uwt, ruw = uws.next()
                ue = uwt[:, 0:nseq * (30 + ls)].rearrange("p (n t) -> p n t", n=nseq)
                S.op("pool", lambda e, ue=ue, i=i: e.tensor_copy(out=ue[:, :, 0:30], in_=uh[:, i, :, :]),
                     reads=[r_uhist], writes=[ruw])
                S.op("dve", lambda e, pa=pa, ue=ue: e.tensor_tensor(
                    out=ue[:, :, 30:30 + ls], in0=pa[:, 0:tn].rearrange("p (n t) -> p n t", n=nseq),
                    in1=tmpA[:, 0:tn].rearrange("p (n t) -> p n t", n=nseq), op=ALU.mult),
                    reads=[rpa, r_tmpA], writes=[ruw])
                S.op("pool", lambda e, ue=ue, i=i: e.tensor_copy(out=uh[:, i, :, :], in_=ue[:, :, ls:ls + 30]),
                     reads=[ruw], writes=[r_uhist])
                S.op("dve", lambda e, ue=ue, cdst=cdst, cw=cw, i=i: e.tensor_scalar(
                    out=cdst, in0=ue[:, :, 0:ls], scalar1=vcol(cw), scalar2=vcol(V_CB + 8 * l + i),
                    op0=ALU.mult, op1=ALU.add), reads=[ruw, r_vecs], writes=[r_cbuf])
                for j in range(1, 31):
                    S.op("dve", lambda e, ue=ue, j=j, cdst=cdst, cw=cw: e.scalar_tensor_tensor(
                        out=cdst, in0=ue[:, :, j:j + ls], scalar=vcol(cw + j), in1=cdst,
                        op0=ALU.mult, op1=ALU.add), reads=[ruw, r_cbuf, r_vecs], writes=[r_cbuf])
            else:
                uwt, ruw = uwork.next()
                S.op("pool", lambda e, uwt=uwt, i=i: e.tensor_copy(out=uwt[:, 0:30], in_=uhist[:, i, 0:30]),
                     reads=[r_uhist], writes=[ruw])
                S.op("dve", lambda e, pa=pa, uwt=uwt: e.tensor_tensor(out=uwt[:, 30:30 + tn], in0=pa[:, 0:tn],
                                                                      in1=tmpA[:, 0:tn], op=ALU.mult),
                     reads=[rpa, r_tmpA], writes=[ruw])
                S.op("dve", lambda e, pa=pa, i=i: e.tensor_tensor(out=uhist[:, i, 0:30], in0=pa[:, tn - 30:tn],
                                                                   in1=tmpA[:, tn - 30:tn], op=ALU.mult),
                     reads=[rpa, r_tmpA, ruw], writes=[r_uhist])
                dgt, rdg = dgbuf.next()
                S.dma("sp", dgt[:].rearrange("p j k -> p (j k)"), dgscr[l][i], reads=[r_dg], writes=[rdg])
                pc, rpc = mainps.next()
                for j in range(31):
                    S.op("pe", lambda e, j=j, dgt=dgt, uwt=uwt, pc=pc: e.matmul(
                        pc[:, 0:tn], dgt[:, j, :], uwt[:, j:j + tn], start=(j == 0), stop=(j == 30)),
                        reads=[rdg, ruw], writes=[rpc], inc=(j == 30))
                S.op("act", lambda e, pc=pc, i=i: e.activation(out=cbuf[:, i, 0:tn], in_=pc[:, 0:tn],
                                                               func=AF.Identity, bias=vcol(V_CB + 8 * l + i),
                                                               scale=1.0), reads=[rpc, r_vecs], writes=[r_cbuf])
        stride = max(1, (len(units) + 7) // 8)
        next_chunk = 0
        pend_back = None
        for ui, (g, keysrc, rk, qrow0, qcol0, i0, nq, r, d) in enumerate(units):
            bk = attention_unit(l, g, keysrc, rk, qrow0, qcol0, i0, nq, r, d, sample)
            if pend_back is not None:
                pend_back()
            pend_back = bk
            if ui % stride == stride - 1 and next_chunk < 8:
                conv_chunk(next_chunk)
                next_chunk += 1
        if pend_back is not None:
            pend_back()
        while next_chunk < 8:
            conv_chunk(next_chunk)
            next_chunk += 1
        S.op("dve", lambda e: e.tensor_scalar(out=acc[64:128, :, 0:tn], in0=acc[64:128, :, 0:tn], scalar1=1e-30,
                                              scalar2=None, op0=ALU.max), reads=[r_acc], writes=[r_acc])
        for h, (tt_, rt_) in enumerate(((tmpA, r_tmpA), (tmpB, r_tmpB), (rstd, r_rstd), (mean, r_mean))):
            S.op("dve", lambda e, h=h, tt_=tt_: e.reciprocal(out=tt_[0:64, 0:tn], in_=acc[64:128, h, 0:tn]),
                 reads=[r_acc], writes=[rt_])
            S.op("dve", lambda e, h=h, tt_=tt_: e.tensor_tensor(out=attn_sb[:, h, 0:tn], in0=acc[0:64, h, 0:tn],
                                                                in1=tt_[0:64, 0:tn], op=ALU.mult),
                 reads=[r_acc, rt_], writes=[r_attn])
        if sample:
            S.dma("pool", ncs[l].rearrange("(c p) n t -> p c n t", p=128), uh, reads=[r_uhist], writes=[out_res], merge=True)
        elif ti == NT - 1:
            S.dma("pool", ncp[l].rearrange("(c p) t -> p c t", p=128), uhist[:, :, 0:30],
                  reads=[r_uhist], writes=[out_res], merge=True)
        psum_sum, r_psum = mainps.next()
        for i in range(8):
            S.op("pe", lambda e, i=i: e.matmul(psum_sum[:, 0:tn], ones_t[:], cbuf[:, i, 0:tn],
                                                start=(i == 0), stop=(i == 7)),
                 reads=[r_cbuf, r_ones], writes=[r_psum], inc=(i == 7))
        psq, r_psq = mainps.next()
        for i in range(8):
            tt, rtt = (tmpA, r_tmpA) if i % 2 == 0 else (tmpB, r_tmpB)
            S.op("act", lambda e, i=i, tt=tt: e.activation(out=tt[:, 0:tn], in_=cbuf[:, i, 0:tn], func=AF.Square),
                 reads=[r_cbuf], writes=[rtt])
            S.op("pe", lambda e, i=i, tt=tt: e.matmul(psq[:, 0:tn], ones_t[:], tt[:, 0:tn],
                                                       start=(i == 0), stop=(i == 7)),
                 reads=[rtt, r_ones], writes=[r_psq])
        S.op("act", lambda e: e.activation(out=mean[:, 0:tn], in_=psum_sum[:, 0:tn], func=AF.Copy, scale=1.0 / D),
             reads=[r_psum], writes=[r_mean])
        S.op("dve", lambda e: e.tensor_tensor(out=tmpA[:, 0:tn], in0=mean[:, 0:tn], in1=mean[:, 0:tn], op=ALU.mult),
             reads=[r_mean], writes=[r_tmpA])
        S.op("dve", lambda e: e.scalar_tensor_tensor(out=tmpA[:, 0:tn], in0=psq[:, 0:tn], scalar=1.0 / D,
                                                     in1=tmpA[:, 0:tn], op0=ALU.mult, op1=ALU.subtract),
             reads=[r_psq, r_tmpA], writes=[r_tmpA])
        S.op("dve", lambda e: e.tensor_scalar(out=tmpA[:, 0:tn], in0=tmpA[:, 0:tn], scalar1=0.0, scalar2=None,
                                              op0=ALU.max), reads=[r_tmpA], writes=[r_tmpA])
        S.op("act", lambda e: e.activation(out=tmpA[:, 0:tn], in_=tmpA[:, 0:tn], func=AF.Sqrt, bias=eps_t[:, 0:1],
                                           scale=1.0), reads=[r_tmpA, r_eps], writes=[r_tmpA])
        S.op("dve", lambda e: e.reciprocal(out=tmpA[:, 0:tn], in_=tmpA[:, 0:tn]), reads=[r_tmpA], writes=[r_tmpA])
        for i in range(8):
            S.op("dve", lambda e, i=i: e.tensor_tensor(out=cbuf[:, i, 0:tn], in0=cbuf[:, i, 0:tn],
                                                       in1=mean[:, 0:tn], op=ALU.subtract),
                 reads=[r_cbuf, r_mean], writes=[r_cbuf])
            S.op("pool", lambda e, i=i: e.tensor_tensor(out=cbuf[:, i, 0:tn], in0=cbuf[:, i, 0:tn],
                                                        in1=tmpA[:, 0:tn], op=ALU.mult),
                 reads=[r_cbuf, r_tmpA], writes=[r_cbuf])
            S.op("act", lambda e, i=i: e.activation(out=r_(bufR[:, i, 0:tn]), in_=cbuf[:, i, 0:tn], func=AF.Silu,
                                                    bias=vcol(V_LNB + 8 * l + i), scale=vcol(V_LNG + 8 * l + i)),
                 reads=[r_cbuf, r_vecs], writes=[r_b0])
        if _CACHE.get("stop") == "G" and kind != "kv":
            return
        for j2 in range(4):
            if j2 % 2 == 0:
                S.dma("sp", wao[:], wbf["w_attn_out"][l][:, 256 * j2:256 * j2 + 512].rearrange(
                    "(h p) e -> p h e", p=64), reads=[r_wb["w_attn_out"][l]], writes=[r_wao])
            wgc, rwgc = load_w("w_in", l, GCOFF + 256 * j2)
            wga, rwga = load_w("w_in", l, GAOFF + 256 * j2)
            wco, rwco = load_w("w_conv_out", l, 256 * j2)
            for k in range(2):
                j = 2 * j2 + k
                pgc, rpgc = proj_fm(wgc, rwgc, 128 * k, h_sb, r_h, 0, tn)
                S.op("act", lambda e, pgc=pgc: e.activation(out=tmpA[:, 0:tn], in_=pgc[:, 0:tn], func=AF.Sigmoid),
                     reads=[rpgc], writes=[r_tmpA])
                pga, rpga = proj_fm(wga, rwga, 128 * k, h_sb, r_h, 0, tn)
                S.op("act", lambda e, pga=pga: e.activation(out=tmpB[:, 0:tn], in_=pga[:, 0:tn], func=AF.Sigmoid),
                     reads=[rpga], writes=[r_tmpB])
                pbc, rpbc = proj_fm(wco, rwco, 128 * k, bufR, r_b0, 0, tn)
                S.op("dve", lambda e, pbc=pbc: e.tensor_tensor(out=tmpA[:, 0:tn], in0=pbc[:, 0:tn], in1=tmpA[:, 0:tn],
                                                               op=ALU.mult), reads=[rpbc, r_tmpA], writes=[r_tmpA])
                pba, rpba = mainps.next()
                wc0 = 128 * (j % 4)
                for h in range(4):
                    S.op("pe", lambda e, h=h, wc0=wc0, pba=pba: e.matmul(
                        pba[:, 0:tn], r_(wao[:, h, wc0:wc0 + 128]), r_(attn_sb[:, h, 0:tn]),
                        start=(h == 0), stop=(h == 3)), reads=[r_wao, r_attn], writes=[rpba], inc=(h == 3))
                S.op("dve", lambda e, pba=pba: e.tensor_tensor(out=tmpB[:, 0:tn], in0=pba[:, 0:tn], in1=tmpB[:, 0:tn],
                                                               op=ALU.mult), reads=[rpba, r_tmpB], writes=[r_tmpB])
                S.op("pool", lambda e, j=j: e.tensor_tensor(out=r_(bufR[:, 8 + j, 0:tn]), in0=tmpA[:, 0:tn],
                                                            in1=tmpB[:, 0:tn], op=ALU.add),
                     reads=[r_tmpA, r_tmpB], writes=[r_b1])
        if _CACHE.get("stop") == "H" and kind != "kv":
            return
        for j2 in range(4):
            wo, rwo = load_w("w_out", l, 256 * j2)
            for k in range(2):
                j = 2 * j2 + k
                po, rpo = proj_fm(wo, rwo, 128 * k, bufR, r_b1, 8, tn)
                S.op("dve", lambda e, j=j, po=po: e.tensor_tensor(out=x_sb[:, j, 0:tn], in0=po[:, 0:tn],
                                                                  in1=x_sb[:, j, 0:tn], op=ALU.add),
                     reads=[rpo, r_x], writes=[r_x])
                if not sample:
                    S.op("pool", lambda e, j=j: e.tensor_tensor(out=x_sb[:, j, 0:tn], in0=x_sb[:, j, 0:tn],
                                                                in1=vmask[:, 0:tn], op=ALU.mult),
                         reads=[r_x, r_vmask], writes=[r_x])
        if _CACHE.get("stop") == "I" and kind != "kv":
            return
        rms(V_GFFN + 8 * l, tn)
        fh = fhist[:, :, 0:nseq * 2].rearrange("p c (n t) -> p c n t", n=nseq)
        if sample:
            S.dma("sp", fh, sffn[l].rearrange("(c p) n t -> p c n t", p=128), writes=r_fh)
        elif warm:
            S.op("pool", lambda e: e.memset(fhist[:], 0.0), writes=r_fh)
        Wu = w_up[l]
        Wd = w_down[l]
        ffn_pend = [None]
        uxh_res = _CACHE.setdefault("uxh_res_%d" % id(S), {})
        for ip in range(3):
            for i2 in range(4):
                wv_, rwv_ = load_w("w_up", l, 1024 * ip + 256 * i2)
                wg_, rwg_ = load_w("w_up", l, DFF + 1024 * ip + 256 * i2)
                for k in range(2):
                    il = 2 * i2 + k
                    i = 8 * ip + il
                    res_c = []
                    for (wt, rw, idx) in ((wv_, rwv_, i), (wg_, rwg_, 24 + i)):
                        pu, rpu = proj_fm(wt, rw, 128 * k, h_sb, r_h, 0, tn)
                        uxt, ruxt = upext.next()
                        ruxh = uxh_res.setdefault(id(uxt), Res("uxh"))
                        ux = uxt[:, 0:nseq * (2 + ls)].rearrange("p (n t) -> p n t", n=nseq)
                        S.op("pool", lambda e, ux=ux, idx=idx: e.tensor_copy(out=ux[:, :, 0:2], in_=fh[:, idx, :, :]),
                             reads=[r_fh[idx]], writes=[ruxh])
                        S.op("act", lambda e, ux=ux, pu=pu: e.activation(
                            out=ux[:, :, 2:2 + ls], in_=pu[:, 0:tn].rearrange("p (n t) -> p n t", n=nseq),
                            func=AF.Copy), reads=[rpu], writes=[ruxt])
                        S.op("pool", lambda e, ux=ux, idx=idx: e.tensor_copy(out=fh[:, idx, :, :],
                                                                             in_=ux[:, :, ls:ls + 2]),
                             reads=[ruxt], writes=[r_fh[idx]])
                        uct, ruc = upc.next()
                        uc = uct[:, 0:tn].rearrange("p (n t) -> p n t", n=nseq)
                        fw = V_FW + (l * 48 + idx) * 3
                        S.op("act", lambda e, ux=ux, uc=uc, fw=fw, idx=idx: e.activation(
                            out=uc, in_=ux[:, :, 0:ls], func=AF.Identity, scale=vcol(fw),
                            bias=vcol(V_FB + 48 * l + idx)), reads=[ruxt, ruxh, r_vecs], writes=[ruc])
                        for j in (1, 2):
                            S.op("dve", lambda e, ux=ux, uc=uc, fw=fw, j=j: e.scalar_tensor_tensor(
                                out=uc, in0=ux[:, :, j:j + ls], scalar=vcol(fw + j), in1=uc,
                                op0=ALU.mult, op1=ALU.add), reads=[ruxt, ruxh, ruc, r_vecs], writes=[ruc])
                        res_c.append((uct, ruc))
                    (vt, rv), (gt, rg) = res_c

                    def pair_back(vt=vt, rv=rv, gt=gt, rg=rg, il=il):
                        S.op("act", lambda e: e.activation(out=gt[:, 0:tn], in_=gt[:, 0:tn], func=AF.Silu),
                             reads=[rg], writes=[rg])
                        S.op("dve", lambda e: e.tensor_tensor(
                            out=r_(bufR[:, il, 0:tn]), in0=vt[:, 0:tn], in1=gt[:, 0:tn], op=ALU.mult),
                            reads=[rv, rg], writes=[r_b0])

                    if ffn_pend[0] is not None:
                        ffn_pend[0]()
                    ffn_pend[0] = pair_back
            if ffn_pend[0] is not None:
                ffn_pend[0]()
                ffn_pend[0] = None
            for j2 in range(4):
                wd_, rwd_ = load_w("w_down", l, 256 * j2, r0=1024 * ip)
                for k in range(2):
                    j = 2 * j2 + k
                    pd, rpd = proj_fm(wd_, rwd_, 128 * k, bufR, r_b0, 0, tn)
                    S.op("dve", lambda e, j=j, pd=pd: e.tensor_tensor(out=x_sb[:, j, 0:tn], in0=pd[:, 0:tn],
                                                                      in1=x_sb[:, j, 0:tn], op=ALU.add),
                         reads=[rpd, r_x], writes=[r_x])
                    if ip == 2 and not sample:
                        S.op("pool", lambda e, j=j: e.tensor_tensor(out=x_sb[:, j, 0:tn], in0=x_sb[:, j, 0:tn],
                                                                    in1=vmask[:, 0:tn], op=ALU.mult),
                             reads=[r_x, r_vmask], writes=[r_x])
        if sample:
            S.dma("pool", nfs[l].rearrange("(c p) n t -> p c n t", p=128), fh, reads=r_fh, writes=[out_res], merge=True)
        elif ti == NT - 1:
            S.dma("pool", nfp[l].rearrange("(c p) t -> p c t", p=128), fhist[:, :, 0:2],
                  reads=r_fh, writes=[out_res], merge=True)
        if _CACHE.get("stop") == "J" and kind != "kv":
            return
        if warm:
            return
        if not last_layer:
            if sample:
                S.dma("pool", xs1scr.rearrange("(c p) t -> p c t", p=128), x_sb[:, :, 0:tn],
                      reads=[r_x], writes=[r_xs1])
            else:
                S.dma("pool", x1scr[:, tok0:tok0 + T].rearrange("(c p) t -> p c t", p=128), x_sb[:, :, :],
                      reads=[r_x], writes=[r_x1[ti]])
        else:
            if sample or tok0 >= OWN0:
                rms(V_GFIN, tn)
                if sample:
                    S.dma("pool", ysT.rearrange("(c p) t -> p c t", p=128), h_sb[:, :, 0:tn],
                          reads=[r_h], writes=[out_res], merge=True)
                else:
                    S.dma("pool", yT[:, tok0 - OWN0:tok0 - OWN0 + T].rearrange("(c p) t -> p c t", p=128),
                          h_sb[:, :, :], reads=[r_h], writes=[out_res], merge=True)

    xs1scr = dscr("xs1scr", [D, TS])
    r_xs1 = Res("xs1")

    ntile_done = 0
    for l in range(2):
        first_full = 4 if l == 0 else 9
        first_kv = 0 if l == 0 else 5
        sched = ([(ti, "kv") for ti in range(first_kv, first_full + 1)] + [(first_full, "warm")]
                 + [(ti, "full") for ti in range(first_full + 1, NT)] + [(0, "sample")])
        for (ti, kind) in sched:
            if max_tiles is not None and ntile_done >= max_tiles:
                continue
            if _CACHE.get("only") == "sample" and kind != "sample":
                ntile_done += 1
                continue
            run_tile(l, ti, kind)
            ntile_done += 1
            if dbg_tile == (l, ti, kind):
                tn_ = TS if kind == "sample" else T
                S.dma("pool", dbgx[:, 0:tn_].rearrange("(c p) t -> p c t", p=128), x_sb[:, :, 0:tn_],
                      reads=[r_x], writes=[out_res], merge=True)
    S.finish("pool", [out_res])
    S.finish("sp", [out_res])
    return nc, S


_CACHE = {}


def _consts():
    jj = np.arange(128)[:, None]
    ii = np.arange(128)[None, :]
    mprev = np.tile((jj >= ii).astype(np.float32), (1, 4))
    mcur = np.tile((jj <= ii).astype(np.float32), (1, 4))
    return mprev, mcur


def _pack_vecs(norm_attn_g, norm_ffn_g, norm_final_g, conv_dw_b, conv_ln_g, conv_ln_b, conv_dw_w, ffn_dw_b, ffn_dw_w):
    v = np.zeros((128, NV), np.float32)

    def pc(a):
        a = np.asarray(a, np.float32)
        c = a.shape[-1] // 128
        a = a.reshape(a.shape[:-1] + (c, 128))
        return np.moveaxis(a, -1, 0)

    v[:, V_GATT:V_GATT + 16] = pc(norm_attn_g).reshape(128, 16)
    v[:, V_GFFN:V_GFFN + 16] = pc(norm_ffn_g).reshape(128, 16)
    v[:, V_GFIN:V_GFIN + 8] = pc(norm_final_g).reshape(128, 8)
    v[:, V_CB:V_CB + 16] = pc(conv_dw_b).reshape(128, 16)
    v[:, V_LNG:V_LNG + 16] = pc(conv_ln_g).reshape(128, 16)
    v[:, V_LNB:V_LNB + 16] = pc(conv_ln_b).reshape(128, 16)
    cw = pc(conv_dw_w)
    v[:, V_CW:V_CW + 2 * 8 * 31] = np.transpose(cw, (0, 1, 3, 2)).reshape(128, -1)
    v[:, V_FB:V_FB + 96] = pc(ffn_dw_b).reshape(128, 96)
    fw = pc(ffn_dw_w)
    v[:, V_FW:V_FW + 2 * 48 * 3] = np.transpose(fw, (0, 1, 3, 2)).reshape(128, -1)
    return v


def kernel(x_prompt, x_sample, state_conv, cache_k_w128, cache_v_w128, cache_k_w512, cache_v_w512,
           cache_k_w2048, cache_v_w2048, state_ffn_conv, norm_attn_g, w_in, conv_dw_w, conv_dw_b,
           conv_ln_g, conv_ln_b, w_conv_out, w_attn_out, w_out, norm_ffn_g, w_up, ffn_dw_w, ffn_dw_b,
           w_down, norm_final_g):
    f = lambda a: np.ascontiguousarray(np.asarray(a, dtype=np.float32))
    if "nc" not in _CACHE:
        _CACHE["nc"] = build_program()
    nc, S = _CACHE["nc"]
    mprev, mcur = _consts()
    vecs = _pack_vecs(norm_attn_g, norm_ffn_g, norm_final_g, conv_dw_b, conv_ln_g, conv_ln_b, conv_dw_w,
                      ffn_dw_b, ffn_dw_w)
    xp = f(x_prompt)[0]
    xpT_pad = np.zeros((D, OWN0 + SEQ), np.float32)
    xpT_pad[:, OWN0:] = xp.T
    ck = [f(cache_k_w128), f(cache_k_w512), f(cache_k_w2048)]
    cv = [f(cache_v_w128), f(cache_v_w512), f(cache_v_w2048)]
    shared = dict(w_in=f(w_in), w_conv_out=f(w_conv_out), w_attn_out=f(w_attn_out), w_out=f(w_out), w_up=f(w_up),
                  w_down=f(w_down), vecs=vecs, ident=np.eye(128, dtype=np.float32), mprev=mprev, mcur=mcur,
                  )
    xs = f(x_sample)
    sc = f(state_conv)
    sf = f(state_ffn_conv)
    in_maps = []
    for c in range(NCORES):
        s = c * OWN
        m = dict(shared)
        m["xT"] = np.ascontiguousarray(xpT_pad[:, s:s + NS])
        val = (np.arange(NS) + s - OWN0 >= 0).astype(np.float32)
        m["validT"] = np.ascontiguousarray(np.broadcast_to(val[None, :], (128, NS)))
        m["validtm"] = np.ascontiguousarray(val.reshape(NS // 128, 128).T)
        n0 = c * NSEQ
        m["xsT"] = np.ascontiguousarray(xs[n0:n0 + NSEQ].reshape(TS, D).T)
        m["sconv"] = np.ascontiguousarray(np.transpose(sc[:, n0:n0 + NSEQ], (0, 3, 1, 2)))
        m["sffn"] = np.ascontiguousarray(np.transpose(sf[:, n0:n0 + NSEQ], (0, 3, 1, 2)))
        for g in range(3):
            W = GROUPS[g][0]
            m["ck%d" % g] = np.ascontiguousarray(ck[g][:, n0:n0 + NSEQ].reshape(2, NSEQ, W, 256))
            m["cv%d" % g] = np.ascontiguousarray(cv[g][:, n0:n0 + NSEQ].reshape(2, NSEQ, W, 256))
        in_maps.append(m)
    if _CACHE.get("dbg_in_maps_only"):
        return in_maps
    res = run_bass_kernel_spmd(nc, in_maps, core_ids=list(range(NCORES)))
    R = res.results
    y_prompt = np.concatenate([R[c]["yT"].T for c in range(NCORES)], axis=0)[None]
    y_sample = np.concatenate([R[c]["ysT"].T.reshape(NSEQ, LS, D) for c in range(NCORES)], axis=0)
    last = R[NCORES - 1]
    conv_p = np.transpose(last["ncp"], (0, 2, 1))[:, None]
    conv_s = np.concatenate([np.transpose(R[c]["ncs"], (0, 2, 3, 1)) for c in range(NCORES)], axis=1)
    outs = [np.ascontiguousarray(y_prompt), np.ascontiguousarray(y_sample),
            np.ascontiguousarray(conv_p), np.ascontiguousarray(conv_s)]
    for g in range(3):
        W = GROUPS[g][0]
        outs.append(np.ascontiguousarray(last["nkp%d" % g].reshape(2, 1, W, 4, 64)))
        outs.append(np.ascontiguousarray(last["nvp%d" % g].reshape(2, 1, W, 4, 64)))
        outs.append(np.concatenate([R[c]["nks%d" % g].reshape(2, NSEQ, W, 4, 64) for c in range(NCORES)], axis=1))
        outs.append(np.concatenate([R[c]["nvs%d" % g].reshape(2, NSEQ, W, 4, 64) for c in range(NCORES)], axis=1))
    ffn_p = np.transpose(last["nfp"], (0, 2, 1))[:, None]
    ffn_s = np.concatenate([np.transpose(R[c]["nfs"], (0, 2, 3, 1)) for c in range(NCORES)], axis=1)
    outs.append(np.ascontiguousarray(ffn_p))
    outs.append(np.ascontiguousarray(ffn_s))
    return tuple(np.asarray(o, dtype=np.float32) for o in outs)
```
